# Optimizing a Trainium2 kernel written in Bass

```python
import math
import jax, jax.numpy as jnp
from jax import lax
import numpy as np

D_MODEL = 1024
BATCH = 1
SEQ = 16384
DEPTH = 4

EPS = 1e-6
D_FF = -(-8 * D_MODEL // (3 * 256)) * 256

POOL_WINDOWS = (2, 4, 8, 16)
POOL_WIDTH = D_MODEL // 2
POOL_GROUP = POOL_WIDTH // len(POOL_WINDOWS)

HGRN_HEADS = 4
HGRN_DK = 128
HGRN_DV = (D_MODEL // 2) // HGRN_HEADS
HGRN_CHUNK = 64
HGRN_KW = HGRN_HEADS * HGRN_DK
HGRN_VW = HGRN_HEADS * HGRN_DV

HYB_IN = POOL_WIDTH + 2 * HGRN_KW + 2 * HGRN_VW
HYB_OUT = POOL_WIDTH + HGRN_VW

DIL_GROUPS = ((128, 1), (512, 4), (2048, 16))
ATT_HEADS_PER_GROUP = 8
ATT_HEAD_DIM = 64
ATT_GROUP_WIDTH = ATT_HEADS_PER_GROUP * ATT_HEAD_DIM
N_ATT_HEADS = len(DIL_GROUPS) * ATT_HEADS_PER_GROUP
ATT_QKV = 3 * len(DIL_GROUPS) * ATT_GROUP_WIDTH
ATT_OUT = ATT_GROUP_WIDTH
REL_BUCKETS = 32
REL_MAX_DIST = 2048

N_HYB = (DEPTH + 1) // 2
N_ATT = DEPTH // 2

kernel_name = 'hybrid_pool_hgrn2_dilated_attn_trunk'


def rms_norm(x, gain):
    xf = x.astype(jnp.float32)
    y = xf * lax.rsqrt(jnp.mean(xf * xf, axis=-1, keepdims=True) + EPS)
    return (y * gain.astype(jnp.float32)).astype(x.dtype)


def t5_bucket(dist):
    max_exact = REL_BUCKETS // 2
    df = jnp.maximum(dist, 1).astype(jnp.float32)
    large = max_exact + (jnp.log(df / max_exact) / math.log(REL_MAX_DIST / max_exact)
                         * (REL_BUCKETS - max_exact)).astype(jnp.int32)
    large = jnp.minimum(large, REL_BUCKETS - 1)
    return jnp.where(dist < max_exact, dist, large)


def pool_mixer(u, w, scale):
    B, S, _ = u.shape
    uf = u.astype(jnp.float32)
    wmax = max(POOL_WINDOWS)
    cs = jnp.cumsum(uf, axis=1)
    cs_pad = jnp.pad(cs, ((0, 0), (wmax, 0), (0, 0)))
    pos = jnp.arange(S)
    diffs = []
    for gi, win in enumerate(POOL_WINDOWS):
        sl = slice(gi * POOL_GROUP, (gi + 1) * POOL_GROUP)
        window_sum = cs[:, :, sl] - cs_pad[:, wmax - win: wmax - win + S, sl]
        count = jnp.minimum(pos + 1, win).astype(jnp.float32)[None, :, None]
        diffs.append(window_sum / count - uf[:, :, sl])
    d = jnp.stack(diffs, axis=2)
    y = jnp.einsum('bsgc,gcd->bsgd', d, w.astype(jnp.float32)).reshape(B, S, POOL_WIDTH)
    return (y * scale.astype(jnp.float32)).astype(u.dtype)


def hgrn2_mixer(q_raw, f_raw, v_raw, g_raw, lb, out_gain):
    B, S, _ = q_raw.shape
    H, K, V, C = HGRN_HEADS, HGRN_DK, HGRN_DV, HGRN_CHUNK
    n = S // C
    q = jax.nn.silu(q_raw.astype(jnp.float32)).reshape(B, S, H, K)
    z = f_raw.astype(jnp.float32).reshape(B, S, H, K)
    lb = lb.astype(jnp.float32).reshape(H, K)
    log_f = jnp.logaddexp(jnp.log(lb), jnp.log1p(-lb) + jax.nn.log_sigmoid(z))
    k = (1.0 - lb) * jax.nn.sigmoid(-z)
    v = v_raw.astype(jnp.float32).reshape(B, S, H, V)

    def chunks(t):
        return t.reshape(B, n, C, H, t.shape[-1]).transpose(1, 0, 3, 2, 4)

    qc, kc, vc, lfc = chunks(q), chunks(k), chunks(v), chunks(log_f)
    bc = jnp.cumsum(lfc, axis=3)
    causal = jnp.tril(jnp.ones((C, C), dtype=bool))

    def step(state, inp):
        qt, kt, vt, bt = inp
        b_last = bt[:, :, -1:, :]
        o_inter = jnp.einsum('bhtk,bhkv->bhtv', qt * jnp.exp(bt), state)
        rel = jnp.where(causal[:, :, None], bt[:, :, :, None, :] - bt[:, :, None, :, :], -jnp.inf)
        a = jnp.einsum('bhtsk,bhsk->bhts', qt[:, :, :, None, :] * jnp.exp(rel), kt)
        o_intra = jnp.einsum('bhts,bhsv->bhtv', a, vt)
        k_dec = kt * jnp.exp(b_last - bt)
        state = (jnp.exp(b_last[:, :, 0, :, None]) * state
                 + jnp.einsum('bhsk,bhsv->bhkv', k_dec, vt))
        return state, o_inter + o_intra

    state0 = jnp.zeros((B, H, K, V), jnp.float32)
    _, o = lax.scan(step, state0, (qc, kc, vc, bc))
    o = o.transpose(1, 0, 3, 2, 4).reshape(B, S, H, V)
    o = rms_norm(o, out_gain) * jax.nn.silu(g_raw.astype(jnp.float32).reshape(B, S, H, V))
    return o.reshape(B, S, HGRN_VW).astype(q_raw.dtype)


def _to_strided(t, dil, blk):
    B, S, H, E = t.shape
    L = S // dil
    nb = -(-L // blk)
    t = t.reshape(B, L, dil, H, E).transpose(0, 2, 1, 3, 4)
    t = jnp.pad(t, ((0, 0), (0, 0), (0, nb * blk - L), (0, 0), (0, 0)))
    return t.reshape(B, dil, nb, blk, H, E)


def _from_strided(t, S):
    B, dil, nb, blk = t.shape[:4]
    tail = t.shape[4:]
    L = S // dil
    t = t.reshape(B, dil, nb * blk, *tail)[:, :, :L]
    return jnp.moveaxis(t, 1, 2).reshape(B, S, *tail)


def dilated_group_attention(q, k, v, win, dil, bias_table):
    S = q.shape[1]
    blk = win // dil
    qs = _to_strided(q * ATT_HEAD_DIM ** -0.5, dil, blk)
    ks = _to_strided(k, dil, blk)
    vs = _to_strided(v, dil, blk)
    nb = qs.shape[2]

    def with_prev(t):
        prev = jnp.pad(t[:, :, :-1], ((0, 0), (0, 0), (1, 0), (0, 0), (0, 0), (0, 0)))
        return jnp.concatenate([prev, t], axis=3)

    kk, vv = with_prev(ks), with_prev(vs)
    qi = jnp.arange(blk)[:, None]
    ki = jnp.arange(2 * blk)[None, :]
    dist = blk + qi - ki
    band = (dist >= 0) & (dist <= blk)
    bias = bias_table[t5_bucket(jnp.maximum(dist, 0) * dil)].astype(jnp.float32).transpose(2, 0, 1)
    has_prev = (jnp.arange(nb) > 0)[:, None, None] | (ki >= blk)[None]
    mask = band[None] & has_prev
    s = jnp.einsum('brnqhe,brnkhe->brnhqk', qs, kk) + bias
    s = jnp.where(mask[None, None, :, None], s, -jnp.inf)
    m = jnp.max(s, axis=-1, keepdims=True)
    p = jnp.exp(s - m)
    l = jnp.sum(p, axis=-1, keepdims=True)
    mt = jnp.moveaxis(m, 3, 4)
    lt = jnp.moveaxis(l, 3, 4)
    o = jnp.einsum('brnhqk,brnkhe->brnqhe', p, vv) / lt
    return _from_strided(o, S), _from_strided(mt, S), _from_strided(lt, S)


def dilated_attention_mixer(qkv, rel_bias):
    B, S, _ = qkv.shape
    G, Hg = len(DIL_GROUPS), ATT_HEADS_PER_GROUP
    t = qkv.astype(jnp.float32).reshape(B, S, 3, G, Hg, ATT_HEAD_DIM)
    outs, maxes, dens = [], [], []
    for gi, (win, dil) in enumerate(DIL_GROUPS):
        o, m, l = dilated_group_attention(t[:, :, 0, gi], t[:, :, 1, gi], t[:, :, 2, gi], win, dil,
                                          rel_bias[:, gi * Hg:(gi + 1) * Hg])
        outs.append(o)
        maxes.append(m)
        dens.append(l)
    o_all = jnp.stack(outs)
    m_all = jnp.stack(maxes)
    l_all = jnp.stack(dens)
    w = l_all * jnp.exp(m_all - jnp.max(m_all, axis=0, keepdims=True))
    o = jnp.sum(w * o_all, axis=0) / jnp.sum(w, axis=0)
    return o.reshape(B, S, ATT_OUT).astype(qkv.dtype)


def swiglu(h, w_in, w_out):
    a, b = jnp.split(h @ w_in, 2, axis=-1)
    return (jax.nn.silu(a) * b) @ w_out


def modulation(c, w, b):
    return jnp.split(jax.nn.silu(c) @ w + b, 3, axis=-1)


def sandwich(x, mod, g_pre, g_post, fn):
    shift, scale, gate = mod
    h = rms_norm(x, g_pre) * (1.0 + scale[:, None]) + shift[:, None]
    return x + gate[:, None] * rms_norm(fn(h), g_post)


def setup_inputs(seed: int = 0) -> dict:
    key = jax.random.key(seed)
    ks = jax.random.split(key, 17)

    def nrm(k, shape, scale):
        return jax.random.normal(k, shape, jnp.float32) * scale

    D = D_MODEL
    return {
        'x': nrm(ks[0], (BATCH, SEQ, D), 1.0),
        'c': nrm(ks[1], (BATCH, D), 1.0),
        'ada_w': nrm(ks[2], (DEPTH, 2, D, 3 * D), D ** -0.5),
        'ada_b': nrm(ks[3], (DEPTH, 2, 3 * D), 0.01),
        'norm_pre': 1.0 + nrm(ks[4], (DEPTH, 2, D), 0.1),
        'norm_post': 1.0 + nrm(ks[5], (DEPTH, 2, D), 0.1),
        'ffn_w_in': nrm(ks[6], (DEPTH, D, 2 * D_FF), D ** -0.5),
        'ffn_w_out': nrm(ks[7], (DEPTH, D_FF, D), D_FF ** -0.5),
        'hyb_w_in': nrm(ks[8], (N_HYB, D, HYB_IN), D ** -0.5),
        'hyb_w_out': nrm(ks[9], (N_HYB, HYB_OUT, D), HYB_OUT ** -0.5),
        'pool_w': nrm(ks[10], (N_HYB, len(POOL_WINDOWS), POOL_GROUP, POOL_GROUP), POOL_GROUP ** -0.5),
        'pool_scale': 1.0 + nrm(ks[11], (N_HYB, POOL_WIDTH), 0.1),
        'hgrn_lb_logits': nrm(ks[12], (N_HYB, HGRN_KW), 1.0),
        'hgrn_out_norm': 1.0 + nrm(ks[13], (N_HYB, HGRN_DV), 0.1),
        'att_w_qkv': nrm(ks[14], (N_ATT, D, ATT_QKV), D ** -0.5),
        'att_w_out': nrm(ks[15], (N_ATT, ATT_OUT, D), ATT_OUT ** -0.5),
        'rel_bias': nrm(ks[16], (REL_BUCKETS, N_ATT_HEADS), 0.5),
    }


def reference(x, c, ada_w, ada_b, norm_pre, norm_post, ffn_w_in, ffn_w_out, hyb_w_in, hyb_w_out,
              pool_w, pool_scale, hgrn_lb_logits, hgrn_out_norm, att_w_qkv, att_w_out, rel_bias):
    lb_cum = jnp.cumsum(jax.nn.softmax(hgrn_lb_logits.astype(jnp.float32), axis=0), axis=0)
    lower_bounds = jnp.maximum(lb_cum - lb_cum[:1], 0.0)
    split_at = [POOL_WIDTH, POOL_WIDTH + HGRN_KW, POOL_WIDTH + 2 * HGRN_KW, POOL_WIDTH + 2 * HGRN_KW + HGRN_VW]

    for layer in range(DEPTH):
        j = layer // 2
        if layer % 2 == 0:
            def mixer(h, j=j):
                u, qr, fr, vr, gr = jnp.split(h @ hyb_w_in[j], split_at, axis=-1)
                y = jnp.concatenate([pool_mixer(u, pool_w[j], pool_scale[j]),
                                     hgrn2_mixer(qr, fr, vr, gr, lower_bounds[j], hgrn_out_norm[j])], axis=-1)
                return y @ hyb_w_out[j]
        else:
            def mixer(h, j=j):
                return dilated_attention_mixer(h @ att_w_qkv[j], rel_bias) @ att_w_out[j]

        def ffn(h, l=layer):
            return swiglu(h, ffn_w_in[l], ffn_w_out[l])

        x = sandwich(x, modulation(c, ada_w[layer, 0], ada_b[layer, 0]), norm_pre[layer, 0], norm_post[layer, 0], mixer)
        x = sandwich(x, modulation(c, ada_w[layer, 1], ada_b[layer, 1]), norm_pre[layer, 1], norm_post[layer, 1], ffn)
    return x
```

```python
import numpy as np
from contextlib import ExitStack
import concourse.bass as bass
import concourse.mybir as mybir
from concourse.bass_utils import run_bass_kernel_spmd

F32 = mybir.dt.float32
BF16 = mybir.dt.bfloat16
I32 = mybir.dt.int32
ALU = mybir.AluOpType
AF = mybir.ActivationFunctionType
AX = mybir.AxisListType

NCORES = 8
SEQ = 16384
NT = SEQ // NCORES
NTILE = NT // 128
D = 1024
KC = D // 128
DFF = 2816
FC = DFF // 128
EPS = 1e-6
CH = 16
NCH = 128 // CH


class Prog:
    ENGS = ("tensor", "vector", "scalar", "gpsimd", "sync")
    QUEUES = ("sync", "gpsimd", "scalar")
    NDMA = 8

    def __init__(self, nc, es):
        self.nc = nc
        self.streams = {e: [] for e in self.ENGS}
        self.sem = {e: es.enter_context(nc.semaphore("s_" + e)) for e in self.ENGS}
        self.cnt = {e: 0 for e in self.ENGS}
        self.seen = {e: {} for e in self.ENGS}
        self.dsem = {q: [es.enter_context(nc.semaphore("d_%s%d" % (q, i))) for i in range(self.NDMA)]
                     for q in self.QUEUES}
        self.dcnt = {q: [0] * self.NDMA for q in self.QUEUES}
        self.dnext = {q: 0 for q in self.QUEUES}
        self.lastw = {}
        self.readers = {}
        self.nops = 0

    def _semobj(self, semkey):
        if semkey[0] == "e":
            return self.sem[semkey[1]]
        return self.dsem[semkey[1]][semkey[2]]

    def _deps(self, r, w):
        deps = []
        for k in r:
            t = self.lastw.get(k)
            if t is not None:
                deps.append(t)
        for k in w:
            t = self.lastw.get(k)
            if t is not None:
                deps.append(t)
            rd = self.readers.get(k)
            if rd:
                deps.extend(rd.items())
        return deps

    def _emit_waits(self, eng, deps):
        need = {}
        for semkey, val in deps:
            if eng == "tensor" and semkey == ("e", "tensor"):
                continue
            if need.get(semkey, 0) < val:
                need[semkey] = val
        for semkey, val in need.items():
            if self.seen[eng].get(semkey, 0) >= val:
                continue
            self.seen[eng][semkey] = val
            s = self._semobj(semkey)
            self.streams[eng].append(lambda E, s=s, val=val: E.wait_ge(s, val))

    def _record(self, token, r, w):
        for k in w:
            self.lastw[k] = token
            self.readers[k] = {}
        for k in r:
            rd = self.readers.setdefault(k, {})
            if rd.get(token[0], 0) < token[1]:
                rd[token[0]] = token[1]

    def op(self, eng, fn, r=(), w=(), inc=True):
        self.nops += 1
        self._emit_waits(eng, self._deps(r, w))
        if inc:
            self.cnt[eng] += 1
            s = self.sem[eng]
            self.streams[eng].append(lambda E, fn=fn, s=s: fn(E).then_inc(s, 1))
            token = (("e", eng), self.cnt[eng])
        else:
            self.streams[eng].append(lambda E, fn=fn: fn(E))
            token = (("e", eng), self.cnt[eng] + 1)
        self._record(token, r, w)

    def dma(self, q, out, in_, r=(), w=()):
        self.nops += 1
        self._emit_waits(q, self._deps(r, w))
        i = self.dnext[q]
        self.dnext[q] = (i + 1) % self.NDMA
        if self.dcnt[q][i] > 0:
            self._emit_waits(q, [(("d", q, i), self.dcnt[q][i])])
        self.dcnt[q][i] += 16
        s = self.dsem[q][i]
        self.streams[q].append(lambda E, out=out, in_=in_, s=s: E.dma_start(out=out, in_=in_).then_inc(s, 16))
        token = (("d", q, i), self.dcnt[q][i])
        self._record(token, r, w)

    def barrier(self):
        deps = [(("e", e), self.cnt[e]) for e in self.ENGS if self.cnt[e] > 0]
        for q in self.QUEUES:
            for i in range(self.NDMA):
                if self.dcnt[q][i] > 0:
                    deps.append((("d", q, i), self.dcnt[q][i]))
        for e in self.ENGS:
            self._emit_waits(e, [d for d in deps if not (e == "tensor" and d[0] == ("e", "tensor"))])
        self.lastw = {}
        self.readers = {}

    def replay(self):
        nc = self.nc
        with nc.Block() as block:
            @block.tensor
            def _(E):
                for f in self.streams["tensor"]:
                    f(E)

            @block.vector
            def _(E):
                for f in self.streams["vector"]:
                    f(E)

            @block.scalar
            def _(E):
                for f in self.streams["scalar"]:
                    f(E)

            @block.gpsimd
            def _(E):
                for f in self.streams["gpsimd"]:
                    f(E)

            @block.sync
            def _(E):
                for f in self.streams["sync"]:
                    f(E)


class Arena:
    def __init__(self, nc, es, nbytes):
        self.words = nbytes // 4
        self.t = es.enter_context(nc.sbuf_tensor("arena", [128, self.words], F32))
        self.off = 0
        self.marks = []

    def alloc(self, shape, dtype):
        n = 1
        for s in shape:
            n *= s
        esz = 2 if dtype == BF16 else 4
        nwords = (n * esz + 3) // 4
        nwords = (nwords + 7) // 8 * 8
        assert self.off + nwords <= self.words, ("SBUF arena overflow", self.off, nwords, self.words)
        v = self.t[:, self.off:self.off + nwords]
        self.off += nwords
        if dtype != F32:
            v = v.bitcast(dtype)
        v = v[:, 0:n]
        if len(shape) == 2:
            v = v.rearrange("p (a b) -> p a b", b=shape[1])
        elif len(shape) == 3:
            v = v.rearrange("p (a b c) -> p a b c", b=shape[1], c=shape[2])
        return v

    def mark(self):
        self.marks.append(self.off)

    def release(self):
        self.off = self.marks.pop()


class Ctx:
    pass


def dram_bcast(handle, offset, n):
    return bass.AP(handle, offset, [[0, 128], [1, n]])


def setup_common(nc, es, P, A, io):
    C = Ctx()
    C.nc, C.P, C.A, C.io = nc, P, A, io
    C.x = A.alloc([NTILE, D], F32)
    C.ident = A.alloc([128], BF16)
    C.junk = A.alloc([D], BF16)
    C.cT = A.alloc([KC], BF16)
    C.cB = A.alloc([KC, 128], BF16)
    C.psb = [es.enter_context(nc.psum_tensor("psb%d" % i, [128, 1024], F32))[:, :] for i in range(3)]
    C.pst = [es.enter_context(nc.psum_tensor("pst%d" % i, [128, 1024], BF16))[:, :] for i in range(2)]
    C.psb_i = 0
    C.pst_i = 0
    C.epsc = A.alloc([1], F32)
    P.op("gpsimd", lambda E: E.memset(C.epsc, EPS), w=["epsc"])
    P.dma("gpsimd", C.ident, io["ident"].ap(), w=["ident"])
    ctmp = A.alloc([KC], F32)
    P.dma("sync", ctmp, io["c_col"].ap(), w=["ctmp"])
    csil = A.alloc([KC], F32)
    P.op("scalar", lambda E: E.activation(out=csil, in_=ctmp, func=AF.Silu), r=["ctmp"], w=["csil"])
    P.op("vector", lambda E: E.tensor_copy(out=C.cT, in_=csil), r=["csil"], w=["cT"])
    P.op("vector", lambda E: E.tensor_copy(out=C.cB, in_=csil.unsqueeze(2).to_broadcast([128, KC, 128])),
         r=["csil"], w=["cB"])
    return C


def next_psb(C):
    i = C.psb_i
    C.psb_i = (i + 1) % 3
    return i


def next_pst(C):
    i = C.pst_i
    C.pst_i = (i + 1) % 2
    return i


def load_x(C, name="x"):
    P = C.P
    xv = C.io[name].ap().rearrange("(n p) d -> p n d", p=128)
    for g in range(4):
        P.dma("sync", C.x[:, 4 * g:4 * g + 4, :], xv[:, 4 * g:4 * g + 4, :], w=[("x", i) for i in range(4 * g, 4 * g + 4)])


def store_x(C, name="y"):
    P = C.P
    yv = C.io[name].ap().rearrange("(n p) d -> p n d", p=128)
    for g in range(4):
        P.dma("sync", yv[:, 4 * g:4 * g + 4, :], C.x[:, 4 * g:4 * g + 4, :], r=[("x", i) for i in range(4 * g, 4 * g + 4)])


def modulation(C, ls):
    P, A, io = C.P, C.A, C.io
    M = Ctx()
    M.shiftT = A.alloc([KC], F32)
    M.AT = A.alloc([KC], F32)
    M.GG = A.alloc([D], F32)
    A.mark()
    wbuf = [A.alloc([KC, 512], BF16) for _ in range(6)]
    bcol = A.alloc([24], F32)
    gpre = A.alloc([KC], F32)
    bgate = A.alloc([D], F32)
    gpost = A.alloc([D], F32)
    modT = A.alloc([16], F32)
    key = ("mod", ls)
    li = C.sel["ls"].index(ls)
    wv = io["ada_w"].ap()[li].rearrange("(k p) n -> p k n", p=128)
    P.dma("sync", bcol, io["ada_b_col"].ap()[li], w=[key + ("bcol",)])
    P.dma("sync", gpre, io["norm_pre_col"].ap()[li], w=[key + ("gpre",)])
    P.dma("sync", bgate, dram_bcast(io["ada_b"], li * 3 * D + 2 * D, D), w=[key + ("bgate",)])
    P.dma("sync", gpost, dram_bcast(io["norm_post"], li * D, D), w=[key + ("gpost",)])
    pi = next_psb(C)
    ps = C.psb[pi]
    for piece in range(6):
        P.dma("gpsimd", wbuf[piece], wv[:, :, piece * 512:(piece + 1) * 512], w=[key + ("w", piece)])
    for piece in range(6):
        wb = wbuf[piece]
        wk = key + ("w", piece)
        if piece < 4:
            for jc in range(4):
                j = piece * 4 + jc
                for kc in range(KC):
                    P.op("tensor", lambda E, j=j, jc=jc, kc=kc, wb=wb: E.matmul(
                        ps[:, j:j + 1], lhsT=wb[:, kc, jc * 128:(jc + 1) * 128], rhs=C.cT[:, kc:kc + 1],
                        start=(kc == 0), stop=(kc == KC - 1)),
                        r=[wk, "cT"], w=psbk(pi), inc=(kc == KC - 1))
            if piece == 3:
                P.op("vector", lambda E: E.tensor_tensor(out=modT, in0=ps[:, 0:16], in1=bcol[:, 0:16], op=ALU.add),
                     r=psbk(pi) + [key + ("bcol",)], w=[key + ("modT",)])
                P.op("vector", lambda E: E.tensor_copy(out=M.shiftT, in_=modT[:, 0:8]),
                     r=[key + ("modT",)], w=[key + ("shiftT",)])
                P.op("vector", lambda E: E.scalar_tensor_tensor(out=M.AT, in0=modT[:, 8:16], scalar=1.0, in1=gpre,
                                                                  op0=ALU.add, op1=ALU.mult),
                     r=[key + ("modT",), key + ("gpre",)], w=[key + ("AT",)])
                pi2 = next_psb(C)
                ps2 = C.psb[pi2]
        else:
            ch = piece - 4
            for kc in range(KC):
                P.op("tensor", lambda E, ch=ch, kc=kc, wb=wb: E.matmul(
                    ps2[:, ch * 512:(ch + 1) * 512], lhsT=C.cB[:, kc, :], rhs=wb[:, kc, :],
                    start=(kc == 0), stop=(kc == KC - 1)),
                    r=[wk, "cB"], w=psbk(pi2), inc=(kc == KC - 1))
    P.op("vector", lambda E: E.tensor_tensor(out=M.GG, in0=ps2, in1=bgate, op=ALU.add),
         r=psbk(pi2) + [key + ("bgate",)], w=[key + ("GG",)])
    P.op("gpsimd", lambda E: E.tensor_tensor(out=M.GG, in0=M.GG, in1=gpost, op=ALU.mult),
         r=[key + ("GG",), key + ("gpost",)], w=[key + ("GG",)])
    M.key = key
    P.barrier()
    A.release()
    return M


def prenorm_p1(C, M, xa, xk, i, scratch):
    P = C.P
    ss, rstd, junk, xn = scratch
    P.op("scalar", lambda E: E.activation(out=junk[i % 2], in_=xa, func=AF.Square, accum_out=ss[:, i:i + 1]),
         r=[xk], w=[("pn_ss", i)])
    P.op("scalar", lambda E: E.activation(out=rstd[:, i:i + 1], in_=ss[:, i:i + 1], func=AF.Sqrt,
                                           bias=C.epsc, scale=1.0 / D),
         r=[("pn_ss", i), "epsc"], w=[("pn_rstd", i)])
    P.op("vector", lambda E: E.reciprocal(out=rstd[:, i:i + 1], in_=rstd[:, i:i + 1]),
         r=[("pn_rstd", i)], w=[("pn_rstd", i)])
    eng = "gpsimd"
    P.op(eng, lambda E: E.tensor_scalar(out=xn[i % 2], in0=xa, scalar1=rstd[:, i:i + 1],
                                         scalar2=1.0, op0=ALU.mult, op1=ALU.mult),
         r=[xk, ("pn_rstd", i)], w=[("pn_xn", i % 2)])


def prenorm_p2(C, M, hT, c0, i, scratch):
    P = C.P
    ss, rstd, junk, xn = scratch
    ti = next_pst(C)
    tp = C.pst[ti]
    for kc in range(KC):
        P.op("tensor", lambda E, kc=kc: E.transpose(tp[:, kc * 128:(kc + 1) * 128],
                                                     xn[i % 2][:, kc * 128:(kc + 1) * 128], C.ident),
             r=[("pn_xn", i % 2), "ident"], w=[("pst", ti)], inc=(kc == KC - 1))
    for kc in range(KC):
        if False:
            P.op("scalar", lambda E, kc=kc: E.activation(
                out=hT[:, kc, c0:c0 + 128], in_=tp[:, kc * 128:(kc + 1) * 128], func=AF.Identity,
                bias=M.shiftT[:, kc:kc + 1], scale=M.AT[:, kc:kc + 1]),
                r=[("pst", ti), M.key + ("shiftT",), M.key + ("AT",)], w=[("hT", c0 // 128, kc)])
        else:
            P.op("vector", lambda E, kc=kc: E.tensor_scalar(
                out=hT[:, kc, c0:c0 + 128], in0=tp[:, kc * 128:(kc + 1) * 128],
                scalar1=M.AT[:, kc:kc + 1], scalar2=M.shiftT[:, kc:kc + 1], op0=ALU.mult, op1=ALU.add),
                r=[("pst", ti), M.key + ("shiftT",), M.key + ("AT",)], w=[("hT", c0 // 128, kc)])


def prenorm_tiles(C, M, xsrc_tiles, hT, col0, scratch):
    for i, (xa, xk) in enumerate(xsrc_tiles):
        prenorm_p1(C, M, xa, xk, i, scratch)
        prenorm_p2(C, M, hT, col0 + 128 * i, i, scratch)


def alloc_pn_scratch(C):
    A = C.A
    ss = A.alloc([32], F32)
    rstd = A.alloc([32], F32)
    junk = [C.junk, C.junk]
    xn = [A.alloc([D], BF16) for _ in range(2)]
    return ss, rstd, junk, xn


def post_tile(C, M, pi, tile, scr):
    P = C.P
    ss2, rs2, junk2, tmp = scr
    ps = C.psb[pi]
    b = tile % 2
    P.op("scalar", lambda E: E.activation(out=junk2[b], in_=ps, func=AF.Square, accum_out=ss2[:, tile:tile + 1]),
         r=psbk(pi), w=[("po_ss", tile)])
    P.op("scalar", lambda E: E.activation(out=rs2[:, tile:tile + 1], in_=ss2[:, tile:tile + 1], func=AF.Sqrt,
                                           bias=C.epsc, scale=1.0 / D),
         r=[("po_ss", tile), "epsc"], w=[("po_rs", tile)])
    P.op("vector", lambda E: E.reciprocal(out=rs2[:, tile:tile + 1], in_=rs2[:, tile:tile + 1]),
         r=[("po_rs", tile)], w=[("po_rs", tile)])
    P.op("vector", lambda E: E.scalar_tensor_tensor(out=tmp[b], in0=ps, scalar=rs2[:, tile:tile + 1], in1=M.GG,
                                                     op0=ALU.mult, op1=ALU.mult),
         r=psbk(pi) + [("po_rs", tile), M.key + ("GG",)], w=[("po_tmp", 0)])
    P.op("vector", lambda E: E.tensor_tensor(out=C.x[:, tile, :], in0=C.x[:, tile, :], in1=tmp[b], op=ALU.add),
         r=[("po_tmp", 0), ("x", tile)], w=[("x", tile)])


def alloc_post_scratch(C):
    A = C.A
    ss2 = A.alloc([32], F32)
    rs2 = A.alloc([32], F32)
    junk2 = [C.junk, C.junk]
    t0 = A.alloc([D], F32)
    tmp = [t0, t0]
    return ss2, rs2, junk2, tmp


def ffn_sublayer(C, layer):
    P, A, io = C.P, C.A, C.io
    A.mark()
    M = modulation(C, 2 * layer + 1)
    HT = 1024
    hT = A.alloc([KC, HT], BF16)
    gT = A.alloc([FC, HT], BF16)
    wo = A.alloc([FC, D], BF16)
    wi = [A.alloc([KC, 512], BF16) for _ in range(2)]
    sa = [A.alloc([512], F32) for _ in range(2)]
    pn = alloc_pn_scratch(C)
    po = alloc_post_scratch(C)
    lf = C.sel["ffn"].index(layer)
    wov = io["ffn_w_out"].ap()[lf].rearrange("(j p) n -> p j n", p=128)

    def load_wo():
        for pc in range(2):
            P.dma("gpsimd", wo[:, 11 * pc:11 * pc + 11, :], wov[:, 11 * pc:11 * pc + 11, :], w=[("wo", pc)])
    wiv = io["ffn_w_in"].ap()[lf].rearrange("(k p) n -> p k n", p=128)

    def pre(half):
        tiles = [(C.x[:, half * 8 + i, :], ("x", half * 8 + i)) for i in range(8)]
        prenorm_tiles(C, M, tiles, hT, 0, pn)

    def mm1(half):
        for jj in range(FC // 2):
            wb = wi[jj % 2]
            wk = ("wi", jj % 2)
            P.dma("gpsimd", wb[:, :, 0:256], wiv[:, :, jj * 256:(jj + 1) * 256], w=[wk + ("a",)])
            P.dma("gpsimd", wb[:, :, 256:512], wiv[:, :, DFF + jj * 256:DFF + (jj + 1) * 256], w=[wk + ("b",)])
            if half == 0 and jj == 2:
                load_wo()
            for jl in range(2):
                j = 2 * jj + jl
                for tg in range(2):
                    if half == 0 and jj == 0 and jl == 0:
                        tl = [(C.x[:, tg * 4 + i, :], ("x", tg * 4 + i)) for i in range(4)]
                        prenorm_tiles(C, M, tl, hT, 512 * tg, pn)
                    pi = next_psb(C)
                    ps = C.psb[pi]
                    hk = [("hT", tg * 4 + q) for q in range(4)]
                    for part in range(2):
                        for kc in range(KC):
                            P.op("tensor", lambda E, part=part, kc=kc, wb=wb, jl=jl, tg=tg, ps=ps: E.matmul(
                                ps[:, part * 512:(part + 1) * 512],
                                lhsT=wb[:, kc, part * 256 + jl * 128:part * 256 + (jl + 1) * 128],
                                rhs=hT[:, kc, tg * 512:(tg + 1) * 512], start=(kc == 0), stop=(kc == KC - 1)),
                                r=[wk + ("a",), wk + ("b",)] + [(k_[0], k_[1], kc) for k_ in hk], w=psbk(pi),
                                inc=(part == 1 and kc == KC - 1))
                    sb = sa[(2 * j + tg) % 2]
                    sk = ("sa", (2 * j + tg) % 2)
                    P.op("scalar", lambda E, ps=ps, sb=sb: E.activation(out=sb, in_=ps[:, 0:512], func=AF.Silu),
                         r=psbk(pi), w=[sk])
                    P.op("vector", lambda E, ps=ps, sb=sb, j=j, tg=tg: E.tensor_tensor(
                        out=gT[:, j, tg * 512:(tg + 1) * 512], in0=sb, in1=ps[:, 512:1024], op=ALU.mult),
                        r=psbk(pi) + [sk], w=[("gT", j, tg)])

    def mm2(half):
        for ti in range(8):
            if half == 0:
                prenorm_p1(C, M, C.x[:, 8 + ti, :], ("x", 8 + ti), ti, pn)
            pi = next_psb(C)
            ps = C.psb[pi]
            for ch in range(2):
                for j in range(FC):
                    P.op("tensor", lambda E, ch=ch, j=j, ti=ti, ps=ps: E.matmul(
                        ps[:, ch * 512:(ch + 1) * 512], lhsT=gT[:, j, ti * 128:(ti + 1) * 128],
                        rhs=wo[:, j, ch * 512:(ch + 1) * 512], start=(j == 0), stop=(j == FC - 1)),
                        r=[("gT", j, ti // 4), ("wo", j // 11)], w=psbk(pi),
                        inc=(ch == 1 and j == FC - 1))
            if half == 0:
                prenorm_p2(C, M, hT, 128 * ti, ti, pn)
            post_tile(C, M, pi, half * 8 + ti, po)

    mm1(0)
    mm2(0)
    mm1(1)
    mm2(1)
    P.barrier()
    A.release()


def psbk(pi):
    return [("psq", 2 * pi), ("psq", 2 * pi + 1)]


class Halves:
    def __init__(self, C):
        self.C = C
        self.i = 0

    def next(self):
        i = self.i
        self.i = (i + 1) % 6
        ap = self.C.psb[i // 2][:, (i % 2) * 512:(i % 2) * 512 + 512]
        keys = [("psq", i)]
        return ap, keys

    def bank(self, i):
        return self.C.psb[i // 2][:, (i % 2) * 512:(i % 2) * 512 + 512], [("psq", i)]


def hybrid_consts(C, j):
    P, A, io = C.P, C.A, C.io
    H = Ctx()
    H.maskBD = A.alloc([128], F32)
    H.cmask = A.alloc([NCH, 128], BF16)
    H.rowmask = A.alloc([NCH], F32)
    H.smask = A.alloc([512], F32)
    H.smask128 = A.alloc([512], F32)
    H.lb = A.alloc([4], F32)
    H.oml = A.alloc([4], F32)
    H.poolw = A.alloc([4, 128], BF16)
    H.pscale = A.alloc([4], F32)
    H.gain = A.alloc([512], F32)
    H.invcnt = A.alloc([4, 16], F32)
    H.hmask = A.alloc([1], F32)
    H.epsh = A.alloc([1], F32)
    l0 = A.alloc([4], F32)
    l1 = A.alloc([4], F32)
    P.dma("sync", H.maskBD, io["maskBD"].ap(), w=["maskBD"])
    P.dma("gpsimd", H.cmask, io["cmask"].ap(), w=["cmask"])
    P.dma("sync", H.rowmask, io["rowmask"].ap(), w=["rowmask"])
    P.dma("sync", H.smask, io["smask"].ap(), w=["smask"])
    P.dma("sync", H.smask128, io["smask128"].ap(), w=["smask"])
    jl = C.sel["hyb"].index(j)
    P.dma("gpsimd", H.poolw, io["pool_w"].ap()[jl].rearrange("g c d -> c g d"), w=["poolw"])
    P.dma("sync", H.pscale, io["pool_scale_col"].ap()[jl], w=["pscale"])
    P.dma("sync", H.gain, dram_bcast(io["hgrn_gain4"], jl * 512, 512), w=["gain"])
    P.dma("sync", H.invcnt, io["invcnt"].ap(), w=["invcnt"])
    P.dma("sync", H.hmask, io["hmask"].ap(), w=["hmask"])
    P.op("gpsimd", lambda E: E.memset(H.epsh, EPS), w=["epsh"])
    if j == 0:
        P.op("gpsimd", lambda E: E.memset(H.lb, 0.0), w=["lb"])
        P.op("gpsimd", lambda E: E.memset(H.oml, 1.0), w=["oml"])
    else:
        P.dma("sync", l0, io["lb_logits_col"].ap()[0], w=["l0"])
        P.dma("sync", l1, io["lb_logits_col"].ap()[1], w=["l1"])
        P.op("vector", lambda E: E.tensor_tensor(out=l1, in0=l1, in1=l0, op=ALU.subtract), r=["l0", "l1"], w=["l1"])
        P.op("scalar", lambda E: E.activation(out=H.lb, in_=l1, func=AF.Sigmoid), r=["l1"], w=["lb"])
        P.op("vector", lambda E: E.tensor_scalar(out=H.oml, in0=H.lb, scalar1=-1.0, scalar2=1.0,
                                                  op0=ALU.mult, op1=ALU.add), r=["lb"], w=["oml"])
    return H


def hybrid_mixer(C, layer, full):
    P, A, io = C.P, C.A, C.io
    j = layer // 2
    A.mark()
    M = modulation(C, 2 * layer)
    H = hybrid_consts(C, j)
    hv = Halves(C)
    NB = 512
    chl = CH if full else 128
    nch = 128 // chl
    smask = H.smask if full else H.smask128
    hT = A.alloc([KC, 128 + NB], BF16)
    wbuf = [A.alloc([KC, 512], BF16) for _ in range(2)]
    wcnt = [0]
    S = A.alloc([4, 128], F32)
    btot = A.alloc([4], F32)
    bsum = A.alloc([4], F32)
    Sb = [A.alloc([4, 128], BF16) for _ in range(2)]
    tf = [[A.alloc([NB], F32) for _ in range(4)] for _ in range(3)]
    KH = A.alloc([4, NB], BF16)
    KD = A.alloc([4, NB], BF16)
    Dn = A.alloc([4, 4 * NCH], F32)
    V = A.alloc([4, 512], BF16)
    Vm = A.alloc([NCH, 512], BF16)
    KDt = [A.alloc([4, 128], BF16) for _ in range(2)]
    pn = alloc_pn_scratch(C)
    Skeys = [("S", hd) for hd in range(4)]
    if full:
        QH = A.alloc([4, NB], BF16)
        QHm = A.alloc([4, NCH, 128], BF16)
        G = A.alloc([4, 512], BF16)
        ATs = [A.alloc([4, 128], BF16) for _ in range(2)]
        ub = [A.alloc([16 + NB], F32) for _ in range(2)]
        tails = A.alloc([4, 16], F32)
        pt = [A.alloc([16 + NB], F32) for _ in range(2)]
        db = [A.alloc([NB], BF16) for _ in range(2)]
        yT = A.alloc([KC, NB], BF16)
        ysb = [A.alloc([512], BF16) for _ in range(2)]
        ssh = A.alloc([16, 4], F32)
        rsh = A.alloc([16, 4], F32)
        po = alloc_post_scratch(C)
        xh = po[3][0]
        P.dma("sync", xh, io["x_halo"].ap(), w=[("po_tmp", 0)])
        pSl = [tf[i // 4][i % 4].rearrange("p (h v) -> p h v", v=128) for i in range(7)]
        pD = A.alloc([7, 4], F32)
        P.dma("sync", pD, io["pred_D"].ap().rearrange("i p h -> p i h"), w=["pD"])
        P.op("gpsimd", lambda E: E.memset(S, 0.0), w=Skeys)
        for i in range(7):
            P.dma("sync", pSl[i], io["pred_S"].ap()[i], w=[("pS", i)])
        for i in range(7):
            for hd in range(4):
                P.op("vector", lambda E, i=i, hd=hd: E.scalar_tensor_tensor(
                    out=S[:, hd, :], in0=S[:, hd, :], scalar=pD[:, i, hd:hd + 1], in1=pSl[i][:, hd, :],
                    op0=ALU.mult, op1=ALU.add), r=[("S", hd), ("pS", i), "pD"], w=[("S", hd)])
        P.op("scalar", lambda E: E.activation(out=Sb[0], in_=S, func=AF.Copy), r=Skeys,
             w=[("Sb", 0, hd) for hd in range(4)])
        P.barrier()
        wov = io["hyb_w_out"].ap()[C.sel["hyb"].index(j)].rearrange("(k p) n -> p k n", p=128)
    else:
        P.op("gpsimd", lambda E: E.memset(S, 0.0), w=Skeys)
        P.op("gpsimd", lambda E: E.memset(btot, 0.0), w=["btot"])
    sbi = [0, 0, 0, 0]

    wv = io["hyb_w_in"].ap()[C.sel["hyb"].index(j)].rearrange("(k p) n -> p k n", p=128)

    if full:
        block_seq = [("in", 0), ("in", 3), ("in", 4), ("in", 2), ("in", 1), ("out", 0), ("out", 1)]
    else:
        block_seq = [("in", 3), ("in", 2)]
    wseq = block_seq * 4
    issued = [0]
    used = [0]

    def issue_next():
        k = issued[0]
        if k >= len(wseq):
            return
        kind, piece = wseq[k]
        src = wv if kind == "in" else wov
        bb = k % 2
        P.dma("gpsimd", wbuf[bb], src[:, :, piece * 512:(piece + 1) * 512], w=[("hw", bb)])
        issued[0] += 1

    def load_w(src, piece):
        k = used[0]
        used[0] += 1
        assert wseq[k][1] == piece, (wseq[k], piece)
        while issued[0] <= k:
            issue_next()
        nxt = k + 1
        if nxt < len(wseq) and issued[0] == nxt and not (wseq[k][0] == "out" and wseq[nxt][0] == "in"):
            issue_next()
        bb = k % 2
        return wbuf[bb], ("hw", bb)

    def proj_fm(wb, wk, cc, c0, n, hkeys):
        ap, keys = hv.next()
        for kc in range(KC):
            P.op("tensor", lambda E, kc=kc, ap=ap: E.matmul(ap[:, 0:n], lhsT=wb[:, kc, cc * 128:(cc + 1) * 128],
                                                             rhs=hT[:, kc, c0:c0 + n], start=(kc == 0), stop=(kc == KC - 1)),
                 r=[wk] + [(k_[0], k_[1], kc) for k_ in hkeys], w=keys, inc=(kc == KC - 1))
        return ap, keys

    def proj_tm(wb, wk, t):
        ap, keys = hv.next()
        for kc in range(KC):
            P.op("tensor", lambda E, kc=kc, ap=ap: E.matmul(
                ap, lhsT=hT[:, kc, 128 + t * 128:128 + (t + 1) * 128], rhs=wb[:, kc, :],
                start=(kc == 0), stop=(kc == KC - 1)), r=[wk, ("hT", 1 + t, kc)], w=keys, inc=(kc == KC - 1))
        return ap, keys

    for b in range(4):
        tiles = [(C.x[:, 4 * b + i, :], ("x", 4 * b + i)) for i in range(4)]
        hk = [("hT", 1 + i) for i in range(4)]
        if full and b == 0:
            prenorm_tiles(C, M, [(xh, ("po_tmp", 0))], hT, 0, pn)
        prenorm_tiles(C, M, tiles, hT, 128, pn)

        if full:
            wb, wk = load_w(wv, 0)
            if b == 0:
                for g in range(4):
                    ap, keys = proj_fm(wb, wk, g, 112, 16, [("hT", 0)])
                    P.op("vector", lambda E, g=g, ap=ap: E.tensor_scalar(out=tails[:, g, :], in0=ap[:, 0:16],
                                                                          scalar1=H.hmask[:, 0:1], scalar2=1.0,
                                                                          op0=ALU.mult, op1=ALU.mult),
                         r=keys + ["hmask"], w=[("tails", g)])
            def pool_head(g):
                ug = ub[g % 2]
                ugk = ("ug", g % 2)
                P.op("gpsimd", lambda E, g=g, ug=ug: E.tensor_copy(out=ug[:, 0:16], in_=tails[:, g, :]),
                     r=[("tails", g)], w=[ugk])
                ap, keys = proj_fm(wb, wk, g, 128, NB, hk)
                P.op("scalar", lambda E, ug=ug, ap=ap: E.activation(out=ug[:, 16:16 + NB], in_=ap, func=AF.Copy),
                     r=keys, w=[ugk])
                P.op("gpsimd", lambda E, g=g, ug=ug: E.tensor_copy(out=tails[:, g, :], in_=ug[:, NB:NB + 16]),
                     r=[ugk], w=[("tails", g)])

            def pool_tail(g):
                win = 2 << g
                ug = ub[g % 2]
                ugk = ("ug", g % 2)
                dg = db[g % 2]
                dgk = ("dg", g % 2)
                cur = ug
                curk = ugk
                sh = 1
                for lev in range(g + 1):
                    o_ = pt[lev % 2]
                    ok = ("pt", lev % 2)
                    eng = "gpsimd" if (lev + g) % 2 == 0 else "vector"
                    P.op(eng, lambda E, o_=o_, cur=cur, sh=sh: E.tensor_tensor(
                        out=o_[:, sh:16 + NB], in0=cur[:, sh:16 + NB], in1=cur[:, 0:16 + NB - sh], op=ALU.add),
                        r=[curk], w=[ok])
                    cur, curk = o_, ok
                    sh *= 2
                P.op("vector", lambda E, dg=dg, ug=ug, cur=cur, win=win: E.scalar_tensor_tensor(
                    out=dg, in0=cur[:, 16:16 + NB], scalar=1.0 / win, in1=ug[:, 16:16 + NB],
                    op0=ALU.mult, op1=ALU.subtract), r=[curk, ugk], w=[dgk])
                if b == 0:
                    P.op("vector", lambda E, g=g, cur=cur: E.tensor_tensor(
                        out=cur[:, 0:16], in0=cur[:, 16:32], in1=H.invcnt[:, g, :], op=ALU.mult),
                        r=[curk, "invcnt", dgk], w=[curk])
                    P.op("vector", lambda E, dg=dg, ug=ug, cur=cur: E.tensor_tensor(
                        out=dg[:, 0:16], in0=cur[:, 0:16], in1=ug[:, 16:32], op=ALU.subtract),
                        r=[curk, ugk], w=[dgk])
                ap, keys = hv.next()
                P.op("tensor", lambda E, g=g, ap=ap, dg=dg: E.matmul(ap, lhsT=H.poolw[:, g, :], rhs=dg,
                                                                     start=True, stop=True),
                     r=[dgk, "poolw"], w=keys)
                P.op("scalar", lambda E, g=g, ap=ap: E.activation(
                    out=yT[:, g, :], in_=ap, func=AF.Copy, scale=H.pscale[:, g:g + 1]),
                    r=keys + ["pscale"], w=[("yT", g)])

            pool_head(0)
            for g in range(4):
                if g + 1 < 4:
                    pool_head(g + 1)
                pool_tail(g)

        wb, wk = load_w(wv, 3)
        for t in range(4):
            ap, keys = proj_tm(wb, wk, t)
            P.op("scalar", lambda E, t=t, ap=ap: E.activation(out=V[:, t, :], in_=ap, func=AF.Copy),
                 r=keys, w=[("V", t)])
        if full:
            wb, wk = load_w(wv, 4)
            for t in range(4):
                ap, keys = proj_tm(wb, wk, t)
                P.op("scalar", lambda E, t=t, ap=ap: E.activation(out=tf[0][t], in_=ap, func=AF.Silu),
                     r=keys, w=[("tf", 0, t)])
                P.op("vector", lambda E, t=t: E.tensor_tensor(out=G[:, t, :], in0=tf[0][t], in1=H.gain, op=ALU.mult),
                     r=[("tf", 0, t), "gain"], w=[("G", t)])

        wb, wk = load_w(wv, 2)
        for hd in range(4):
            ap, keys = proj_fm(wb, wk, hd, 128, NB, hk)
            P.op("scalar", lambda E, hd=hd, ap=ap: E.activation(out=tf[0][hd], in_=ap, func=AF.Sigmoid),
                 r=keys, w=[("tf", 0, hd)])
        for hd in range(4):
            P.op("vector", lambda E, hd=hd: E.tensor_scalar(out=tf[0][hd], in0=tf[0][hd], scalar1=H.oml[:, hd:hd + 1],
                                                             scalar2=H.lb[:, hd:hd + 1], op0=ALU.mult, op1=ALU.add),
                 r=[("tf", 0, hd), "lb", "oml"], w=[("tf", 0, hd)])
        for hd in range(4):
            P.op("scalar", lambda E, hd=hd: E.activation(out=tf[1][hd], in_=tf[0][hd], func=AF.Ln),
                 r=[("tf", 0, hd)], w=[("tf", 1, hd)])
        for hd in range(4):
            P.op("gpsimd", lambda E, hd=hd: E.tensor_scalar(out=tf[0][hd], in0=tf[0][hd], scalar1=-1.0, scalar2=1.0,
                                                             op0=ALU.mult, op1=ALU.add),
                 r=[("tf", 0, hd), ("tf", 1, hd)], w=[("tf", 0, hd)])
            P.op("vector", lambda E, hd=hd: E.tensor_tensor_scan(out=tf[2][hd], data0=smask, data1=tf[1][hd],
                                                                  initial=0.0, op0=ALU.mult, op1=ALU.add),
                 r=[("tf", 1, hd), "smask"], w=[("tf", 2, hd)])
        for hd in range(4):
            bend = tf[2][hd].rearrange("p (n c) -> p n c", c=chl)[:, :, chl - 1]
            P.op("scalar", lambda E, hd=hd, bend=bend: E.activation(out=Dn[:, hd, 0:4 * nch], in_=bend, func=AF.Exp),
                 r=[("tf", 2, hd)], w=[("Dn", hd)])
            if not full:
                P.op("vector", lambda E, hd=hd, bend=bend: E.tensor_reduce(out=bsum[:, hd:hd + 1], in_=bend,
                                                                             axis=AX.X, op=ALU.add),
                     r=[("tf", 2, hd)], w=[("bsum", hd)])
                P.op("vector", lambda E, hd=hd: E.tensor_tensor(out=btot[:, hd:hd + 1], in0=btot[:, hd:hd + 1],
                                                                 in1=bsum[:, hd:hd + 1], op=ALU.add),
                     r=[("bsum", hd), "btot"], w=["btot"])
            if full:
                P.op("scalar", lambda E, hd=hd: E.activation(out=tf[1][hd], in_=tf[2][hd], func=AF.Exp, scale=-1.0),
                     r=[("tf", 2, hd)], w=[("tf", 1, hd)])
            else:
                P.op("vector", lambda E, hd=hd, bend=bend: E.tensor_tensor(
                    out=tf[1][hd].rearrange("p (n c) -> p n c", c=chl),
                    in0=bend.unsqueeze(2).to_broadcast([128, 4 * nch, chl]),
                    in1=tf[2][hd].rearrange("p (n c) -> p n c", c=chl), op=ALU.subtract),
                    r=[("tf", 2, hd)], w=[("tf", 1, hd)])
                P.op("scalar", lambda E, hd=hd: E.activation(out=tf[1][hd], in_=tf[1][hd], func=AF.Exp),
                     r=[("tf", 1, hd)], w=[("tf", 1, hd)])
            if full:
                P.op("scalar", lambda E, hd=hd: E.activation(out=tf[2][hd], in_=tf[2][hd], func=AF.Exp),
                     r=[("tf", 2, hd), ("Dn", hd), ("tf", 1, hd)], w=[("tf", 2, hd)])
        for hd in range(4):
            P.op("gpsimd", lambda E, hd=hd: E.tensor_tensor(out=tf[1][hd], in0=tf[1][hd], in1=tf[0][hd], op=ALU.mult),
                 r=[("tf", 0, hd), ("tf", 1, hd)], w=[("tf", 1, hd)])
            if full:
                P.op("gpsimd", lambda E, hd=hd: E.tensor_copy(out=KH[:, hd, :], in_=tf[1][hd]),
                     r=[("tf", 1, hd)], w=[("KH", hd)])
            if full:
                P.op("vector", lambda E, hd=hd: E.tensor_tensor(
                    out=KD[:, hd, :].rearrange("p (n c) -> p n c", c=CH),
                    in0=tf[1][hd].rearrange("p (n c) -> p n c", c=CH),
                    in1=Dn[:, hd, :].unsqueeze(2).to_broadcast([128, 4 * NCH, CH]), op=ALU.mult),
                    r=[("tf", 1, hd), ("Dn", hd)], w=[("KD", hd)])
            else:
                P.op("vector", lambda E, hd=hd: E.tensor_copy(out=KD[:, hd, :], in_=tf[1][hd]),
                     r=[("tf", 1, hd)], w=[("KD", hd)])
        if full:
            wb, wk = load_w(wv, 1)
            for hd in range(4):
                ap, keys = proj_fm(wb, wk, hd, 128, NB, hk)
                P.op("scalar", lambda E, hd=hd, ap=ap: E.activation(out=tf[0][hd], in_=ap, func=AF.Silu),
                     r=keys + [("tf", 1, hd)], w=[("tf", 0, hd)])
                P.op("vector", lambda E, hd=hd: E.tensor_tensor(out=QH[:, hd, :], in0=tf[0][hd], in1=tf[2][hd], op=ALU.mult),
                     r=[("tf", 0, hd), ("tf", 2, hd)], w=[("QH", hd)])
            wo0, wok0 = load_w(wov, 0)
            wo1, wok1 = load_w(wov, 1)

        for t in range(4):
            tile = 4 * b + t
            kb = t % 2
            ti = next_pst(C)
            tp = C.pst[ti]
            for hd in range(4):
                P.op("tensor", lambda E, hd=hd, t=t, tp=tp: E.transpose(tp[:, hd * 128:(hd + 1) * 128],
                                                                          KD[:, hd, t * 128:(t + 1) * 128], C.ident),
                     r=[("KD", hd), "ident"], w=[("pst", ti)], inc=(hd == 3))
            P.op("scalar", lambda E, tp=tp, kb=kb: E.activation(out=KDt[kb], in_=tp[:, 0:512].rearrange("p (h k) -> p h k", k=128),
                                                                func=AF.Copy),
                 r=[("pst", ti)], w=[("KDt", kb)])
            for n in range(NCH if full else 0):
                if n % 2 == 0:
                    P.op("gpsimd", lambda E, t=t, n=n: E.tensor_scalar(out=Vm[:, n, :], in0=V[:, t, :],
                                                                       scalar1=H.rowmask[:, n:n + 1], scalar2=1.0,
                                                                       op0=ALU.mult, op1=ALU.mult),
                         r=[("V", t), "rowmask"], w=[("Vm", n)])
                else:
                    P.op("scalar", lambda E, t=t, n=n: E.activation(out=Vm[:, n, :], in_=V[:, t, :], func=AF.Copy,
                                                                    scale=H.rowmask[:, n:n + 1]),
                         r=[("V", t), "rowmask"], w=[("Vm", n)])
            if full:
                for hd in range(4):
                    P.op("vector", lambda E, hd=hd, t=t: E.tensor_tensor(
                        out=QHm[:, hd, :, :], in0=QH[:, hd, t * 128:(t + 1) * 128].unsqueeze(1).to_broadcast([128, NCH, 128]),
                        in1=H.cmask, op=ALU.mult), r=[("QH", hd), "cmask"], w=[("QHm", hd)])
                a_ap, a_keys = hv.bank(5 - (t % 2))
                for hd in range(4):
                    P.op("tensor", lambda E, hd=hd, t=t, a_ap=a_ap: E.matmul(
                        a_ap[:, hd * 128:(hd + 1) * 128], lhsT=KH[:, hd, t * 128:(t + 1) * 128],
                        rhs=QH[:, hd, t * 128:(t + 1) * 128], start=True, stop=True),
                        r=[("KH", hd), ("QH", hd)], w=a_keys, inc=(hd == 3))
                P.op("vector", lambda E, a_ap=a_ap, kb=kb: E.tensor_tensor(
                    out=ATs[kb], in0=a_ap.rearrange("p (h t) -> p h t", t=128),
                    in1=H.maskBD.unsqueeze(1).to_broadcast([128, 4, 128]), op=ALU.mult),
                    r=a_keys + ["maskBD"], w=[("ATs", kb)])
                o_ap, o_keys = hv.bank(4 + (t % 2))
                for hd in range(4):
                    P.op("tensor", lambda E, hd=hd, t=t, o_ap=o_ap, kb=kb: E.matmul(
                        o_ap[:, hd * 128:(hd + 1) * 128], lhsT=ATs[kb][:, hd, :], rhs=V[:, t, hd * 128:(hd + 1) * 128],
                        start=(hd == 0), stop=False), r=[("ATs", kb), ("V", t)], w=o_keys, inc=False)
            for n in range(nch):
                for hd in range(4):
                    u_ap, u_keys = hv.bank(hd)
                    vrhs = Vm[:, n, hd * 128:(hd + 1) * 128] if full else V[:, t, hd * 128:(hd + 1) * 128]
                    vkey = ("Vm", n) if full else ("V", t)
                    if full:
                        cb = sbi[hd]
                        P.op("tensor", lambda E, hd=hd, n=n, cb=cb, o_ap=o_ap: E.matmul(
                            o_ap[:, hd * 128:(hd + 1) * 128], lhsT=QHm[:, hd, n, :], rhs=Sb[cb][:, hd, :],
                            start=False, stop=(n == NCH - 1 and hd == 3)), r=[("QHm", hd), ("Sb", cb, hd)], w=o_keys,
                            inc=False)
                    P.op("tensor", lambda E, hd=hd, kb=kb, u_ap=u_ap, vrhs=vrhs: E.matmul(
                        u_ap[:, 0:128], lhsT=KDt[kb][:, hd, :], rhs=vrhs,
                        start=True, stop=True), r=[("KDt", kb), vkey], w=u_keys, inc=True)
                    P.op("vector", lambda E, hd=hd, t=t, n=n, u_ap=u_ap: E.scalar_tensor_tensor(
                        out=S[:, hd, :], in0=S[:, hd, :], scalar=Dn[:, hd, nch * t + n:nch * t + n + 1],
                        in1=u_ap[:, 0:128], op0=ALU.mult, op1=ALU.add),
                        r=u_keys + [("Dn", hd), ("S", hd)], w=[("S", hd)])
                    if full:
                        nb_ = 1 - sbi[hd]
                        P.op("scalar", lambda E, hd=hd, nb_=nb_: E.activation(out=Sb[nb_][:, hd, :], in_=S[:, hd, :], func=AF.Copy),
                             r=[("S", hd)], w=[("Sb", nb_, hd)])
                        sbi[hd] = nb_
            if full:
                yb = ysb[t % 2]
                for hd in range(4):
                    P.op("scalar", lambda E, hd=hd, o_ap=o_ap, tile=tile: E.activation(
                        out=C.junk[:, hd * 128:(hd + 1) * 128], in_=o_ap[:, hd * 128:(hd + 1) * 128], func=AF.Square,
                        accum_out=ssh[:, tile, hd:hd + 1]), r=o_keys, w=[("ssh", tile, hd)])
                P.op("scalar", lambda E, tile=tile: E.activation(out=rsh[:, tile, :], in_=ssh[:, tile, :], func=AF.Sqrt,
                                                                  bias=H.epsh, scale=1.0 / 128),
                     r=[("ssh", tile, hd) for hd in range(4)] + ["epsh"], w=[("rsh", tile)])
                P.op("vector", lambda E, tile=tile: E.reciprocal(out=rsh[:, tile, :], in_=rsh[:, tile, :]),
                     r=[("rsh", tile)], w=[("rsh", tile)])
                for hd in range(4):
                    P.op("vector", lambda E, hd=hd, o_ap=o_ap, tile=tile, t=t, yb=yb: E.scalar_tensor_tensor(
                        out=yb[:, hd * 128:(hd + 1) * 128], in0=o_ap[:, hd * 128:(hd + 1) * 128],
                        scalar=rsh[:, tile, hd:hd + 1], in1=G[:, t, hd * 128:(hd + 1) * 128], op0=ALU.mult, op1=ALU.mult),
                        r=o_keys + [("rsh", tile), ("G", t)], w=[("ysb", t % 2)])
                ti2 = next_pst(C)
                tp2 = C.pst[ti2]
                for hd in range(4):
                    P.op("tensor", lambda E, hd=hd, tp2=tp2, yb=yb: E.transpose(tp2[:, hd * 128:(hd + 1) * 128],
                                                                                  yb[:, hd * 128:(hd + 1) * 128], C.ident),
                         r=[("ysb", t % 2), "ident"], w=[("pst", ti2)], inc=(hd == 3))
                P.op("vector", lambda E, tp2=tp2, t=t: E.tensor_copy(
                    out=yT[:, 4:8, t * 128:(t + 1) * 128], in_=tp2[:, 0:512].rearrange("p (h t) -> p h t", t=128)),
                    r=[("pst", ti2)], w=[("yT", 4, t)])
        if full:
            for t in range(4):
                pi = next_psb(C)
                ps = C.psb[pi]
                for ch, (wo_, wok_) in enumerate(((wo0, wok0), (wo1, wok1))):
                    for c in range(KC):
                        P.op("tensor", lambda E, ch=ch, c=c, t=t, ps=ps, wo_=wo_: E.matmul(
                            ps[:, ch * 512:(ch + 1) * 512], lhsT=yT[:, c, t * 128:(t + 1) * 128], rhs=wo_[:, c, :],
                            start=(c == 0), stop=(c == KC - 1)),
                            r=[wok_, ("yT", c) if c < 4 else ("yT", 4, t)], w=psbk(pi), inc=(ch == 1 and c == KC - 1))
                post_tile(C, M, pi, 4 * b + t, po)
            if issued[0] == used[0]:
                issue_next()

    if not full:
        P.op("scalar", lambda E: E.activation(out=btot, in_=btot, func=AF.Exp), r=["btot"], w=["btot"])
        P.dma("sync", io["S_out"].ap(), S, r=Skeys)
        P.dma("sync", io["D_out"].ap(), btot, r=["btot"])
    P.barrier()
    A.release()


DIL = (1, 4, 16)
NEG = -30000.0
DBG = {}


def attn_qkv(C, layer):
    P, A, io = C.P, C.A, C.io
    ja = layer // 2
    A.mark()
    M = modulation(C, 2 * layer)
    hv = Halves(C)
    hT = A.alloc([KC, NT], BF16)
    pn = alloc_pn_scratch(C)
    wbuf = [A.alloc([KC, 512], BF16) for _ in range(2)]
    stg = [A.alloc([4, NT], BF16) for _ in range(2)]
    vst = [A.alloc([NTILE, 512], BF16) for _ in range(2)]
    wv = io["att_w_qkv"].ap()[C.sel["att"].index(ja)].rearrange("(k p) n -> p k n", p=128)
    ev = 0
    for piece in range(9):
        sidx, g = piece // 3, piece % 3
        wb = wbuf[piece % 2]
        wk = ("aw", piece % 2)
        P.dma("gpsimd", wb, wv[:, :, piece * 512:(piece + 1) * 512], w=[wk])
        if sidx < 2:
            st = stg[piece % 2]
            sk = ("stg", piece % 2)
            for tg in range(4):
                if piece == 0:
                    tl = [(C.x[:, tg * 4 + i, :], ("x", tg * 4 + i)) for i in range(4)]
                    prenorm_tiles(C, M, tl, hT, 512 * tg, pn)
                for cc in range(4):
                    ap, keys = hv.next()
                    hk = [("hT", tg * 4 + q) for q in range(4)]
                    for kc in range(KC):
                        P.op("tensor", lambda E, kc=kc, ap=ap, wb=wb, cc=cc, tg=tg: E.matmul(
                            ap, lhsT=wb[:, kc, cc * 128:(cc + 1) * 128], rhs=hT[:, kc, tg * 512:(tg + 1) * 512],
                            start=(kc == 0), stop=(kc == KC - 1)), r=[wk] + [(k_[0], k_[1], kc) for k_ in hk], w=keys, inc=(kc == KC - 1))
                    sc = 0.125 if sidx == 0 else 1.0
                    if ev % 2 == 0:
                        P.op("scalar", lambda E, ap=ap, st=st, cc=cc, tg=tg, sc=sc: E.activation(
                            out=st[:, cc, tg * 512:(tg + 1) * 512], in_=ap, func=AF.Copy, scale=sc), r=keys, w=[sk + (cc, tg)])
                    else:
                        P.op("vector", lambda E, ap=ap, st=st, cc=cc, tg=tg, sc=sc: E.tensor_scalar(
                            out=st[:, cc, tg * 512:(tg + 1) * 512], in0=ap, scalar1=sc, scalar2=1.0,
                            op0=ALU.mult, op1=ALU.mult), r=keys, w=[sk + (cc, tg)])
                    ev += 1
            dst = io["qT_out" if sidx == 0 else "kT_out"].ap()[g].rearrange("c p t -> p c t")
            P.dma("sync", dst, st, r=[sk + (cc_, tg_) for cc_ in range(4) for tg_ in range(4)])
        else:
            st = vst[piece % 2]
            sk = ("vst", piece % 2)
            for t in range(NTILE):
                ap, keys = hv.next()
                for kc in range(KC):
                    P.op("tensor", lambda E, kc=kc, ap=ap, wb=wb, t=t: E.matmul(
                        ap, lhsT=hT[:, kc, t * 128:(t + 1) * 128], rhs=wb[:, kc, :],
                        start=(kc == 0), stop=(kc == KC - 1)), r=[wk, ("hT", t, kc)], w=keys, inc=(kc == KC - 1))
                if ev % 2 == 0:
                    P.op("scalar", lambda E, ap=ap, st=st, t=t: E.activation(out=st[:, t, :], in_=ap, func=AF.Copy),
                         r=keys, w=[sk + (t,)])
                else:
                    P.op("vector", lambda E, ap=ap, st=st, t=t: E.tensor_copy(out=st[:, t, :], in_=ap), r=keys, w=[sk + (t,)])
                ev += 1
            dst = io["v_out"].ap()[g].rearrange("(n p) f -> p n f", p=128)
            P.dma("sync", dst, st, r=[sk + (t_,) for t_ in range(NTILE)])
    P.barrier()
    A.release()


def attn_bias_tables(C):
    P, A, io = C.P, C.A, C.io
    A.mark()
    hv = Halves(C)
    tab = A.alloc([24], F32)
    oh = A.alloc([3 * 510], F32)
    ngm = A.alloc([3 * 510], F32)
    wsb = A.alloc([3 * 510], F32)
    P.dma("sync", tab[0:32, :], io["rel_bias"].ap(), w=["tab"])
    P.dma("sync", oh[0:32, :], io["bias_onehot"].ap(), w=["oh"])
    P.dma("sync", ngm[0:24, :], io["bias_neg"].ap(), w=["ngm"])
    for g in range(3):
        ap, keys = hv.next()
        P.op("tensor", lambda E, ap=ap, g=g: E.matmul(ap[0:24, 0:510], lhsT=tab[0:32, :], rhs=oh[0:32, g * 510:(g + 1) * 510],
                                                       start=True, stop=True), r=["tab", "oh"], w=keys)
        P.op("vector", lambda E, ap=ap, g=g: E.tensor_tensor(out=wsb[0:24, g * 510:(g + 1) * 510], in0=ap[0:24, 0:510],
                                                              in1=ngm[0:24, g * 510:(g + 1) * 510], op=ALU.add),
             r=keys + ["ngm"], w=["wsb"])
    P.dma("sync", C.wd.ap(), wsb[0:24, :], r=["wsb"], w=["wd"])
    P.barrier()
    A.release()


def attn_core(C, layer):
    P, A, io = C.P, C.A, C.io
    ja = layer // 2
    A.mark()
    M = modulation(C, 2 * layer)
    hv = Halves(C)
    acc = A.alloc([2, NT], F32)
    yT = A.alloc([8, NT], BF16)
    sel = A.alloc([64], F32)
    hb = A.alloc([1], F32)
    zb = A.alloc([1], F32)
    hm01 = A.alloc([1], F32)
    rec = [A.alloc([512], F32) for _ in range(2)]
    A.mark()
    qb = [A.alloc([2, 16, 128], BF16) for _ in range(2)]
    kb_ = [A.alloc([2, 16, 128], BF16) for _ in range(2)]
    khb = [A.alloc([2, 16, 128], BF16) for _ in range(2)]
    vb = [A.alloc([16, 2, 65], BF16) for _ in range(2)]
    vhb = [A.alloc([16, 2, 65], BF16) for _ in range(2)]
    btb = [A.alloc([2, 256], F32) for _ in range(2)]
    Tb = [A.alloc([512], F32) for _ in range(2)]
    Pb = [A.alloc([512], BF16) for _ in range(4)]
    ebh = [A.alloc([2, 256], F32) for _ in range(2)]
    P.dma("sync", sel[0:65, :], io["sel65"].ap(), w=["sel"])
    P.dma("sync", hb, io["halo_bias"].ap(), w=["hb"])
    P.op("gpsimd", lambda E: E.memset(zb, 0.0), w=["zb"])
    P.op("vector", lambda E: E.tensor_scalar(out=hm01, in0=hb, scalar1=-1.0 / NEG, scalar2=1.0, op0=ALU.mult, op1=ALU.add),
         r=["hb"], w=["hm01"])
    wov = io["att_w_out"].ap()[C.sel["att"].index(ja)].rearrange("(h e) n -> e h n", e=64)
    NH = (1, 4, 16)
    it = 0
    for c in range(4):
        for g in range(3):
            b = it % 2
            it += 1
            dil = DIL[g]
            nh = NH[g]
            P.dma("sync", qb[b][0:64], io["qn"].ap()[g, 2 * c:2 * c + 2].rearrange("h p b t -> p h b t"), w=[("qb", b)])
            P.dma("sync", kb_[b][0:64], io["kn"].ap()[g, 2 * c:2 * c + 2].rearrange("h p b t -> p h b t"), w=[("kb", b)])
            P.dma("sync", khb[b][0:64, :, 0:nh, :], io["kh%d" % g].ap()[2 * c:2 * c + 2].rearrange("h p b t -> p h b t"),
                  w=[("khb", b)])
            if DBG.get("nov"):
                P.op("gpsimd", lambda E, b=b: E.memset(vb[b], 1.0), w=[("vb", b)])
                P.op("gpsimd", lambda E, b=b: E.memset(vhb[b], 1.0), w=[("vhb", b)])
            else:
                P.dma("sync", vb[b], io["vn"].ap()[g, :, :, 2 * c:2 * c + 2, :], w=[("vb", b)])
                P.dma("sync", vhb[b][:, 0:nh, :, :], io["vh%d" % g].ap()[:, :, 2 * c:2 * c + 2, :], w=[("vhb", b)])
            for hh in range(2):
                for part in range(2):
                    if DBG.get("nobias"):
                        P.op("gpsimd", lambda E, b=b, hh=hh, part=part: E.memset(btb[b][:, hh, part * 128:(part + 1) * 128], 0.0),
                             w=[("btb", b)])
                        continue
                    src = bass.AP(C.wd, (2 * c + hh + 8 * g) * 1530 + g * 510 + part * 255, [[1, 128], [1, 128]])
                    P.dma("sync", btb[b][:, hh, part * 128:(part + 1) * 128], src, r=["wd"], w=[("btb", b)])
            P.op("scalar", lambda E, b=b: E.activation(out=btb[b], in_=btb[b], func=AF.Exp, bias=zb), r=[("btb", b), "zb"], w=[("btb", b)])
            P.op("vector", lambda E, b=b: E.tensor_scalar(out=ebh[b][:, :, 0:128], in0=btb[b][:, :, 0:128], scalar1=hm01[:, 0:1],
                                                      scalar2=1.0, op0=ALU.mult, op1=ALU.mult), r=[("btb", b), "hm01"], w=[("ebh", b)])
            P.op("gpsimd", lambda E, b=b: E.tensor_copy(out=ebh[b][:, :, 128:256], in_=btb[b][:, :, 128:256]), r=[("btb", b)], w=[("ebh", b)])
            nres = dil
            nblk = 16 // dil
            units = [(r, n) for r in range(nres) for n in range(nblk)]
            if DBG.get("g0only") and g > 0:
                units = []
            if DBG.get("noattn"):
                if g == 0:
                    P.op("gpsimd", lambda E: E.memset(acc[0:65, :, :], 1.0), w=[("acc", q) for q in range(4)])
                units = []
            st = {}

            def stage_a(i):
                r, n = units[i]
                bid = r * nblk + n
                halo = (n == 0)
                s_ap, s_keys = hv.next()
                st[i] = (s_ap, s_keys)
                for hh in range(2):
                    kprev = khb[b][0:64, hh, r, :] if halo else kb_[b][0:64, hh, bid - 1, :]
                    P.op("tensor", lambda E, b=b, hh=hh, kprev=kprev, bid=bid, s_ap=s_ap: E.matmul(
                        s_ap[:, hh * 256:hh * 256 + 128], lhsT=kprev, rhs=qb[b][0:64, hh, bid, :], start=True, stop=True),
                        r=[("khb", b), ("kb", b), ("qb", b)], w=s_keys, inc=False)
                    P.op("tensor", lambda E, b=b, hh=hh, bid=bid, s_ap=s_ap: E.matmul(
                        s_ap[:, hh * 256 + 128:hh * 256 + 256], lhsT=kb_[b][0:64, hh, bid, :], rhs=qb[b][0:64, hh, bid, :],
                        start=True, stop=True), r=[("kb", b), ("qb", b)], w=s_keys, inc=(hh == 1))

            def stage_b(i):
                r, n = units[i]
                halo = (n == 0)
                s_ap, s_keys = st[i]
                tb = i % 2
                pb = i % 4
                P.op("scalar", lambda E, b=b, s_ap=s_ap, tb=tb: E.activation(out=Tb[tb], in_=s_ap, func=AF.Exp, bias=zb),
                     r=s_keys + ["zb"], w=[("Tb", tb)])
                ebt = ebh[b] if halo else btb[b]
                ebk = ("ebh", b) if halo else ("btb", b)
                eng = "vector" if i % 2 == 0 else "gpsimd"
                P.op(eng, lambda E, b=b, tb=tb, pb=pb, ebt=ebt: E.tensor_tensor(
                    out=Pb[pb], in0=Tb[tb], in1=ebt.rearrange("p h c -> p (h c)"), op=ALU.mult),
                    r=[("Tb", tb), ebk], w=[("Pb", pb)])

            def stage_c(i):
                r, n = units[i]
                bid = r * nblk + n
                halo = (n == 0)
                tb = i % 4
                o_ap, o_keys = hv.next()
                for hh in range(2):
                    vprev = vhb[b][:, r, hh, :] if halo else vb[b][:, bid - 1, hh, :]
                    P.op("tensor", lambda E, b=b, hh=hh, vprev=vprev, o_ap=o_ap, tb=tb: E.matmul(
                        o_ap[0:65, hh * 128:(hh + 1) * 128], lhsT=vprev, rhs=Pb[tb][:, hh * 256:hh * 256 + 128],
                        start=True, stop=False), r=[("vhb", b), ("vb", b), ("Pb", tb)], w=o_keys, inc=False)
                    P.op("tensor", lambda E, b=b, hh=hh, bid=bid, o_ap=o_ap, tb=tb: E.matmul(
                        o_ap[0:65, hh * 128:(hh + 1) * 128], lhsT=vb[b][:, bid, hh, :], rhs=Pb[tb][:, hh * 256 + 128:hh * 256 + 256],
                        start=False, stop=True), r=[("vb", b), ("Pb", tb)], w=o_keys, inc=(hh == 1))
                t0 = dil * 128 * n + r
                dst = acc[0:65, :, t0:t0 + dil * 127 + 1:dil]
                srcp = o_ap[0:65, 0:256].rearrange("p (h t) -> p h t", t=128)
                akeys = [("acc", q) for q in range(4)] if dil == 16 else [("acc", (dil * 128 * n) // 512)]
                if g == 0:
                    P.op("scalar", lambda E, b=b, dst=dst, srcp=srcp: E.activation(out=dst, in_=srcp, func=AF.Copy),
                         r=o_keys, w=akeys)
                else:
                    P.op("vector", lambda E, b=b, dst=dst, srcp=srcp: E.tensor_tensor(out=dst, in0=dst, in1=srcp, op=ALU.add),
                         r=o_keys + akeys, w=akeys)

            nu = len(units)
            for i in range(nu + 4):
                if i < nu:
                    stage_a(i)
                if 0 <= i - 2 < nu:
                    stage_b(i - 2)
                if 0 <= i - 4 < nu:
                    stage_c(i - 4)
        for hh in range(2):
            for tq in range(4):
                if DBG.get("nonorm"):
                    P.op("gpsimd", lambda E, hh=hh, tq=tq, c=c: E.tensor_copy(
                        out=yT[0:64, 2 * c + hh, tq * 512:(tq + 1) * 512], in_=acc[0:64, hh, tq * 512:(tq + 1) * 512]),
                        r=[("acc", tq)], w=[("yT", 2 * c + hh, tq)])
                    continue
                l_ap, l_keys = hv.next()
                P.op("tensor", lambda E, l_ap=l_ap, hh=hh, tq=tq: E.matmul(
                    l_ap[0:64, :], lhsT=sel[64:65, :], rhs=acc[64:65, hh, tq * 512:(tq + 1) * 512], start=True, stop=True),
                    r=["sel", ("acc", tq)], w=l_keys)
                rb = (hh * 4 + tq) % 2
                P.op("vector", lambda E, l_ap=l_ap, rb=rb: E.reciprocal(out=rec[rb][0:64, :], in_=l_ap[0:64, :]),
                     r=l_keys, w=[("rec", rb)])
                P.op("vector", lambda E, rb=rb, hh=hh, tq=tq, c=c: E.tensor_tensor(
                    out=yT[0:64, 2 * c + hh, tq * 512:(tq + 1) * 512], in0=acc[0:64, hh, tq * 512:(tq + 1) * 512],
                    in1=rec[rb][0:64, :], op=ALU.mult), r=[("rec", rb), ("acc", tq)], w=[("yT", 2 * c + hh, tq)])
    P.barrier()
    A.release()
    wo = A.alloc([8, D], BF16)
    po = alloc_post_scratch(C)
    P.dma("gpsimd", wo[0:64, :, :], wov, w=["wo"])
    for t in range(NTILE):
        pi = next_psb(C)
        ps = C.psb[pi]
        for ch in range(2):
            for h in range(8):
                P.op("tensor", lambda E, ch=ch, h=h, t=t, ps=ps: E.matmul(
                    ps[:, ch * 512:(ch + 1) * 512], lhsT=yT[0:64, h, t * 128:(t + 1) * 128],
                    rhs=wo[0:64, h, ch * 512:(ch + 1) * 512], start=(h == 0), stop=(h == 7)),
                    r=["wo", ("yT", h, t // 4)], w=psbk(pi), inc=(ch == 1 and h == 7))
        post_tile(C, M, pi, t, po)
    P.barrier()
    A.release()


def declare_io(nc, names_shapes_in, names_shapes_out):
    io = {}
    for name, shape, dt in names_shapes_in:
        io[name] = nc.dram_tensor(name, list(shape), dt, kind="ExternalInput")
    for name, shape, dt in names_shapes_out:
        io[name] = nc.dram_tensor(name, list(shape), dt, kind="ExternalOutput")
    return io


HYB_IN = [
    ("hyb_w_in", (1, D, 2560), F32),
    ("hyb_w_out", (1, D, D), F32),
    ("pool_w", (1, 4, 128, 128), F32),
    ("pool_scale_col", (1, 128, 4), F32),
    ("hgrn_gain4", (512,), F32),
    ("lb_logits_col", (2, 128, 4), F32),
    ("maskBD", (128, 128), F32),
    ("cmask", (128, NCH, 128), F32),
    ("rowmask", (128, NCH), F32),
    ("smask", (128, 512), F32),
    ("smask128", (128, 512), F32),
    ("invcnt", (128, 4, 16), F32),
    ("hmask", (128, 1), F32),
]


def hybrid_inputs(inp, core):
    p = np.arange(128)
    maskBD = ((p[:, None] // CH == p[None, :] // CH) & (p[:, None] <= p[None, :])).astype(np.float32)
    cmask = np.zeros((128, NCH, 128), np.float32)
    for n in range(NCH):
        cmask[:, n, CH * n:CH * n + CH] = 1.0
    rowmask = (p[:, None] // CH == np.arange(NCH)[None, :]).astype(np.float32)
    smask = np.ones((128, 512), np.float32)
    smask[:, ::CH] = 0.0
    smask128 = np.ones((128, 512), np.float32)
    smask128[:, ::128] = 0.0
    invcnt = np.zeros((128, 4, 16), np.float32)
    for g in range(4):
        win = 2 << g
        if core == 0:
            invcnt[:, g, :] = 1.0 / np.minimum(np.arange(16) + 1, win)
        else:
            invcnt[:, g, :] = 1.0 / win
    return {
        "maskBD": maskBD, "cmask": cmask, "rowmask": rowmask, "smask": smask, "smask128": smask128, "invcnt": invcnt,
        "hmask": np.full((128, 1), 0.0 if core == 0 else 1.0, np.float32),
    }


def x_halo_tile(xfull, core):
    t = np.zeros((128, D), np.float32)
    if core > 0:
        t[112:128] = xfull[core * NT - 16:core * NT]
    return t


def pred_states(S_list, D_list, core):
    pS = np.zeros((7, 128, 4, 128), np.float32)
    pD = np.ones((7, 128, 4), np.float32)
    for i in range(core):
        pS[i] = np.asarray(S_list[i]).reshape(128, 4, 128)
        pD[i] = np.asarray(D_list[i]).reshape(128, 4)
    return pS, pD


ATT_QKV_IN = [("att_w_qkv", (1, D, 4608), F32)]
ATT_QKV_OUT = [("qT_out", (3, 4, 128, NT), BF16), ("kT_out", (3, 4, 128, NT), BF16), ("v_out", (3, NT, 512), BF16)]
ATT_CORE_IN = [
    ("att_w_out", (1, 512, D), F32),
    ("rel_bias", (32, 24), F32),
    ("bias_onehot", (32, 1530), F32),
    ("bias_neg", (24, 1530), F32),
    ("sel65", (65, 64), F32),
    ("halo_bias", (128, 1), F32),
    ("qn", (3, 8, 64, 16, 128), BF16),
    ("kn", (3, 8, 64, 16, 128), BF16),
    ("vn", (3, 128, 16, 8, 65), BF16),
    ("kh0", (8, 64, 1, 128), BF16), ("kh1", (8, 64, 4, 128), BF16), ("kh2", (8, 64, 16, 128), BF16),
    ("vh0", (128, 1, 8, 65), BF16), ("vh1", (128, 4, 8, 65), BF16), ("vh2", (128, 16, 8, 65), BF16),
]


def t5_bucket_np(dist):
    dist = np.asarray(dist, np.int64)
    df = np.maximum(dist, 1).astype(np.float32)
    large = 16 + (np.log(df / np.float32(16)) / np.float32(np.log(2048 / 16)) * np.float32(16)).astype(np.int32)
    large = np.minimum(large, 31)
    return np.where(dist < 16, dist, large)


def attn_consts(inp, core):
    oh = np.zeros((32, 1530), np.float32)
    ng = np.zeros((24, 1530), np.float32)
    m = np.arange(255)
    for g in range(3):
        dil = DIL[g]
        dp = 1 + m
        bp = t5_bucket_np(dp * dil)
        do = m - 127
        bo = t5_bucket_np(np.maximum(do, 0) * dil)
        for mm in range(255):
            if mm <= 127:
                oh[bp[mm], g * 510 + mm] = 1.0
            else:
                ng[:, g * 510 + mm] = NEG
            if mm >= 127:
                oh[bo[mm], g * 510 + 255 + mm] = 1.0
            else:
                ng[:, g * 510 + 255 + mm] = NEG
    sel = np.zeros((65, 64), np.float32)
    sel[64, :] = 1.0
    return {
        "att_w_out": np.ascontiguousarray(inp["att_w_out"], np.float32),
        "rel_bias": np.ascontiguousarray(inp["rel_bias"], np.float32),
        "bias_onehot": oh, "bias_neg": ng, "sel65": sel,
        "halo_bias": np.full((128, 1), NEG if core == 0 else 0.0, np.float32),
    }


def block_tokens(g):
    dil = DIL[g]
    nblk = 16 // dil
    idx = np.zeros((16, 128), np.int64)
    for r in range(dil):
        for n in range(nblk):
            idx[r * nblk + n] = dil * (128 * n + np.arange(128)) + r
    return idx


def attn_layout(qT, kT, v, kT_prev, v_prev):
    out = {}
    bf = qT.dtype
    qn = np.zeros((3, 8, 64, 16, 128), bf)
    kn = np.zeros((3, 8, 64, 16, 128), bf)
    qT = qT.reshape(3, 8, 64, NT)
    kT = kT.reshape(3, 8, 64, NT)
    if kT_prev is not None:
        kT_prev = kT_prev.reshape(3, 8, 64, NT)
    vn = np.zeros((3, 128, 16, 8, 65), bf)
    NHs = (1, 4, 16)
    for g in range(3):
        idx = block_tokens(g)
        ridx = idx[:, ::-1]
        qn[g] = qT[g][:, :, idx]
        kn[g] = kT[g][:, :, ridx]
        vg = v[g][ridx]
        vn[g, :, :, :, 0:64] = vg.reshape(16, 128, 8, 64).transpose(1, 0, 2, 3)
        vn[g, :, :, :, 64] = 1.0
        nh = NHs[g]
        nblk = 16 // DIL[g]
        kh = np.zeros((8, 64, nh, 128), bf)
        vh = np.zeros((128, nh, 8, 65), bf)
        if kT_prev is not None:
            last = np.array([r * nblk + (nblk - 1) for r in range(DIL[g])])
            hidx = ridx[last]
            kh[:] = kT_prev[g][:, :, hidx]
            vhh = v_prev[g][hidx]
            vh[:, :, :, 0:64] = vhh.reshape(nh, 128, 8, 64).transpose(1, 0, 2, 3)
            vh[:, :, :, 64] = 1.0
        out["kh%d" % g] = kh
        out["vh%d" % g] = vh
    out["qn"], out["kn"], out["vn"] = qn, kn, vn
    return out


def common_spec(nls):
    return [
        ("ident", (128, 128), F32),
        ("c_col", (128, KC), F32),
        ("ada_w", (nls, D, 3 * D), F32),
        ("ada_b", (nls * 3 * D,), F32),
        ("ada_b_col", (nls, 128, 24), F32),
        ("norm_pre_col", (nls, 128, KC), F32),
        ("norm_post", (nls * D,), F32),
    ]


def common_inputs(inp, lss):
    c = np.asarray(inp["c"], np.float32).reshape(D)
    aw = np.asarray(inp["ada_w"], np.float32).reshape(8, D, 3 * D)
    ab = np.asarray(inp["ada_b"], np.float32).reshape(8, 3 * D)
    npre = np.asarray(inp["norm_pre"], np.float32).reshape(8, D)
    npost = np.asarray(inp["norm_post"], np.float32).reshape(8, D)
    return {
        "ident": np.eye(128, dtype=np.float32),
        "c_col": np.ascontiguousarray(c.reshape(KC, 128).T),
        "ada_w": np.ascontiguousarray(aw[lss]),
        "ada_b": np.ascontiguousarray(ab[lss].reshape(-1)),
        "ada_b_col": np.ascontiguousarray(ab[lss].reshape(len(lss), 24, 128).transpose(0, 2, 1)),
        "norm_pre_col": np.ascontiguousarray(npre[lss].reshape(len(lss), KC, 128).transpose(0, 2, 1)),
        "norm_post": np.ascontiguousarray(npost[lss].reshape(-1)),
    }


def hyb_weights(inp, j):
    return {
        "hyb_w_in": np.ascontiguousarray(np.asarray(inp["hyb_w_in"], np.float32)[j:j + 1]),
        "hyb_w_out": np.ascontiguousarray(np.asarray(inp["hyb_w_out"], np.float32)[j:j + 1]),
        "pool_w": np.ascontiguousarray(np.asarray(inp["pool_w"], np.float32)[j:j + 1]),
        "pool_scale_col": np.ascontiguousarray(np.asarray(inp["pool_scale"], np.float32)[j].reshape(1, 4, 128).transpose(0, 2, 1)),
        "hgrn_gain4": np.ascontiguousarray(np.tile(np.asarray(inp["hgrn_out_norm"], np.float32)[j], 4)),
        "lb_logits_col": np.ascontiguousarray(np.asarray(inp["hgrn_lb_logits"], np.float32).reshape(2, 4, 128).transpose(0, 2, 1)),
    }


def build_launch(stages):
    lss, ffn, hyb, att = [], [], [], []
    ins, outs = [("x", (NT, D), F32)], []
    kinds = [k for k, _ in stages]
    for kind, layer in stages:
        if kind in ("hyb_state", "hyb_full"):
            if 2 * layer not in lss:
                lss.append(2 * layer)
            if layer // 2 not in hyb:
                hyb.append(layer // 2)
        elif kind == "ffn":
            lss.append(2 * layer + 1)
            ffn.append(layer)
        elif kind in ("qkv", "att"):
            if 2 * layer not in lss:
                lss.append(2 * layer)
            att.append(layer // 2)
    ins += common_spec(len(lss))
    if hyb:
        ins += HYB_IN
    if "hyb_full" in kinds:
        ins += [("x_halo", (128, D), F32), ("pred_S", (7, 128, 4, 128), F32), ("pred_D", (7, 128, 4), F32)]
    if "hyb_state" in kinds:
        outs += [("S_out", (128, 4, 128), F32), ("D_out", (128, 4), F32)]
    if ffn:
        ins += [("ffn_w_in", (1, D, 2 * DFF), F32), ("ffn_w_out", (1, DFF, D), F32)]
    if "qkv" in kinds:
        ins += ATT_QKV_IN
        outs += ATT_QKV_OUT
    if "att" in kinds:
        ins += ATT_CORE_IN
    if kinds != ["hyb_state"]:
        outs += [("y", (NT, D), F32)]
    nc = bass.Bass("TRN2", target_bir_lowering=False)
    io = declare_io(nc, ins, outs)
    with ExitStack() as es:
        P = Prog(nc, es)
        A = Arena(nc, es, 212000)
        C = setup_common(nc, es, P, A, io)
        C.sel = {"ls": lss, "ffn": ffn, "hyb": hyb, "att": att}
        if "att" in kinds:
            C.wd = nc.dram_tensor("wd", [24, 1530], F32)
        load_x(C)
        stored = False
        for kind, layer in stages:
            if kind == "hyb_state":
                if len(stages) > 1 and not stored:
                    store_x(C)
                    stored = True
                hybrid_mixer(C, layer, False)
            elif kind == "hyb_full":
                hybrid_mixer(C, layer, True)
            elif kind == "ffn":
                ffn_sublayer(C, layer)
            elif kind == "qkv":
                store_x(C)
                stored = True
                attn_qkv(C, layer)
            elif kind == "att":
                if not DBG.get("nobias"):
                    attn_bias_tables(C)
                attn_core(C, layer)
        if not stored and kinds != ["hyb_state"]:
            store_x(C)
        P.barrier()
        P.replay()
    return nc


_PROGS = {}


def get_prog(stages):
    key = tuple(stages)
    if key not in _PROGS:
        _PROGS[key] = build_launch(list(stages))
    return _PROGS[key]


def run_launch(stages, in_maps):
    nc = get_prog(stages)
    res = run_bass_kernel_spmd(nc, in_maps, core_ids=list(range(NCORES)))
    return res.results


def kernel(**inp):
    inp = {k: np.asarray(v) for k, v in inp.items()}
    x = np.ascontiguousarray(inp["x"].reshape(SEQ, D).astype(np.float32))

    def xs(xa, c):
        return np.ascontiguousarray(xa[c * NT:(c + 1) * NT])

    def ffn_w(layer):
        return {"ffn_w_in": np.ascontiguousarray(np.asarray(inp["ffn_w_in"], np.float32)[layer:layer + 1]),
                "ffn_w_out": np.ascontiguousarray(np.asarray(inp["ffn_w_out"], np.float32)[layer:layer + 1])}

    def att_w(ja):
        return {"att_w_qkv": np.ascontiguousarray(np.asarray(inp["att_w_qkv"], np.float32)[ja:ja + 1])}

    hconst = [hybrid_inputs(inp, c) for c in range(NCORES)]
    for hc in hconst:
        for k in ("hyb_w_in", "hyb_w_out", "pool_w", "pool_scale_col", "hgrn_gain4", "lb_logits_col"):
            hc.pop(k, None)
    aconst = [attn_consts(inp, c) for c in range(NCORES)]

    stages = (("hyb_state", 0),)
    com = common_inputs(inp, [0])
    hw = hyb_weights(inp, 0)
    maps = [dict(com, **hw, **hconst[c], x=xs(x, c)) for c in range(NCORES)]
    res = run_launch(stages, maps)
    S_list = [r["S_out"] for r in res]
    D_list = [r["D_out"] for r in res]
    xcur = x
    for lh in (0, 2):
        la = lh + 1
        stages = (("hyb_full", lh), ("ffn", lh), ("qkv", la))
        com = common_inputs(inp, [2 * lh, 2 * lh + 1, 2 * la])
        hw = hyb_weights(inp, lh // 2)
        fw = ffn_w(lh)
        aw = att_w(la // 2)
        maps = []
        for c in range(NCORES):
            pS, pD = pred_states(S_list, D_list, c)
            maps.append(dict(com, **hw, **fw, **aw, **hconst[c], x=xs(xcur, c), x_halo=x_halo_tile(xcur, c),
                             pred_S=pS, pred_D=pD))
        res = run_launch(stages, maps)
        xcur = np.concatenate([r["y"].reshape(NT, D) for r in res], 0)
        proj = [(r["qT_out"].reshape(3, 4, 128, NT), r["kT_out"].reshape(3, 4, 128, NT), r["v_out"].reshape(3, NT, 512))
                for r in res]
        last = (la == 3)
        stages = (("att", la), ("ffn", la)) if last else (("att", la), ("ffn", la), ("hyb_state", la + 1))
        lss = [2 * la, 2 * la + 1] + ([] if last else [2 * (la + 1)])
        com = common_inputs(inp, lss)
        fw = ffn_w(la)
        awo = {"att_w_out": np.ascontiguousarray(np.asarray(inp["att_w_out"], np.float32)[la // 2:la // 2 + 1])}
        hw = {} if last else hyb_weights(inp, (la + 1) // 2)
        maps = []
        for c in range(NCORES):
            q, k, v = proj[c]
            lay = attn_layout(q, k, v, proj[c - 1][1] if c > 0 else None, proj[c - 1][2] if c > 0 else None)
            m = dict(com, **fw, **aconst[c], **lay, x=xs(xcur, c))
            m.update(awo)
            if not last:
                m.update(hw)
                m.update(hconst[c])
            maps.append(m)
        res = run_launch(stages, maps)
        xcur = np.concatenate([r["y"].reshape(NT, D) for r in res], 0)
        if not last:
            S_list = [r["S_out"] for r in res]
            D_list = [r["D_out"] for r in res]
    return xcur.reshape(1, SEQ, D).astype(np.float32)
```

```python
import numpy as np
from contextlib import ExitStack
import concourse.bass as bass
import concourse.mybir as mybir
from concourse.bass_utils import run_bass_kernel_spmd

F32 = mybir.dt.float32
BF16 = mybir.dt.bfloat16
I32 = mybir.dt.int32
ALU = mybir.AluOpType
AF = mybir.ActivationFunctionType
AX = mybir.AxisListType

NCORES = 8
SEQ = 16384
NT = SEQ // NCORES
NTILE = NT // 128
D = 1024
KC = D // 128
DFF = 2816
FC = DFF // 128
EPS = 1e-6
CH = 16
NCH = 128 // CH


class Prog:
    ENGS = ("tensor", "vector", "scalar", "gpsimd", "sync")
    QUEUES = ("sync", "gpsimd", "scalar")
    NDMA = 8

    def __init__(self, nc, es):
        self.nc = nc
        self.streams = {e: [] for e in self.ENGS}
        self.sem = {e: es.enter_context(nc.semaphore("s_" + e)) for e in self.ENGS}
        self.cnt = {e: 0 for e in self.ENGS}
        self.seen = {e: {} for e in self.ENGS}
        self.dsem = {q: [es.enter_context(nc.semaphore("d_%s%d" % (q, i))) for i in range(self.NDMA)]
                     for q in self.QUEUES}
        self.dcnt = {q: [0] * self.NDMA for q in self.QUEUES}
        self.dnext = {q: 0 for q in self.QUEUES}
        self.lastw = {}
        self.readers = {}
        self.nops = 0

    def _semobj(self, semkey):
        if semkey[0] == "e":
            return self.sem[semkey[1]]
        return self.dsem[semkey[1]][semkey[2]]

    def _deps(self, r, w):
        deps = []
        for k in r:
            t = self.lastw.get(k)
            if t is not None:
                deps.append(t)
        for k in w:
            t = self.lastw.get(k)
            if t is not None:
                deps.append(t)
            rd = self.readers.get(k)
            if rd:
                deps.extend(rd.items())
        return deps

    def _emit_waits(self, eng, deps):
        need = {}
        for semkey, val in deps:
            if eng == "tensor" and semkey == ("e", "tensor"):
                continue
            if need.get(semkey, 0) < val:
                need[semkey] = val
        for semkey, val in need.items():
            if self.seen[eng].get(semkey, 0) >= val:
                continue
            self.seen[eng][semkey] = val
            s = self._semobj(semkey)
            self.streams[eng].append(lambda E, s=s, val=val: E.wait_ge(s, val))

    def _record(self, token, r, w):
        for k in w:
            self.lastw[k] = token
            self.readers[k] = {}
        for k in r:
            rd = self.readers.setdefault(k, {})
            if rd.get(token[0], 0) < token[1]:
                rd[token[0]] = token[1]

    def op(self, eng, fn, r=(), w=(), inc=True):
        self.nops += 1
        self._emit_waits(eng, self._deps(r, w))
        if inc:
            self.cnt[eng] += 1
            s = self.sem[eng]
            self.streams[eng].append(lambda E, fn=fn, s=s: fn(E).then_inc(s, 1))
            token = (("e", eng), self.cnt[eng])
        else:
            self.streams[eng].append(lambda E, fn=fn: fn(E))
            token = (("e", eng), self.cnt[eng] + 1)
        self._record(token, r, w)

    def dma(self, q, out, in_, r=(), w=()):
        self.nops += 1
        self._emit_waits(q, self._deps(r, w))
        i = self.dnext[q]
        self.dnext[q] = (i + 1) % self.NDMA
        if self.dcnt[q][i] > 0:
            self._emit_waits(q, [(("d", q, i), self.dcnt[q][i])])
        self.dcnt[q][i] += 16
        s = self.dsem[q][i]
        self.streams[q].append(lambda E, out=out, in_=in_, s=s: E.dma_start(out=out, in_=in_).then_inc(s, 16))
        token = (("d", q, i), self.dcnt[q][i])
        self._record(token, r, w)

    def barrier(self):
        deps = [(("e", e), self.cnt[e]) for e in self.ENGS if self.cnt[e] > 0]
        for q in self.QUEUES:
            for i in range(self.NDMA):
                if self.dcnt[q][i] > 0:
                    deps.append((("d", q, i), self.dcnt[q][i]))
        for e in self.ENGS:
            self._emit_waits(e, [d for d in deps if not (e == "tensor" and d[0] == ("e", "tensor"))])
        self.lastw = {}
        self.readers = {}

    def replay(self):
        nc = self.nc
        with nc.Block() as block:
            @block.tensor
            def _(E):
                for f in self.streams["tensor"]:
                    f(E)

            @block.vector
            def _(E):
                for f in self.streams["vector"]:
                    f(E)

            @block.scalar
            def _(E):
                for f in self.streams["scalar"]:
                    f(E)

            @block.gpsimd
            def _(E):
                for f in self.streams["gpsimd"]:
                    f(E)

            @block.sync
            def _(E):
                for f in self.streams["sync"]:
                    f(E)


class Arena:
    def __init__(self, nc, es, nbytes):
        self.words = nbytes // 4
        self.t = es.enter_context(nc.sbuf_tensor("arena", [128, self.words], F32))
        self.off = 0
        self.marks = []

    def alloc(self, shape, dtype):
        n = 1
        for s in shape:
            n *= s
        esz = 2 if dtype == BF16 else 4
        nwords = (n * esz + 3) // 4
        nwords = (nwords + 7) // 8 * 8
        assert self.off + nwords <= self.words, ("SBUF arena overflow", self.off, nwords, self.words)
        v = self.t[:, self.off:self.off + nwords]
        self.off += nwords
        if dtype != F32:
            v = v.bitcast(dtype)
        v = v[:, 0:n]
        if len(shape) == 2:
            v = v.rearrange("p (a b) -> p a b", b=shape[1])
        elif len(shape) == 3:
            v = v.rearrange("p (a b c) -> p a b c", b=shape[1], c=shape[2])
        return v

    def mark(self):
        self.marks.append(self.off)

    def release(self):
        self.off = self.marks.pop()


class Ctx:
    pass


def dram_bcast(handle, offset, n):
    return bass.AP(handle, offset, [[0, 128], [1, n]])


def setup_common(nc, es, P, A, io):
    C = Ctx()
    C.nc, C.P, C.A, C.io = nc, P, A, io
    C.x = A.alloc([NTILE, D], F32)
    C.ident = A.alloc([128], BF16)
    C.junk = A.alloc([D], BF16)
    C.cT = A.alloc([KC], BF16)
    C.cB = A.alloc([KC, 128], BF16)
    C.psb = [es.enter_context(nc.psum_tensor("psb%d" % i, [128, 1024], F32))[:, :] for i in range(3)]
    C.pst = [es.enter_context(nc.psum_tensor("pst%d" % i, [128, 1024], BF16))[:, :] for i in range(2)]
    C.psb_i = 0
    C.pst_i = 0
    C.epsc = A.alloc([1], F32)
    P.op("gpsimd", lambda E: E.memset(C.epsc, EPS), w=["epsc"])
    P.dma("gpsimd", C.ident, io["ident"].ap(), w=["ident"])
    ctmp = A.alloc([KC], F32)
    P.dma("sync", ctmp, io["c_col"].ap(), w=["ctmp"])
    csil = A.alloc([KC], F32)
    P.op("scalar", lambda E: E.activation(out=csil, in_=ctmp, func=AF.Silu), r=["ctmp"], w=["csil"])
    P.op("vector", lambda E: E.tensor_copy(out=C.cT, in_=csil), r=["csil"], w=["cT"])
    P.op("vector", lambda E: E.tensor_copy(out=C.cB, in_=csil.unsqueeze(2).to_broadcast([128, KC, 128])),
         r=["csil"], w=["cB"])
    return C


def next_psb(C):
    i = C.psb_i
    C.psb_i = (i + 1) % 3
    return i


def next_pst(C):
    i = C.pst_i
    C.pst_i = (i + 1) % 2
    return i


def load_x(C, name="x"):
    P = C.P
    xv = C.io[name].ap().rearrange("(n p) d -> p n d", p=128)
    for g in range(4):
        P.dma("sync", C.x[:, 4 * g:4 * g + 4, :], xv[:, 4 * g:4 * g + 4, :], w=[("x", i) for i in range(4 * g, 4 * g + 4)])


def store_x(C, name="y"):
    P = C.P
    yv = C.io[name].ap().rearrange("(n p) d -> p n d", p=128)
    for g in range(4):
        P.dma("sync", yv[:, 4 * g:4 * g + 4, :], C.x[:, 4 * g:4 * g + 4, :], r=[("x", i) for i in range(4 * g, 4 * g + 4)])


def modulation(C, ls):
    P, A, io = C.P, C.A, C.io
    M = Ctx()
    M.shiftT = A.alloc([KC], F32)
    M.AT = A.alloc([KC], F32)
    M.GG = A.alloc([D], F32)
    A.mark()
    wbuf = [A.alloc([KC, 512], BF16) for _ in range(6)]
    bcol = A.alloc([24], F32)
    gpre = A.alloc([KC], F32)
    bgate = A.alloc([D], F32)
    gpost = A.alloc([D], F32)
    modT = A.alloc([16], F32)
    key = ("mod", ls)
    li = C.sel["ls"].index(ls)
    wv = io["ada_w"].ap()[li].rearrange("(k p) n -> p k n", p=128)
    P.dma("sync", bcol, io["ada_b_col"].ap()[li], w=[key + ("bcol",)])
    P.dma("sync", gpre, io["norm_pre_col"].ap()[li], w=[key + ("gpre",)])
    P.dma("sync", bgate, dram_bcast(io["ada_b"], li * 3 * D + 2 * D, D), w=[key + ("bgate",)])
    P.dma("sync", gpost, dram_bcast(io["norm_post"], li * D, D), w=[key + ("gpost",)])
    pi = next_psb(C)
    ps = C.psb[pi]
    for piece in range(6):
        P.dma("gpsimd", wbuf[piece], wv[:, :, piece * 512:(piece + 1) * 512], w=[key + ("w", piece)])
    for piece in range(6):
        wb = wbuf[piece]
        wk = key + ("w", piece)
        if piece < 4:
            for jc in range(4):
                j = piece * 4 + jc
                for kc in range(KC):
                    P.op("tensor", lambda E, j=j, jc=jc, kc=kc, wb=wb: E.matmul(
                        ps[:, j:j + 1], lhsT=wb[:, kc, jc * 128:(jc + 1) * 128], rhs=C.cT[:, kc:kc + 1],
                        start=(kc == 0), stop=(kc == KC - 1)),
                        r=[wk, "cT"], w=psbk(pi), inc=(kc == KC - 1))
            if piece == 3:
                P.op("vector", lambda E: E.tensor_tensor(out=modT, in0=ps[:, 0:16], in1=bcol[:, 0:16], op=ALU.add),
                     r=psbk(pi) + [key + ("bcol",)], w=[key + ("modT",)])
                P.op("vector", lambda E: E.tensor_copy(out=M.shiftT, in_=modT[:, 0:8]),
                     r=[key + ("modT",)], w=[key + ("shiftT",)])
                P.op("vector", lambda E: E.scalar_tensor_tensor(out=M.AT, in0=modT[:, 8:16], scalar=1.0, in1=gpre,
                                                                  op0=ALU.add, op1=ALU.mult),
                     r=[key + ("modT",), key + ("gpre",)], w=[key + ("AT",)])
                pi2 = next_psb(C)
                ps2 = C.psb[pi2]
        else:
            ch = piece - 4
            for kc in range(KC):
                P.op("tensor", lambda E, ch=ch, kc=kc, wb=wb: E.matmul(
                    ps2[:, ch * 512:(ch + 1) * 512], lhsT=C.cB[:, kc, :], rhs=wb[:, kc, :],
                    start=(kc == 0), stop=(kc == KC - 1)),
                    r=[wk, "cB"], w=psbk(pi2), inc=(kc == KC - 1))
    P.op("vector", lambda E: E.tensor_tensor(out=M.GG, in0=ps2, in1=bgate, op=ALU.add),
         r=psbk(pi2) + [key + ("bgate",)], w=[key + ("GG",)])
    P.op("gpsimd", lambda E: E.tensor_tensor(out=M.GG, in0=M.GG, in1=gpost, op=ALU.mult),
         r=[key + ("GG",), key + ("gpost",)], w=[key + ("GG",)])
    M.key = key
    P.barrier()
    A.release()
    return M


def prenorm_p1(C, M, xa, xk, i, scratch):
    P = C.P
    ss, rstd, junk, xn = scratch
    P.op("scalar", lambda E: E.activation(out=junk[i % 2], in_=xa, func=AF.Square, accum_out=ss[:, i:i + 1]),
         r=[xk], w=[("pn_ss", i)])
    P.op("scalar", lambda E: E.activation(out=rstd[:, i:i + 1], in_=ss[:, i:i + 1], func=AF.Sqrt,
                                           bias=C.epsc, scale=1.0 / D),
         r=[("pn_ss", i), "epsc"], w=[("pn_rstd", i)])
    P.op("vector", lambda E: E.reciprocal(out=rstd[:, i:i + 1], in_=rstd[:, i:i + 1]),
         r=[("pn_rstd", i)], w=[("pn_rstd", i)])
    eng = "gpsimd"
    P.op(eng, lambda E: E.tensor_scalar(out=xn[i % 2], in0=xa, scalar1=rstd[:, i:i + 1],
                                         scalar2=1.0, op0=ALU.mult, op1=ALU.mult),
         r=[xk, ("pn_rstd", i)], w=[("pn_xn", i % 2)])


def prenorm_p2(C, M, hT, c0, i, scratch):
    P = C.P
    ss, rstd, junk, xn = scratch
    ti = next_pst(C)
    tp = C.pst[ti]
    for kc in range(KC):
        P.op("tensor", lambda E, kc=kc: E.transpose(tp[:, kc * 128:(kc + 1) * 128],
                                                     xn[i % 2][:, kc * 128:(kc + 1) * 128], C.ident),
             r=[("pn_xn", i % 2), "ident"], w=[("pst", ti)], inc=(kc == KC - 1))
    for kc in range(KC):
        if False:
            P.op("scalar", lambda E, kc=kc: E.activation(
                out=hT[:, kc, c0:c0 + 128], in_=tp[:, kc * 128:(kc + 1) * 128], func=AF.Identity,
                bias=M.shiftT[:, kc:kc + 1], scale=M.AT[:, kc:kc + 1]),
                r=[("pst", ti), M.key + ("shiftT",), M.key + ("AT",)], w=[("hT", c0 // 128, kc)])
        else:
            P.op("vector", lambda E, kc=kc: E.tensor_scalar(
                out=hT[:, kc, c0:c0 + 128], in0=tp[:, kc * 128:(kc + 1) * 128],
                scalar1=M.AT[:, kc:kc + 1], scalar2=M.shiftT[:, kc:kc + 1], op0=ALU.mult, op1=ALU.add),
                r=[("pst", ti), M.key + ("shiftT",), M.key + ("AT",)], w=[("hT", c0 // 128, kc)])


def prenorm_tiles(C, M, xsrc_tiles, hT, col0, scratch):
    for i, (xa, xk) in enumerate(xsrc_tiles):
        prenorm_p1(C, M, xa, xk, i, scratch)
        prenorm_p2(C, M, hT, col0 + 128 * i, i, scratch)


def alloc_pn_scratch(C):
    A = C.A
    ss = A.alloc([32], F32)
    rstd = A.alloc([32], F32)
    junk = [C.junk, C.junk]
    xn = [A.alloc([D], BF16) for _ in range(2)]
    return ss, rstd, junk, xn


def post_tile(C, M, pi, tile, scr):
    P = C.P
    ss2, rs2, junk2, tmp = scr
    ps = C.psb[pi]
    b = tile % 2
    P.op("scalar", lambda E: E.activation(out=junk2[b], in_=ps, func=AF.Square, accum_out=ss2[:, tile:tile + 1]),
         r=psbk(pi), w=[("po_ss", tile)])
    P.op("scalar", lambda E: E.activation(out=rs2[:, tile:tile + 1], in_=ss2[:, tile:tile + 1], func=AF.Sqrt,
                                           bias=C.epsc, scale=1.0 / D),
         r=[("po_ss", tile), "epsc"], w=[("po_rs", tile)])
    P.op("vector", lambda E: E.reciprocal(out=rs2[:, tile:tile + 1], in_=rs2[:, tile:tile + 1]),
         r=[("po_rs", tile)], w=[("po_rs", tile)])
    P.op("vector", lambda E: E.scalar_tensor_tensor(out=tmp[b], in0=ps, scalar=rs2[:, tile:tile + 1], in1=M.GG,
                                                     op0=ALU.mult, op1=ALU.mult),
         r=psbk(pi) + [("po_rs", tile), M.key + ("GG",)], w=[("po_tmp", 0)])
    P.op("vector", lambda E: E.tensor_tensor(out=C.x[:, tile, :], in0=C.x[:, tile, :], in1=tmp[b], op=ALU.add),
         r=[("po_tmp", 0), ("x", tile)], w=[("x", tile)])


def alloc_post_scratch(C):
    A = C.A
    ss2 = A.alloc([32], F32)
    rs2 = A.alloc([32], F32)
    junk2 = [C.junk, C.junk]
    t0 = A.alloc([D], F32)
    tmp = [t0, t0]
    return ss2, rs2, junk2, tmp


def ffn_sublayer(C, layer):
    P, A, io = C.P, C.A, C.io
    A.mark()
    M = modulation(C, 2 * layer + 1)
    HT = 1024
    hT = A.alloc([KC, HT], BF16)
    gT = A.alloc([FC, HT], BF16)
    wo = A.alloc([FC, D], BF16)
    wi = [A.alloc([KC, 512], BF16) for _ in range(2)]
    sa = [A.alloc([512], F32) for _ in range(2)]
    pn = alloc_pn_scratch(C)
    po = alloc_post_scratch(C)
    lf = C.sel["ffn"].index(layer)
    wov = io["ffn_w_out"].ap()[lf].rearrange("(j p) n -> p j n", p=128)

    def load_wo():
        for pc in range(2):
            P.dma("gpsimd", wo[:, 11 * pc:11 * pc + 11, :], wov[:, 11 * pc:11 * pc + 11, :], w=[("wo", pc)])
    wiv = io["ffn_w_in"].ap()[lf].rearrange("(k p) n -> p k n", p=128)

    def pre(half):
        tiles = [(C.x[:, half * 8 + i, :], ("x", half * 8 + i)) for i in range(8)]
        prenorm_tiles(C, M, tiles, hT, 0, pn)

    def mm1(half):
        for jj in range(FC // 2):
            wb = wi[jj % 2]
            wk = ("wi", jj % 2)
            P.dma("gpsimd", wb[:, :, 0:256], wiv[:, :, jj * 256:(jj + 1) * 256], w=[wk + ("a",)])
            P.dma("gpsimd", wb[:, :, 256:512], wiv[:, :, DFF + jj * 256:DFF + (jj + 1) * 256], w=[wk + ("b",)])
            if half == 0 and jj == 2:
                load_wo()
            for jl in range(2):
                j = 2 * jj + jl
                for tg in range(2):
                    if half == 0 and jj == 0 and jl == 0:
                        tl = [(C.x[:, tg * 4 + i, :], ("x", tg * 4 + i)) for i in range(4)]
                        prenorm_tiles(C, M, tl, hT, 512 * tg, pn)
                    pi = next_psb(C)
                    ps = C.psb[pi]
                    hk = [("hT", tg * 4 + q) for q in range(4)]
                    for part in range(2):
                        for kc in range(KC):
                            P.op("tensor", lambda E, part=part, kc=kc, wb=wb, jl=jl, tg=tg, ps=ps: E.matmul(
                                ps[:, part * 512:(part + 1) * 512],
                                lhsT=wb[:, kc, part * 256 + jl * 128:part * 256 + (jl + 1) * 128],
                                rhs=hT[:, kc, tg * 512:(tg + 1) * 512], start=(kc == 0), stop=(kc == KC - 1)),
                                r=[wk + ("a",), wk + ("b",)] + [(k_[0], k_[1], kc) for k_ in hk], w=psbk(pi),
                                inc=(part == 1 and kc == KC - 1))
                    sb = sa[(2 * j + tg) % 2]
                    sk = ("sa", (2 * j + tg) % 2)
                    P.op("scalar", lambda E, ps=ps, sb=sb: E.activation(out=sb, in_=ps[:, 0:512], func=AF.Silu),
                         r=psbk(pi), w=[sk])
                    P.op("vector", lambda E, ps=ps, sb=sb, j=j, tg=tg: E.tensor_tensor(
                        out=gT[:, j, tg * 512:(tg + 1) * 512], in0=sb, in1=ps[:, 512:1024], op=ALU.mult),
                        r=psbk(pi) + [sk], w=[("gT", j, tg)])

    def mm2(half):
        for ti in range(8):
            if half == 0:
                prenorm_p1(C, M, C.x[:, 8 + ti, :], ("x", 8 + ti), ti, pn)
            pi = next_psb(C)
            ps = C.psb[pi]
            for ch in range(2):
                for j in range(FC):
                    P.op("tensor", lambda E, ch=ch, j=j, ti=ti, ps=ps: E.matmul(
                        ps[:, ch * 512:(ch + 1) * 512], lhsT=gT[:, j, ti * 128:(ti + 1) * 128],
                        rhs=wo[:, j, ch * 512:(ch + 1) * 512], start=(j == 0), stop=(j == FC - 1)),
                        r=[("gT", j, ti // 4), ("wo", j // 11)], w=psbk(pi),
                        inc=(ch == 1 and j == FC - 1))
            if half == 0:
                prenorm_p2(C, M, hT, 128 * ti, ti, pn)
            post_tile(C, M, pi, half * 8 + ti, po)

    mm1(0)
    mm2(0)
    mm1(1)
    mm2(1)
    P.barrier()
    A.release()


def psbk(pi):
    return [("psq", 2 * pi), ("psq", 2 * pi + 1)]


class Halves:
    def __init__(self, C):
        self.C = C
        self.i = 0

    def next(self):
        i = self.i
        self.i = (i + 1) % 6
        ap = self.C.psb[i // 2][:, (i % 2) * 512:(i % 2) * 512 + 512]
        keys = [("psq", i)]
        return ap, keys

    def bank(self, i):
        return self.C.psb[i // 2][:, (i % 2) * 512:(i % 2) * 512 + 512], [("psq", i)]


def hybrid_consts(C, j):
    P, A, io = C.P, C.A, C.io
    H = Ctx()
    H.maskBD = A.alloc([128], F32)
    H.cmask = A.alloc([NCH, 128], BF16)
    H.rowmask = A.alloc([NCH], F32)
    H.smask = A.alloc([512], F32)
    H.smask128 = A.alloc([512], F32)
    H.lb = A.alloc([4], F32)
    H.oml = A.alloc([4], F32)
    H.poolw = A.alloc([4, 128], BF16)
    H.pscale = A.alloc([4], F32)
    H.gain = A.alloc([512], F32)
    H.invcnt = A.alloc([4, 16], F32)
    H.hmask = A.alloc([1], F32)
    H.epsh = A.alloc([1], F32)
    l0 = A.alloc([4], F32)
    l1 = A.alloc([4], F32)
    P.dma("sync", H.maskBD, io["maskBD"].ap(), w=["maskBD"])
    P.dma("gpsimd", H.cmask, io["cmask"].ap(), w=["cmask"])
    P.dma("sync", H.rowmask, io["rowmask"].ap(), w=["rowmask"])
    P.dma("sync", H.smask, io["smask"].ap(), w=["smask"])
    P.dma("sync", H.smask128, io["smask128"].ap(), w=["smask"])
    jl = C.sel["hyb"].index(j)
    P.dma("gpsimd", H.poolw, io["pool_w"].ap()[jl].rearrange("g c d -> c g d"), w=["poolw"])
    P.dma("sync", H.pscale, io["pool_scale_col"].ap()[jl], w=["pscale"])
    P.dma("sync", H.gain, dram_bcast(io["hgrn_gain4"], jl * 512, 512), w=["gain"])
    P.dma("sync", H.invcnt, io["invcnt"].ap(), w=["invcnt"])
    P.dma("sync", H.hmask, io["hmask"].ap(), w=["hmask"])
    P.op("gpsimd", lambda E: E.memset(H.epsh, EPS), w=["epsh"])
    if j == 0:
        P.op("gpsimd", lambda E: E.memset(H.lb, 0.0), w=["lb"])
        P.op("gpsimd", lambda E: E.memset(H.oml, 1.0), w=["oml"])
    else:
        P.dma("sync", l0, io["lb_logits_col"].ap()[0], w=["l0"])
        P.dma("sync", l1, io["lb_logits_col"].ap()[1], w=["l1"])
        P.op("vector", lambda E: E.tensor_tensor(out=l1, in0=l1, in1=l0, op=ALU.subtract), r=["l0", "l1"], w=["l1"])
        P.op("scalar", lambda E: E.activation(out=H.lb, in_=l1, func=AF.Sigmoid), r=["l1"], w=["lb"])
        P.op("vector", lambda E: E.tensor_scalar(out=H.oml, in0=H.lb, scalar1=-1.0, scalar2=1.0,
                                                  op0=ALU.mult, op1=ALU.add), r=["lb"], w=["oml"])
    return H


def hybrid_mixer(C, layer, full):
    P, A, io = C.P, C.A, C.io
    j = layer // 2
    A.mark()
    M = modulation(C, 2 * layer)
    H = hybrid_consts(C, j)
    hv = Halves(C)
    NB = 512
    chl = CH if full else 128
    nch = 128 // chl
    smask = H.smask if full else H.smask128
    hT = A.alloc([KC, 128 + NB], BF16)
    wbuf = [A.alloc([KC, 512], BF16) for _ in range(2)]
    wcnt = [0]
    S = A.alloc([4, 128], F32)
    btot = A.alloc([4], F32)
    bsum = A.alloc([4], F32)
    Sb = [A.alloc([4, 128], BF16) for _ in range(2)]
    tf = [[A.alloc([NB], F32) for _ in range(4)] for _ in range(3)]
    KH = A.alloc([4, NB], BF16)
    KD = A.alloc([4, NB], BF16)
    Dn = A.alloc([4, 4 * NCH], F32)
    V = A.alloc([4, 512], BF16)
    Vm = A.alloc([NCH, 512], BF16)
    KDt = [A.alloc([4, 128], BF16) for _ in range(2)]
    pn = alloc_pn_scratch(C)
    Skeys = [("S", hd) for hd in range(4)]
    if full:
        QH = A.alloc([4, NB], BF16)
        QHm = A.alloc([4, NCH, 128], BF16)
        G = A.alloc([4, 512], BF16)
        ATs = [A.alloc([4, 128], BF16) for _ in range(2)]
        ub = [A.alloc([16 + NB], F32) for _ in range(2)]
        tails = A.alloc([4, 16], F32)
        pt = [A.alloc([16 + NB], F32) for _ in range(2)]
        db = [A.alloc([NB], BF16) for _ in range(2)]
        yT = A.alloc([KC, NB], BF16)
        ysb = [A.alloc([512], BF16) for _ in range(2)]
        ssh = A.alloc([16, 4], F32)
        rsh = A.alloc([16, 4], F32)
        po = alloc_post_scratch(C)
        xh = po[3][0]
        P.dma("sync", xh, io["x_halo"].ap(), w=[("po_tmp", 0)])
        pSl = [tf[i // 4][i % 4].rearrange("p (h v) -> p h v", v=128) for i in range(7)]
        pD = A.alloc([7, 4], F32)
        P.dma("sync", pD, io["pred_D"].ap().rearrange("i p h -> p i h"), w=["pD"])
        P.op("gpsimd", lambda E: E.memset(S, 0.0), w=Skeys)
        for i in range(7):
            P.dma("sync", pSl[i], io["pred_S"].ap()[i], w=[("pS", i)])
        for i in range(7):
            for hd in range(4):
                P.op("vector", lambda E, i=i, hd=hd: E.scalar_tensor_tensor(
                    out=S[:, hd, :], in0=S[:, hd, :], scalar=pD[:, i, hd:hd + 1], in1=pSl[i][:, hd, :],
                    op0=ALU.mult, op1=ALU.add), r=[("S", hd), ("pS", i), "pD"], w=[("S", hd)])
        P.op("scalar", lambda E: E.activation(out=Sb[0], in_=S, func=AF.Copy), r=Skeys,
             w=[("Sb", 0, hd) for hd in range(4)])
        P.barrier()
        wov = io["hyb_w_out"].ap()[C.sel["hyb"].index(j)].rearrange("(k p) n -> p k n", p=128)
    else:
        P.op("gpsimd", lambda E: E.memset(S, 0.0), w=Skeys)
        P.op("gpsimd", lambda E: E.memset(btot, 0.0), w=["btot"])
    sbi = [0, 0, 0, 0]

    wv = io["hyb_w_in"].ap()[C.sel["hyb"].index(j)].rearrange("(k p) n -> p k n", p=128)

    if full:
        block_seq = [("in", 0), ("in", 3), ("in", 4), ("in", 2), ("in", 1), ("out", 0), ("out", 1)]
    else:
        block_seq = [("in", 3), ("in", 2)]
    wseq = block_seq * 4
    issued = [0]
    used = [0]

    def issue_next():
        k = issued[0]
        if k >= len(wseq):
            return
        kind, piece = wseq[k]
        src = wv if kind == "in" else wov
        bb = k % 2
        P.dma("gpsimd", wbuf[bb], src[:, :, piece * 512:(piece + 1) * 512], w=[("hw", bb)])
        issued[0] += 1

    def load_w(src, piece):
        k = used[0]
        used[0] += 1
        assert wseq[k][1] == piece, (wseq[k], piece)
        while issued[0] <= k:
            issue_next()
        nxt = k + 1
        if nxt < len(wseq) and issued[0] == nxt and not (wseq[k][0] == "out" and wseq[nxt][0] == "in"):
            issue_next()
        bb = k % 2
        return wbuf[bb], ("hw", bb)

    def proj_fm(wb, wk, cc, c0, n, hkeys):
        ap, keys = hv.next()
        for kc in range(KC):
            P.op("tensor", lambda E, kc=kc, ap=ap: E.matmul(ap[:, 0:n], lhsT=wb[:, kc, cc * 128:(cc + 1) * 128],
                                                             rhs=hT[:, kc, c0:c0 + n], start=(kc == 0), stop=(kc == KC - 1)),
                 r=[wk] + [(k_[0], k_[1], kc) for k_ in hkeys], w=keys, inc=(kc == KC - 1))
        return ap, keys

    def proj_tm(wb, wk, t):
        ap, keys = hv.next()
        for kc in range(KC):
            P.op("tensor", lambda E, kc=kc, ap=ap: E.matmul(
                ap, lhsT=hT[:, kc, 128 + t * 128:128 + (t + 1) * 128], rhs=wb[:, kc, :],
                start=(kc == 0), stop=(kc == KC - 1)), r=[wk, ("hT", 1 + t, kc)], w=keys, inc=(kc == KC - 1))
        return ap, keys

    for b in range(4):
        tiles = [(C.x[:, 4 * b + i, :], ("x", 4 * b + i)) for i in range(4)]
        hk = [("hT", 1 + i) for i in range(4)]
        if full and b == 0:
            prenorm_tiles(C, M, [(xh, ("po_tmp", 0))], hT, 0, pn)
        if b == 0:
            prenorm_tiles(C, M, tiles, hT, 128, pn)

        if full:
            wb, wk = load_w(wv, 0)
            if b == 0:
                for g in range(4):
                    ap, keys = proj_fm(wb, wk, g, 112, 16, [("hT", 0)])
                    P.op("vector", lambda E, g=g, ap=ap: E.tensor_scalar(out=tails[:, g, :], in0=ap[:, 0:16],
                                                                          scalar1=H.hmask[:, 0:1], scalar2=1.0,
                                                                          op0=ALU.mult, op1=ALU.mult),
                         r=keys + ["hmask"], w=[("tails", g)])
            def pool_head(g):
                ug = ub[g % 2]
                ugk = ("ug", g % 2)
                P.op("gpsimd", lambda E, g=g, ug=ug: E.tensor_copy(out=ug[:, 0:16], in_=tails[:, g, :]),
                     r=[("tails", g)], w=[ugk])
                ap, keys = proj_fm(wb, wk, g, 128, NB, hk)
                P.op("scalar", lambda E, ug=ug, ap=ap: E.activation(out=ug[:, 16:16 + NB], in_=ap, func=AF.Copy),
                     r=keys, w=[ugk])
                P.op("gpsimd", lambda E, g=g, ug=ug: E.tensor_copy(out=tails[:, g, :], in_=ug[:, NB:NB + 16]),
                     r=[ugk], w=[("tails", g)])

            def pool_tail(g):
                win = 2 << g
                ug = ub[g % 2]
                ugk = ("ug", g % 2)
                dg = db[g % 2]
                dgk = ("dg", g % 2)
                cur = ug
                curk = ugk
                sh = 1
                for lev in range(g + 1):
                    o_ = pt[lev % 2]
                    ok = ("pt", lev % 2)
                    eng = "gpsimd" if (lev + g) % 2 == 0 else "vector"
                    P.op(eng, lambda E, o_=o_, cur=cur, sh=sh: E.tensor_tensor(
                        out=o_[:, sh:16 + NB], in0=cur[:, sh:16 + NB], in1=cur[:, 0:16 + NB - sh], op=ALU.add),
                        r=[curk], w=[ok])
                    cur, curk = o_, ok
                    sh *= 2
                P.op("vector", lambda E, dg=dg, ug=ug, cur=cur, win=win: E.scalar_tensor_tensor(
                    out=dg, in0=cur[:, 16:16 + NB], scalar=1.0 / win, in1=ug[:, 16:16 + NB],
                    op0=ALU.mult, op1=ALU.subtract), r=[curk, ugk], w=[dgk])
                if b == 0:
                    P.op("vector", lambda E, g=g, cur=cur: E.tensor_tensor(
                        out=cur[:, 0:16], in0=cur[:, 16:32], in1=H.invcnt[:, g, :], op=ALU.mult),
                        r=[curk, "invcnt", dgk], w=[curk])
                    P.op("vector", lambda E, dg=dg, ug=ug, cur=cur: E.tensor_tensor(
                        out=dg[:, 0:16], in0=cur[:, 0:16], in1=ug[:, 16:32], op=ALU.subtract),
                        r=[curk, ugk], w=[dgk])
                ap, keys = hv.next()
                P.op("tensor", lambda E, g=g, ap=ap, dg=dg: E.matmul(ap, lhsT=H.poolw[:, g, :], rhs=dg,
                                                                     start=True, stop=True),
                     r=[dgk, "poolw"], w=keys)
                P.op("scalar", lambda E, g=g, ap=ap: E.activation(
                    out=yT[:, g, :], in_=ap, func=AF.Copy, scale=H.pscale[:, g:g + 1]),
                    r=keys + ["pscale"], w=[("yT", g)])

            pool_head(0)
            for g in range(4):
                if g + 1 < 4:
                    pool_head(g + 1)
                pool_tail(g)

        wb, wk = load_w(wv, 3)
        for t in range(4):
            ap, keys = proj_tm(wb, wk, t)
            P.op("scalar", lambda E, t=t, ap=ap: E.activation(out=V[:, t, :], in_=ap, func=AF.Copy),
                 r=keys, w=[("V", t)])
        if full:
            wb, wk = load_w(wv, 4)
            for t in range(4):
                ap, keys = proj_tm(wb, wk, t)
                P.op("scalar", lambda E, t=t, ap=ap: E.activation(out=tf[0][t], in_=ap, func=AF.Silu),
                     r=keys, w=[("tf", 0, t)])
                P.op("vector", lambda E, t=t: E.tensor_tensor(out=G[:, t, :], in0=tf[0][t], in1=H.gain, op=ALU.mult),
                     r=[("tf", 0, t), "gain"], w=[("G", t)])

        wb, wk = load_w(wv, 2)
        for hd in range(4):
            ap, keys = proj_fm(wb, wk, hd, 128, NB, hk)
            P.op("scalar", lambda E, hd=hd, ap=ap: E.activation(out=tf[0][hd], in_=ap, func=AF.Sigmoid),
                 r=keys, w=[("tf", 0, hd)])
        for hd in range(4):
            P.op("vector", lambda E, hd=hd: E.tensor_scalar(out=tf[0][hd], in0=tf[0][hd], scalar1=H.oml[:, hd:hd + 1],
                                                             scalar2=H.lb[:, hd:hd + 1], op0=ALU.mult, op1=ALU.add),
                 r=[("tf", 0, hd), "lb", "oml"], w=[("tf", 0, hd)])
        for hd in range(4):
            P.op("scalar", lambda E, hd=hd: E.activation(out=tf[1][hd], in_=tf[0][hd], func=AF.Ln),
                 r=[("tf", 0, hd)], w=[("tf", 1, hd)])
        for hd in range(4):
            P.op("gpsimd", lambda E, hd=hd: E.tensor_scalar(out=tf[0][hd], in0=tf[0][hd], scalar1=-1.0, scalar2=1.0,
                                                             op0=ALU.mult, op1=ALU.add),
                 r=[("tf", 0, hd), ("tf", 1, hd)], w=[("tf", 0, hd)])
            P.op("vector", lambda E, hd=hd: E.tensor_tensor_scan(out=tf[2][hd], data0=smask, data1=tf[1][hd],
                                                                  initial=0.0, op0=ALU.mult, op1=ALU.add),
                 r=[("tf", 1, hd), "smask"], w=[("tf", 2, hd)])
        for hd in range(4):
            bend = tf[2][hd].rearrange("p (n c) -> p n c", c=chl)[:, :, chl - 1]
            P.op("scalar", lambda E, hd=hd, bend=bend: E.activation(out=Dn[:, hd, 0:4 * nch], in_=bend, func=AF.Exp),
                 r=[("tf", 2, hd)], w=[("Dn", hd)])
            if not full:
                P.op("vector", lambda E, hd=hd, bend=bend: E.tensor_reduce(out=bsum[:, hd:hd + 1], in_=bend,
                                                                             axis=AX.X, op=ALU.add),
                     r=[("tf", 2, hd)], w=[("bsum", hd)])
                P.op("vector", lambda E, hd=hd: E.tensor_tensor(out=btot[:, hd:hd + 1], in0=btot[:, hd:hd + 1],
                                                                 in1=bsum[:, hd:hd + 1], op=ALU.add),
                     r=[("bsum", hd), "btot"], w=["btot"])
            if full:
                P.op("scalar", lambda E, hd=hd: E.activation(out=tf[1][hd], in_=tf[2][hd], func=AF.Exp, scale=-1.0),
                     r=[("tf", 2, hd)], w=[("tf", 1, hd)])
            else:
                P.op("vector", lambda E, hd=hd, bend=bend: E.tensor_tensor(
                    out=tf[1][hd].rearrange("p (n c) -> p n c", c=chl),
                    in0=bend.unsqueeze(2).to_broadcast([128, 4 * nch, chl]),
                    in1=tf[2][hd].rearrange("p (n c) -> p n c", c=chl), op=ALU.subtract),
                    r=[("tf", 2, hd)], w=[("tf", 1, hd)])
                P.op("scalar", lambda E, hd=hd: E.activation(out=tf[1][hd], in_=tf[1][hd], func=AF.Exp),
                     r=[("tf", 1, hd)], w=[("tf", 1, hd)])
            if full:
                P.op("scalar", lambda E, hd=hd: E.activation(out=tf[2][hd], in_=tf[2][hd], func=AF.Exp),
                     r=[("tf", 2, hd), ("Dn", hd), ("tf", 1, hd)], w=[("tf", 2, hd)])
        for hd in range(4):
            P.op("gpsimd", lambda E, hd=hd: E.tensor_tensor(out=tf[1][hd], in0=tf[1][hd], in1=tf[0][hd], op=ALU.mult),
                 r=[("tf", 0, hd), ("tf", 1, hd)], w=[("tf", 1, hd)])
            if full:
                P.op("gpsimd", lambda E, hd=hd: E.tensor_copy(out=KH[:, hd, :], in_=tf[1][hd]),
                     r=[("tf", 1, hd)], w=[("KH", hd)])
            if full:
                P.op("vector", lambda E, hd=hd: E.tensor_tensor(
                    out=KD[:, hd, :].rearrange("p (n c) -> p n c", c=CH),
                    in0=tf[1][hd].rearrange("p (n c) -> p n c", c=CH),
                    in1=Dn[:, hd, :].unsqueeze(2).to_broadcast([128, 4 * NCH, CH]), op=ALU.mult),
                    r=[("tf", 1, hd), ("Dn", hd)], w=[("KD", hd)])
            else:
                P.op("vector", lambda E, hd=hd: E.tensor_copy(out=KD[:, hd, :], in_=tf[1][hd]),
                     r=[("tf", 1, hd)], w=[("KD", hd)])
        if full:
            wb, wk = load_w(wv, 1)
            for hd in range(4):
                ap, keys = proj_fm(wb, wk, hd, 128, NB, hk)
                P.op("scalar", lambda E, hd=hd, ap=ap: E.activation(out=tf[0][hd], in_=ap, func=AF.Silu),
                     r=keys + [("tf", 1, hd)], w=[("tf", 0, hd)])
                P.op("vector", lambda E, hd=hd: E.tensor_tensor(out=QH[:, hd, :], in0=tf[0][hd], in1=tf[2][hd], op=ALU.mult),
                     r=[("tf", 0, hd), ("tf", 2, hd)], w=[("QH", hd)])
            wo0, wok0 = load_w(wov, 0)
            wo1, wok1 = load_w(wov, 1)

        for t in range(4):
            tile = 4 * b + t
            kb = t % 2
            if b < 3:
                prenorm_p1(C, M, C.x[:, 4 * (b + 1) + t, :], ("x", 4 * (b + 1) + t), t, pn)
            ti = next_pst(C)
            tp = C.pst[ti]
            for hd in range(4):
                P.op("tensor", lambda E, hd=hd, t=t, tp=tp: E.transpose(tp[:, hd * 128:(hd + 1) * 128],
                                                                          KD[:, hd, t * 128:(t + 1) * 128], C.ident),
                     r=[("KD", hd), "ident"], w=[("pst", ti)], inc=(hd == 3))
            P.op("scalar", lambda E, tp=tp, kb=kb: E.activation(out=KDt[kb], in_=tp[:, 0:512].rearrange("p (h k) -> p h k", k=128),
                                                                func=AF.Copy),
                 r=[("pst", ti)], w=[("KDt", kb)])
            for n in range(NCH if full else 0):
                if n % 2 == 0:
                    P.op("gpsimd", lambda E, t=t, n=n: E.tensor_scalar(out=Vm[:, n, :], in0=V[:, t, :],
                                                                       scalar1=H.rowmask[:, n:n + 1], scalar2=1.0,
                                                                       op0=ALU.mult, op1=ALU.mult),
                         r=[("V", t), "rowmask"], w=[("Vm", n)])
                else:
                    P.op("scalar", lambda E, t=t, n=n: E.activation(out=Vm[:, n, :], in_=V[:, t, :], func=AF.Copy,
                                                                    scale=H.rowmask[:, n:n + 1]),
                         r=[("V", t), "rowmask"], w=[("Vm", n)])
            if full:
                for hd in range(4):
                    P.op("vector", lambda E, hd=hd, t=t: E.tensor_tensor(
                        out=QHm[:, hd, :, :], in0=QH[:, hd, t * 128:(t + 1) * 128].unsqueeze(1).to_broadcast([128, NCH, 128]),
                        in1=H.cmask, op=ALU.mult), r=[("QH", hd), "cmask"], w=[("QHm", hd)])
                a_ap, a_keys = hv.bank(5 - (t % 2))
                for hd in range(4):
                    P.op("tensor", lambda E, hd=hd, t=t, a_ap=a_ap: E.matmul(
                        a_ap[:, hd * 128:(hd + 1) * 128], lhsT=KH[:, hd, t * 128:(t + 1) * 128],
                        rhs=QH[:, hd, t * 128:(t + 1) * 128], start=True, stop=True),
                        r=[("KH", hd), ("QH", hd)], w=a_keys, inc=(hd == 3))
                P.op("vector", lambda E, a_ap=a_ap, kb=kb: E.tensor_tensor(
                    out=ATs[kb], in0=a_ap.rearrange("p (h t) -> p h t", t=128),
                    in1=H.maskBD.unsqueeze(1).to_broadcast([128, 4, 128]), op=ALU.mult),
                    r=a_keys + ["maskBD"], w=[("ATs", kb)])
                o_ap, o_keys = hv.bank(4 + (t % 2))
                for hd in range(4):
                    P.op("tensor", lambda E, hd=hd, t=t, o_ap=o_ap, kb=kb: E.matmul(
                        o_ap[:, hd * 128:(hd + 1) * 128], lhsT=ATs[kb][:, hd, :], rhs=V[:, t, hd * 128:(hd + 1) * 128],
                        start=(hd == 0), stop=False), r=[("ATs", kb), ("V", t)], w=o_keys, inc=False)
            for n in range(nch):
                for hd in range(4):
                    u_ap, u_keys = hv.bank(hd)
                    vrhs = Vm[:, n, hd * 128:(hd + 1) * 128] if full else V[:, t, hd * 128:(hd + 1) * 128]
                    vkey = ("Vm", n) if full else ("V", t)
                    if full:
                        cb = sbi[hd]
                        P.op("tensor", lambda E, hd=hd, n=n, cb=cb, o_ap=o_ap: E.matmul(
                            o_ap[:, hd * 128:(hd + 1) * 128], lhsT=QHm[:, hd, n, :], rhs=Sb[cb][:, hd, :],
                            start=False, stop=(n == NCH - 1 and hd == 3)), r=[("QHm", hd), ("Sb", cb, hd)], w=o_keys,
                            inc=False)
                    P.op("tensor", lambda E, hd=hd, kb=kb, u_ap=u_ap, vrhs=vrhs: E.matmul(
                        u_ap[:, 0:128], lhsT=KDt[kb][:, hd, :], rhs=vrhs,
                        start=True, stop=True), r=[("KDt", kb), vkey], w=u_keys, inc=True)
                    P.op("vector", lambda E, hd=hd, t=t, n=n, u_ap=u_ap: E.scalar_tensor_tensor(
                        out=S[:, hd, :], in0=S[:, hd, :], scalar=Dn[:, hd, nch * t + n:nch * t + n + 1],
                        in1=u_ap[:, 0:128], op0=ALU.mult, op1=ALU.add),
                        r=u_keys + [("Dn", hd), ("S", hd)], w=[("S", hd)])
                    if full:
                        nb_ = 1 - sbi[hd]
                        P.op("scalar", lambda E, hd=hd, nb_=nb_: E.activation(out=Sb[nb_][:, hd, :], in_=S[:, hd, :], func=AF.Copy),
                             r=[("S", hd)], w=[("Sb", nb_, hd)])
                        sbi[hd] = nb_
            if full:
                yb = ysb[t % 2]
                for hd in range(4):
                    P.op("scalar", lambda E, hd=hd, o_ap=o_ap, tile=tile: E.activation(
                        out=C.junk[:, hd * 128:(hd + 1) * 128], in_=o_ap[:, hd * 128:(hd + 1) * 128], func=AF.Square,
                        accum_out=ssh[:, tile, hd:hd + 1]), r=o_keys, w=[("ssh", tile, hd)])
                P.op("scalar", lambda E, tile=tile: E.activation(out=rsh[:, tile, :], in_=ssh[:, tile, :], func=AF.Sqrt,
                                                                  bias=H.epsh, scale=1.0 / 128),
                     r=[("ssh", tile, hd) for hd in range(4)] + ["epsh"], w=[("rsh", tile)])
                P.op("vector", lambda E, tile=tile: E.reciprocal(out=rsh[:, tile, :], in_=rsh[:, tile, :]),
                     r=[("rsh", tile)], w=[("rsh", tile)])
                for hd in range(4):
                    P.op("vector", lambda E, hd=hd, o_ap=o_ap, tile=tile, t=t, yb=yb: E.scalar_tensor_tensor(
                        out=yb[:, hd * 128:(hd + 1) * 128], in0=o_ap[:, hd * 128:(hd + 1) * 128],
                        scalar=rsh[:, tile, hd:hd + 1], in1=G[:, t, hd * 128:(hd + 1) * 128], op0=ALU.mult, op1=ALU.mult),
                        r=o_keys + [("rsh", tile), ("G", t)], w=[("ysb", t % 2)])
                ti2 = next_pst(C)
                tp2 = C.pst[ti2]
                for hd in range(4):
                    P.op("tensor", lambda E, hd=hd, tp2=tp2, yb=yb: E.transpose(tp2[:, hd * 128:(hd + 1) * 128],
                                                                                  yb[:, hd * 128:(hd + 1) * 128], C.ident),
                         r=[("ysb", t % 2), "ident"], w=[("pst", ti2)], inc=(hd == 3))
                P.op("vector", lambda E, tp2=tp2, t=t: E.tensor_copy(
                    out=yT[:, 4:8, t * 128:(t + 1) * 128], in_=tp2[:, 0:512].rearrange("p (h t) -> p h t", t=128)),
                    r=[("pst", ti2)], w=[("yT", 4, t)])
            if b < 3:
                prenorm_p2(C, M, hT, 128 + 128 * t, t, pn)
        if full:
            for t in range(4):
                pi = next_psb(C)
                ps = C.psb[pi]
                for ch, (wo_, wok_) in enumerate(((wo0, wok0), (wo1, wok1))):
                    for c in range(KC):
                        P.op("tensor", lambda E, ch=ch, c=c, t=t, ps=ps, wo_=wo_: E.matmul(
                            ps[:, ch * 512:(ch + 1) * 512], lhsT=yT[:, c, t * 128:(t + 1) * 128], rhs=wo_[:, c, :],
                            start=(c == 0), stop=(c == KC - 1)),
                            r=[wok_, ("yT", c) if c < 4 else ("yT", 4, t)], w=psbk(pi), inc=(ch == 1 and c == KC - 1))
                post_tile(C, M, pi, 4 * b + t, po)
            if issued[0] == used[0]:
                issue_next()

    if not full:
        P.op("scalar", lambda E: E.activation(out=btot, in_=btot, func=AF.Exp), r=["btot"], w=["btot"])
        P.dma("sync", io["S_out"].ap(), S, r=Skeys)
        P.dma("sync", io["D_out"].ap(), btot, r=["btot"])
    P.barrier()
    A.release()


DIL = (1, 4, 16)
NEG = -30000.0
DBG = {}


def attn_qkv(C, layer):
    P, A, io = C.P, C.A, C.io
    ja = layer // 2
    A.mark()
    M = modulation(C, 2 * layer)
    hv = Halves(C)
    hT = A.alloc([KC, NT], BF16)
    pn = alloc_pn_scratch(C)
    wbuf = [A.alloc([KC, 512], BF16) for _ in range(2)]
    stg = [A.alloc([4, NT], BF16) for _ in range(2)]
    vst = [A.alloc([NTILE, 512], BF16) for _ in range(2)]
    wv = io["att_w_qkv"].ap()[C.sel["att"].index(ja)].rearrange("(k p) n -> p k n", p=128)
    ev = 0
    for piece in range(9):
        sidx, g = piece // 3, piece % 3
        wb = wbuf[piece % 2]
        wk = ("aw", piece % 2)
        P.dma("gpsimd", wb, wv[:, :, piece * 512:(piece + 1) * 512], w=[wk])
        if sidx < 2:
            st = stg[piece % 2]
            sk = ("stg", piece % 2)
            for tg in range(4):
                if piece == 0:
                    tl = [(C.x[:, tg * 4 + i, :], ("x", tg * 4 + i)) for i in range(4)]
                    prenorm_tiles(C, M, tl, hT, 512 * tg, pn)
                for cc in range(4):
                    ap, keys = hv.next()
                    hk = [("hT", tg * 4 + q) for q in range(4)]
                    for kc in range(KC):
                        P.op("tensor", lambda E, kc=kc, ap=ap, wb=wb, cc=cc, tg=tg: E.matmul(
                            ap, lhsT=wb[:, kc, cc * 128:(cc + 1) * 128], rhs=hT[:, kc, tg * 512:(tg + 1) * 512],
                            start=(kc == 0), stop=(kc == KC - 1)), r=[wk] + [(k_[0], k_[1], kc) for k_ in hk], w=keys, inc=(kc == KC - 1))
                    sc = 0.125 if sidx == 0 else 1.0
                    if ev % 2 == 0:
                        P.op("scalar", lambda E, ap=ap, st=st, cc=cc, tg=tg, sc=sc: E.activation(
                            out=st[:, cc, tg * 512:(tg + 1) * 512], in_=ap, func=AF.Copy, scale=sc), r=keys, w=[sk + (cc, tg)])
                    else:
                        P.op("vector", lambda E, ap=ap, st=st, cc=cc, tg=tg, sc=sc: E.tensor_scalar(
                            out=st[:, cc, tg * 512:(tg + 1) * 512], in0=ap, scalar1=sc, scalar2=1.0,
                            op0=ALU.mult, op1=ALU.mult), r=keys, w=[sk + (cc, tg)])
                    ev += 1
            dst = io["qT_out" if sidx == 0 else "kT_out"].ap()[g].rearrange("c p t -> p c t")
            P.dma("sync", dst, st, r=[sk + (cc_, tg_) for cc_ in range(4) for tg_ in range(4)])
        else:
            st = vst[piece % 2]
            sk = ("vst", piece % 2)
            for t in range(NTILE):
                ap, keys = hv.next()
                for kc in range(KC):
                    P.op("tensor", lambda E, kc=kc, ap=ap, wb=wb, t=t: E.matmul(
                        ap, lhsT=hT[:, kc, t * 128:(t + 1) * 128], rhs=wb[:, kc, :],
                        start=(kc == 0), stop=(kc == KC - 1)), r=[wk, ("hT", t, kc)], w=keys, inc=(kc == KC - 1))
                if ev % 2 == 0:
                    P.op("scalar", lambda E, ap=ap, st=st, t=t: E.activation(out=st[:, t, :], in_=ap, func=AF.Copy),
                         r=keys, w=[sk + (t,)])
                else:
                    P.op("vector", lambda E, ap=ap, st=st, t=t: E.tensor_copy(out=st[:, t, :], in_=ap), r=keys, w=[sk + (t,)])
                ev += 1
            dst = io["v_out"].ap()[g].rearrange("(n p) f -> p n f", p=128)
            P.dma("sync", dst, st, r=[sk + (t_,) for t_ in range(NTILE)])
    P.barrier()
    A.release()


def attn_bias_tables(C):
    P, A, io = C.P, C.A, C.io
    A.mark()
    hv = Halves(C)
    tab = A.alloc([24], F32)
    oh = A.alloc([3 * 510], F32)
    ngm = A.alloc([3 * 510], F32)
    wsb = A.alloc([3 * 510], F32)
    P.dma("sync", tab[0:32, :], io["rel_bias"].ap(), w=["tab"])
    P.dma("sync", oh[0:32, :], io["bias_onehot"].ap(), w=["oh"])
    P.dma("sync", ngm[0:24, :], io["bias_neg"].ap(), w=["ngm"])
    for g in range(3):
        ap, keys = hv.next()
        P.op("tensor", lambda E, ap=ap, g=g: E.matmul(ap[0:24, 0:510], lhsT=tab[0:32, :], rhs=oh[0:32, g * 510:(g + 1) * 510],
                                                       start=True, stop=True), r=["tab", "oh"], w=keys)
        P.op("vector", lambda E, ap=ap, g=g: E.tensor_tensor(out=wsb[0:24, g * 510:(g + 1) * 510], in0=ap[0:24, 0:510],
                                                              in1=ngm[0:24, g * 510:(g + 1) * 510], op=ALU.add),
             r=keys + ["ngm"], w=["wsb"])
    P.dma("sync", C.wd.ap(), wsb[0:24, :], r=["wsb"], w=["wd"])
    P.barrier()
    A.release()


def attn_core(C, layer):
    P, A, io = C.P, C.A, C.io
    ja = layer // 2
    A.mark()
    M = modulation(C, 2 * layer)
    hv = Halves(C)
    acc = A.alloc([2, NT], F32)
    yT = A.alloc([8, NT], BF16)
    sel = A.alloc([64], F32)
    hb = A.alloc([1], F32)
    zb = A.alloc([1], F32)
    hm01 = A.alloc([1], F32)
    rec = [A.alloc([512], F32) for _ in range(2)]
    A.mark()
    qb = [A.alloc([2, 16, 128], BF16) for _ in range(2)]
    kb_ = [A.alloc([2, 16, 128], BF16) for _ in range(2)]
    khb = [A.alloc([2, 16, 128], BF16) for _ in range(2)]
    vb = [A.alloc([16, 2, 65], BF16) for _ in range(2)]
    vhb = [A.alloc([16, 2, 65], BF16) for _ in range(2)]
    btb = [A.alloc([2, 256], F32) for _ in range(2)]
    Tb = [A.alloc([512], F32) for _ in range(2)]
    Pb = [A.alloc([512], BF16) for _ in range(4)]
    ebh = [A.alloc([2, 256], F32) for _ in range(2)]
    P.dma("sync", sel[0:65, :], io["sel65"].ap(), w=["sel"])
    P.dma("sync", hb, io["halo_bias"].ap(), w=["hb"])
    P.op("gpsimd", lambda E: E.memset(zb, 0.0), w=["zb"])
    P.op("vector", lambda E: E.tensor_scalar(out=hm01, in0=hb, scalar1=-1.0 / NEG, scalar2=1.0, op0=ALU.mult, op1=ALU.add),
         r=["hb"], w=["hm01"])
    wov = io["att_w_out"].ap()[C.sel["att"].index(ja)].rearrange("(h e) n -> e h n", e=64)
    NH = (1, 4, 16)
    it = 0
    for c in range(4):
        for g in range(3):
            b = it % 2
            it += 1
            dil = DIL[g]
            nh = NH[g]
            P.dma("sync", qb[b][0:64], io["qn"].ap()[g, 2 * c:2 * c + 2].rearrange("h p b t -> p h b t"), w=[("qb", b)])
            P.dma("sync", kb_[b][0:64], io["kn"].ap()[g, 2 * c:2 * c + 2].rearrange("h p b t -> p h b t"), w=[("kb", b)])
            P.dma("sync", khb[b][0:64, :, 0:nh, :], io["kh%d" % g].ap()[2 * c:2 * c + 2].rearrange("h p b t -> p h b t"),
                  w=[("khb", b)])
            if DBG.get("nov"):
                P.op("gpsimd", lambda E, b=b: E.memset(vb[b], 1.0), w=[("vb", b)])
                P.op("gpsimd", lambda E, b=b: E.memset(vhb[b], 1.0), w=[("vhb", b)])
            else:
                P.dma("sync", vb[b], io["vn"].ap()[g, :, :, 2 * c:2 * c + 2, :], w=[("vb", b)])
                P.dma("sync", vhb[b][:, 0:nh, :, :], io["vh%d" % g].ap()[:, :, 2 * c:2 * c + 2, :], w=[("vhb", b)])
            for hh in range(2):
                for part in range(2):
                    if DBG.get("nobias"):
                        P.op("gpsimd", lambda E, b=b, hh=hh, part=part: E.memset(btb[b][:, hh, part * 128:(part + 1) * 128], 0.0),
                             w=[("btb", b)])
                        continue
                    src = bass.AP(C.wd, (2 * c + hh + 8 * g) * 1530 + g * 510 + part * 255, [[1, 128], [1, 128]])
                    P.dma("sync", btb[b][:, hh, part * 128:(part + 1) * 128], src, r=["wd"], w=[("btb", b)])
            P.op("scalar", lambda E, b=b: E.activation(out=btb[b], in_=btb[b], func=AF.Exp, bias=zb), r=[("btb", b), "zb"], w=[("btb", b)])
            P.op("vector", lambda E, b=b: E.tensor_scalar(out=ebh[b][:, :, 0:128], in0=btb[b][:, :, 0:128], scalar1=hm01[:, 0:1],
                                                      scalar2=1.0, op0=ALU.mult, op1=ALU.mult), r=[("btb", b), "hm01"], w=[("ebh", b)])
            P.op("gpsimd", lambda E, b=b: E.tensor_copy(out=ebh[b][:, :, 128:256], in_=btb[b][:, :, 128:256]), r=[("btb", b)], w=[("ebh", b)])
            nres = dil
            nblk = 16 // dil
            units = [(r, n) for r in range(nres) for n in range(nblk)]
            if DBG.get("g0only") and g > 0:
                units = []
            if DBG.get("noattn"):
                if g == 0:
                    P.op("gpsimd", lambda E: E.memset(acc[0:65, :, :], 1.0), w=[("acc", q) for q in range(4)])
                units = []
            st = {}

            def stage_a(i):
                r, n = units[i]
                bid = r * nblk + n
                halo = (n == 0)
                s_ap, s_keys = hv.next()
                st[i] = (s_ap, s_keys)
                for hh in range(2):
                    kprev = khb[b][0:64, hh, r, :] if halo else kb_[b][0:64, hh, bid - 1, :]
                    P.op("tensor", lambda E, b=b, hh=hh, kprev=kprev, bid=bid, s_ap=s_ap: E.matmul(
                        s_ap[:, hh * 256:hh * 256 + 128], lhsT=kprev, rhs=qb[b][0:64, hh, bid, :], start=True, stop=True),
                        r=[("khb", b), ("kb", b), ("qb", b)], w=s_keys, inc=False)
                    P.op("tensor", lambda E, b=b, hh=hh, bid=bid, s_ap=s_ap: E.matmul(
                        s_ap[:, hh * 256 + 128:hh * 256 + 256], lhsT=kb_[b][0:64, hh, bid, :], rhs=qb[b][0:64, hh, bid, :],
                        start=True, stop=True), r=[("kb", b), ("qb", b)], w=s_keys, inc=(hh == 1))

            def stage_b(i):
                r, n = units[i]
                halo = (n == 0)
                s_ap, s_keys = st[i]
                tb = i % 2
                pb = i % 4
                P.op("scalar", lambda E, b=b, s_ap=s_ap, tb=tb: E.activation(out=Tb[tb], in_=s_ap, func=AF.Exp, bias=zb),
                     r=s_keys + ["zb"], w=[("Tb", tb)])
                ebt = ebh[b] if halo else btb[b]
                ebk = ("ebh", b) if halo else ("btb", b)
                eng = "vector" if i % 2 == 0 else "gpsimd"
                P.op(eng, lambda E, b=b, tb=tb, pb=pb, ebt=ebt: E.tensor_tensor(
                    out=Pb[pb], in0=Tb[tb], in1=ebt.rearrange("p h c -> p (h c)"), op=ALU.mult),
                    r=[("Tb", tb), ebk], w=[("Pb", pb)])

            def stage_c(i):
                r, n = units[i]
                bid = r * nblk + n
                halo = (n == 0)
                tb = i % 4
                o_ap, o_keys = hv.next()
                for hh in range(2):
                    vprev = vhb[b][:, r, hh, :] if halo else vb[b][:, bid - 1, hh, :]
                    P.op("tensor", lambda E, b=b, hh=hh, vprev=vprev, o_ap=o_ap, tb=tb: E.matmul(
                        o_ap[0:65, hh * 128:(hh + 1) * 128], lhsT=vprev, rhs=Pb[tb][:, hh * 256:hh * 256 + 128],
                        start=True, stop=False), r=[("vhb", b), ("vb", b), ("Pb", tb)], w=o_keys, inc=False)
                    P.op("tensor", lambda E, b=b, hh=hh, bid=bid, o_ap=o_ap, tb=tb: E.matmul(
                        o_ap[0:65, hh * 128:(hh + 1) * 128], lhsT=vb[b][:, bid, hh, :], rhs=Pb[tb][:, hh * 256 + 128:hh * 256 + 256],
                        start=False, stop=True), r=[("vb", b), ("Pb", tb)], w=o_keys, inc=(hh == 1))
                t0 = dil * 128 * n + r
                dst = acc[0:65, :, t0:t0 + dil * 127 + 1:dil]
                srcp = o_ap[0:65, 0:256].rearrange("p (h t) -> p h t", t=128)
                akeys = [("acc", q) for q in range(4)] if dil == 16 else [("acc", (dil * 128 * n) // 512)]
                if g == 0:
                    P.op("scalar", lambda E, b=b, dst=dst, srcp=srcp: E.activation(out=dst, in_=srcp, func=AF.Copy),
                         r=o_keys, w=akeys)
                else:
                    P.op("vector", lambda E, b=b, dst=dst, srcp=srcp: E.tensor_tensor(out=dst, in0=dst, in1=srcp, op=ALU.add),
                         r=o_keys + akeys, w=akeys)

            nu = len(units)
            for i in range(nu + 4):
                if i < nu:
                    stage_a(i)
                if 0 <= i - 2 < nu:
                    stage_b(i - 2)
                if 0 <= i - 4 < nu:
                    stage_c(i - 4)
        for hh in range(2):
            for tq in range(4):
                if DBG.get("nonorm"):
                    P.op("gpsimd", lambda E, hh=hh, tq=tq, c=c: E.tensor_copy(
                        out=yT[0:64, 2 * c + hh, tq * 512:(tq + 1) * 512], in_=acc[0:64, hh, tq * 512:(tq + 1) * 512]),
                        r=[("acc", tq)], w=[("yT", 2 * c + hh, tq)])
                    continue
                l_ap, l_keys = hv.next()
                P.op("tensor", lambda E, l_ap=l_ap, hh=hh, tq=tq: E.matmul(
                    l_ap[0:64, :], lhsT=sel[64:65, :], rhs=acc[64:65, hh, tq * 512:(tq + 1) * 512], start=True, stop=True),
                    r=["sel", ("acc", tq)], w=l_keys)
                rb = (hh * 4 + tq) % 2
                P.op("vector", lambda E, l_ap=l_ap, rb=rb: E.reciprocal(out=rec[rb][0:64, :], in_=l_ap[0:64, :]),
                     r=l_keys, w=[("rec", rb)])
                P.op("vector", lambda E, rb=rb, hh=hh, tq=tq, c=c: E.tensor_tensor(
                    out=yT[0:64, 2 * c + hh, tq * 512:(tq + 1) * 512], in0=acc[0:64, hh, tq * 512:(tq + 1) * 512],
                    in1=rec[rb][0:64, :], op=ALU.mult), r=[("rec", rb), ("acc", tq)], w=[("yT", 2 * c + hh, tq)])
    P.barrier()
    A.release()
    wo = A.alloc([8, D], BF16)
    po = alloc_post_scratch(C)
    P.dma("gpsimd", wo[0:64, :, :], wov, w=["wo"])
    for t in range(NTILE):
        pi = next_psb(C)
        ps = C.psb[pi]
        for ch in range(2):
            for h in range(8):
                P.op("tensor", lambda E, ch=ch, h=h, t=t, ps=ps: E.matmul(
                    ps[:, ch * 512:(ch + 1) * 512], lhsT=yT[0:64, h, t * 128:(t + 1) * 128],
                    rhs=wo[0:64, h, ch * 512:(ch + 1) * 512], start=(h == 0), stop=(h == 7)),
                    r=["wo", ("yT", h, t // 4)], w=psbk(pi), inc=(ch == 1 and h == 7))
        post_tile(C, M, pi, t, po)
    P.barrier()
    A.release()


def declare_io(nc, names_shapes_in, names_shapes_out):
    io = {}
    for name, shape, dt in names_shapes_in:
        io[name] = nc.dram_tensor(name, list(shape), dt, kind="ExternalInput")
    for name, shape, dt in names_shapes_out:
        io[name] = nc.dram_tensor(name, list(shape), dt, kind="ExternalOutput")
    return io


HYB_IN = [
    ("hyb_w_in", (1, D, 2560), F32),
    ("hyb_w_out", (1, D, D), F32),
    ("pool_w", (1, 4, 128, 128), F32),
    ("pool_scale_col", (1, 128, 4), F32),
    ("hgrn_gain4", (512,), F32),
    ("lb_logits_col", (2, 128, 4), F32),
    ("maskBD", (128, 128), F32),
    ("cmask", (128, NCH, 128), F32),
    ("rowmask", (128, NCH), F32),
    ("smask", (128, 512), F32),
    ("smask128", (128, 512), F32),
    ("invcnt", (128, 4, 16), F32),
    ("hmask", (128, 1), F32),
]


def hybrid_inputs(inp, core):
    p = np.arange(128)
    maskBD = ((p[:, None] // CH == p[None, :] // CH) & (p[:, None] <= p[None, :])).astype(np.float32)
    cmask = np.zeros((128, NCH, 128), np.float32)
    for n in range(NCH):
        cmask[:, n, CH * n:CH * n + CH] = 1.0
    rowmask = (p[:, None] // CH == np.arange(NCH)[None, :]).astype(np.float32)
    smask = np.ones((128, 512), np.float32)
    smask[:, ::CH] = 0.0
    smask128 = np.ones((128, 512), np.float32)
    smask128[:, ::128] = 0.0
    invcnt = np.zeros((128, 4, 16), np.float32)
    for g in range(4):
        win = 2 << g
        if core == 0:
            invcnt[:, g, :] = 1.0 / np.minimum(np.arange(16) + 1, win)
        else:
            invcnt[:, g, :] = 1.0 / win
    return {
        "maskBD": maskBD, "cmask": cmask, "rowmask": rowmask, "smask": smask, "smask128": smask128, "invcnt": invcnt,
        "hmask": np.full((128, 1), 0.0 if core == 0 else 1.0, np.float32),
    }


def x_halo_tile(xfull, core):
    t = np.zeros((128, D), np.float32)
    if core > 0:
        t[112:128] = xfull[core * NT - 16:core * NT]
    return t


def pred_states(S_list, D_list, core):
    pS = np.zeros((7, 128, 4, 128), np.float32)
    pD = np.ones((7, 128, 4), np.float32)
    for i in range(core):
        pS[i] = np.asarray(S_list[i]).reshape(128, 4, 128)
        pD[i] = np.asarray(D_list[i]).reshape(128, 4)
    return pS, pD


ATT_QKV_IN = [("att_w_qkv", (1, D, 4608), F32)]
ATT_QKV_OUT = [("qT_out", (3, 4, 128, NT), BF16), ("kT_out", (3, 4, 128, NT), BF16), ("v_out", (3, NT, 512), BF16)]
ATT_CORE_IN = [
    ("att_w_out", (1, 512, D), F32),
    ("rel_bias", (32, 24), F32),
    ("bias_onehot", (32, 1530), F32),
    ("bias_neg", (24, 1530), F32),
    ("sel65", (65, 64), F32),
    ("halo_bias", (128, 1), F32),
    ("qn", (3, 8, 64, 16, 128), BF16),
    ("kn", (3, 8, 64, 16, 128), BF16),
    ("vn", (3, 128, 16, 8, 65), BF16),
    ("kh0", (8, 64, 1, 128), BF16), ("kh1", (8, 64, 4, 128), BF16), ("kh2", (8, 64, 16, 128), BF16),
    ("vh0", (128, 1, 8, 65), BF16), ("vh1", (128, 4, 8, 65), BF16), ("vh2", (128, 16, 8, 65), BF16),
]


def t5_bucket_np(dist):
    dist = np.asarray(dist, np.int64)
    df = np.maximum(dist, 1).astype(np.float32)
    large = 16 + (np.log(df / np.float32(16)) / np.float32(np.log(2048 / 16)) * np.float32(16)).astype(np.int32)
    large = np.minimum(large, 31)
    return np.where(dist < 16, dist, large)


def attn_consts(inp, core):
    oh = np.zeros((32, 1530), np.float32)
    ng = np.zeros((24, 1530), np.float32)
    m = np.arange(255)
    for g in range(3):
        dil = DIL[g]
        dp = 1 + m
        bp = t5_bucket_np(dp * dil)
        do = m - 127
        bo = t5_bucket_np(np.maximum(do, 0) * dil)
        for mm in range(255):
            if mm <= 127:
                oh[bp[mm], g * 510 + mm] = 1.0
            else:
                ng[:, g * 510 + mm] = NEG
            if mm >= 127:
                oh[bo[mm], g * 510 + 255 + mm] = 1.0
            else:
                ng[:, g * 510 + 255 + mm] = NEG
    sel = np.zeros((65, 64), np.float32)
    sel[64, :] = 1.0
    return {
        "att_w_out": np.ascontiguousarray(inp["att_w_out"], np.float32),
        "rel_bias": np.ascontiguousarray(inp["rel_bias"], np.float32),
        "bias_onehot": oh, "bias_neg": ng, "sel65": sel,
        "halo_bias": np.full((128, 1), NEG if core == 0 else 0.0, np.float32),
    }


def block_tokens(g):
    dil = DIL[g]
    nblk = 16 // dil
    idx = np.zeros((16, 128), np.int64)
    for r in range(dil):
        for n in range(nblk):
            idx[r * nblk + n] = dil * (128 * n + np.arange(128)) + r
    return idx


def attn_layout(qT, kT, v, kT_prev, v_prev):
    out = {}
    bf = qT.dtype
    qn = np.zeros((3, 8, 64, 16, 128), bf)
    kn = np.zeros((3, 8, 64, 16, 128), bf)
    qT = qT.reshape(3, 8, 64, NT)
    kT = kT.reshape(3, 8, 64, NT)
    if kT_prev is not None:
        kT_prev = kT_prev.reshape(3, 8, 64, NT)
    vn = np.zeros((3, 128, 16, 8, 65), bf)
    NHs = (1, 4, 16)
    for g in range(3):
        idx = block_tokens(g)
        ridx = idx[:, ::-1]
        qn[g] = qT[g][:, :, idx]
        kn[g] = kT[g][:, :, ridx]
        vg = v[g][ridx]
        vn[g, :, :, :, 0:64] = vg.reshape(16, 128, 8, 64).transpose(1, 0, 2, 3)
        vn[g, :, :, :, 64] = 1.0
        nh = NHs[g]
        nblk = 16 // DIL[g]
        kh = np.zeros((8, 64, nh, 128), bf)
        vh = np.zeros((128, nh, 8, 65), bf)
        if kT_prev is not None:
            last = np.array([r * nblk + (nblk - 1) for r in range(DIL[g])])
            hidx = ridx[last]
            kh[:] = kT_prev[g][:, :, hidx]
            vhh = v_prev[g][hidx]
            vh[:, :, :, 0:64] = vhh.reshape(nh, 128, 8, 64).transpose(1, 0, 2, 3)
            vh[:, :, :, 64] = 1.0
        out["kh%d" % g] = kh
        out["vh%d" % g] = vh
    out["qn"], out["kn"], out["vn"] = qn, kn, vn
    return out


def common_spec(nls):
    return [
        ("ident", (128, 128), F32),
        ("c_col", (128, KC), F32),
        ("ada_w", (nls, D, 3 * D), F32),
        ("ada_b", (nls * 3 * D,), F32),
        ("ada_b_col", (nls, 128, 24), F32),
        ("norm_pre_col", (nls, 128, KC), F32),
        ("norm_post", (nls * D,), F32),
    ]


def common_inputs(inp, lss):
    c = np.asarray(inp["c"], np.float32).reshape(D)
    aw = np.asarray(inp["ada_w"], np.float32).reshape(8, D, 3 * D)
    ab = np.asarray(inp["ada_b"], np.float32).reshape(8, 3 * D)
    npre = np.asarray(inp["norm_pre"], np.float32).reshape(8, D)
    npost = np.asarray(inp["norm_post"], np.float32).reshape(8, D)
    return {
        "ident": np.eye(128, dtype=np.float32),
        "c_col": np.ascontiguousarray(c.reshape(KC, 128).T),
        "ada_w": np.ascontiguousarray(aw[lss]),
        "ada_b": np.ascontiguousarray(ab[lss].reshape(-1)),
        "ada_b_col": np.ascontiguousarray(ab[lss].reshape(len(lss), 24, 128).transpose(0, 2, 1)),
        "norm_pre_col": np.ascontiguousarray(npre[lss].reshape(len(lss), KC, 128).transpose(0, 2, 1)),
        "norm_post": np.ascontiguousarray(npost[lss].reshape(-1)),
    }


def hyb_weights(inp, j):
    return {
        "hyb_w_in": np.ascontiguousarray(np.asarray(inp["hyb_w_in"], np.float32)[j:j + 1]),
        "hyb_w_out": np.ascontiguousarray(np.asarray(inp["hyb_w_out"], np.float32)[j:j + 1]),
        "pool_w": np.ascontiguousarray(np.asarray(inp["pool_w"], np.float32)[j:j + 1]),
        "pool_scale_col": np.ascontiguousarray(np.asarray(inp["pool_scale"], np.float32)[j].reshape(1, 4, 128).transpose(0, 2, 1)),
        "hgrn_gain4": np.ascontiguousarray(np.tile(np.asarray(inp["hgrn_out_norm"], np.float32)[j], 4)),
        "lb_logits_col": np.ascontiguousarray(np.asarray(inp["hgrn_lb_logits"], np.float32).reshape(2, 4, 128).transpose(0, 2, 1)),
    }


def build_launch(stages):
    lss, ffn, hyb, att = [], [], [], []
    ins, outs = [("x", (NT, D), F32)], []
    kinds = [k for k, _ in stages]
    for kind, layer in stages:
        if kind in ("hyb_state", "hyb_full"):
            if 2 * layer not in lss:
                lss.append(2 * layer)
            if layer // 2 not in hyb:
                hyb.append(layer // 2)
        elif kind == "ffn":
            lss.append(2 * layer + 1)
            ffn.append(layer)
        elif kind in ("qkv", "att"):
            if 2 * layer not in lss:
                lss.append(2 * layer)
            att.append(layer // 2)
    ins += common_spec(len(lss))
    if hyb:
        ins += HYB_IN
    if "hyb_full" in kinds:
        ins += [("x_halo", (128, D), F32), ("pred_S", (7, 128, 4, 128), F32), ("pred_D", (7, 128, 4), F32)]
    if "hyb_state" in kinds:
        outs += [("S_out", (128, 4, 128), F32), ("D_out", (128, 4), F32)]
    if ffn:
        ins += [("ffn_w_in", (1, D, 2 * DFF), F32), ("ffn_w_out", (1, DFF, D), F32)]
    if "qkv" in kinds:
        ins += ATT_QKV_IN
        outs += ATT_QKV_OUT
    if "att" in kinds:
        ins += ATT_CORE_IN
    if kinds != ["hyb_state"]:
        outs += [("y", (NT, D), F32)]
    nc = bass.Bass("TRN2", target_bir_lowering=False)
    io = declare_io(nc, ins, outs)
    with ExitStack() as es:
        P = Prog(nc, es)
        A = Arena(nc, es, 212000)
        C = setup_common(nc, es, P, A, io)
        C.sel = {"ls": lss, "ffn": ffn, "hyb": hyb, "att": att}
        if "att" in kinds:
            C.wd = nc.dram_tensor("wd", [24, 1530], F32)
        load_x(C)
        stored = False
        for kind, layer in stages:
            if kind == "hyb_state":
                if len(stages) > 1 and not stored:
                    store_x(C)
                    stored = True
                hybrid_mixer(C, layer, False)
            elif kind == "hyb_full":
                hybrid_mixer(C, layer, True)
            elif kind == "ffn":
                ffn_sublayer(C, layer)
            elif kind == "qkv":
                store_x(C)
                stored = True
                attn_qkv(C, layer)
            elif kind == "att":
                if not DBG.get("nobias"):
                    attn_bias_tables(C)
                attn_core(C, layer)
        if not stored and kinds != ["hyb_state"]:
            store_x(C)
        P.barrier()
        P.replay()
    return nc


_PROGS = {}


def get_prog(stages):
    key = tuple(stages)
    if key not in _PROGS:
        _PROGS[key] = build_launch(list(stages))
    return _PROGS[key]


def run_launch(stages, in_maps):
    nc = get_prog(stages)
    res = run_bass_kernel_spmd(nc, in_maps, core_ids=list(range(NCORES)))
    return res.results


def kernel(**inp):
    inp = {k: np.asarray(v) for k, v in inp.items()}
    x = np.ascontiguousarray(inp["x"].reshape(SEQ, D).astype(np.float32))

    def xs(xa, c):
        return np.ascontiguousarray(xa[c * NT:(c + 1) * NT])

    def ffn_w(layer):
        return {"ffn_w_in": np.ascontiguousarray(np.asarray(inp["ffn_w_in"], np.float32)[layer:layer + 1]),
                "ffn_w_out": np.ascontiguousarray(np.asarray(inp["ffn_w_out"], np.float32)[layer:layer + 1])}

    def att_w(ja):
        return {"att_w_qkv": np.ascontiguousarray(np.asarray(inp["att_w_qkv"], np.float32)[ja:ja + 1])}

    hconst = [hybrid_inputs(inp, c) for c in range(NCORES)]
    for hc in hconst:
        for k in ("hyb_w_in", "hyb_w_out", "pool_w", "pool_scale_col", "hgrn_gain4", "lb_logits_col"):
            hc.pop(k, None)
    aconst = [attn_consts(inp, c) for c in range(NCORES)]

    stages = (("hyb_state", 0),)
    com = common_inputs(inp, [0])
    hw = hyb_weights(inp, 0)
    maps = [dict(com, **hw, **hconst[c], x=xs(x, c)) for c in range(NCORES)]
    res = run_launch(stages, maps)
    S_list = [r["S_out"] for r in res]
    D_list = [r["D_out"] for r in res]
    xcur = x
    for lh in (0, 2):
        la = lh + 1
        stages = (("hyb_full", lh), ("ffn", lh), ("qkv", la))
        com = common_inputs(inp, [2 * lh, 2 * lh + 1, 2 * la])
        hw = hyb_weights(inp, lh // 2)
        fw = ffn_w(lh)
        aw = att_w(la // 2)
        maps = []
        for c in range(NCORES):
            pS, pD = pred_states(S_list, D_list, c)
            maps.append(dict(com, **hw, **fw, **aw, **hconst[c], x=xs(xcur, c), x_halo=x_halo_tile(xcur, c),
                             pred_S=pS, pred_D=pD))
        res = run_launch(stages, maps)
        xcur = np.concatenate([r["y"].reshape(NT, D) for r in res], 0)
        proj = [(r["qT_out"].reshape(3, 4, 128, NT), r["kT_out"].reshape(3, 4, 128, NT), r["v_out"].reshape(3, NT, 512))
                for r in res]
        last = (la == 3)
        stages = (("att", la), ("ffn", la)) if last else (("att", la), ("ffn", la), ("hyb_state", la + 1))
        lss = [2 * la, 2 * la + 1] + ([] if last else [2 * (la + 1)])
        com = common_inputs(inp, lss)
        fw = ffn_w(la)
        awo = {"att_w_out": np.ascontiguousarray(np.asarray(inp["att_w_out"], np.float32)[la // 2:la // 2 + 1])}
        hw = {} if last else hyb_weights(inp, (la + 1) // 2)
        maps = []
        for c in range(NCORES):
            q, k, v = proj[c]
            lay = attn_layout(q, k, v, proj[c - 1][1] if c > 0 else None, proj[c - 1][2] if c > 0 else None)
            m = dict(com, **fw, **aconst[c], **lay, x=xs(xcur, c))
            m.update(awo)
            if not last:
                m.update(hw)
                m.update(hconst[c])
            maps.append(m)
        res = run_launch(stages, maps)
        xcur = np.concatenate([r["y"].reshape(NT, D) for r in res], 0)
        if not last:
            S_list = [r["S_out"] for r in res]
            D_list = [r["D_out"] for r in res]
    return xcur.reshape(1, SEQ, D).astype(np.float32)
```

```python
import numpy as np
from contextlib import ExitStack
import concourse.bass as bass
import concourse.mybir as mybir
from concourse.bass_utils import run_bass_kernel_spmd

F32 = mybir.dt.float32
BF16 = mybir.dt.bfloat16
I32 = mybir.dt.int32
ALU = mybir.AluOpType
AF = mybir.ActivationFunctionType
AX = mybir.AxisListType

NCORES = 8
SEQ = 16384
NT = SEQ // NCORES
NTILE = NT // 128
D = 1024
KC = D // 128
DFF = 2816
FC = DFF // 128
EPS = 1e-6
CH = 16
NCH = 128 // CH


class Prog:
    ENGS = ("tensor", "vector", "scalar", "gpsimd", "sync")
    QUEUES = ("sync", "gpsimd", "scalar")
    NDMA = 16

    def __init__(self, nc, es):
        self.nc = nc
        self.streams = {e: [] for e in self.ENGS}
        self.sem = {e: es.enter_context(nc.semaphore("s_" + e)) for e in self.ENGS}
        self.cnt = {e: 0 for e in self.ENGS}
        self.seen = {e: {} for e in self.ENGS}
        self.dsem = {q: [es.enter_context(nc.semaphore("d_%s%d" % (q, i))) for i in range(self.NDMA)]
                     for q in self.QUEUES}
        self.dcnt = {q: [0] * self.NDMA for q in self.QUEUES}
        self.dnext = {q: 0 for q in self.QUEUES}
        self.lastw = {}
        self.readers = {}
        self.nops = 0

    def _semobj(self, semkey):
        if semkey[0] == "e":
            return self.sem[semkey[1]]
        return self.dsem[semkey[1]][semkey[2]]

    def _deps(self, r, w):
        deps = []
        for k in r:
            t = self.lastw.get(k)
            if t is not None:
                deps.append(t)
        for k in w:
            t = self.lastw.get(k)
            if t is not None:
                deps.append(t)
            rd = self.readers.get(k)
            if rd:
                deps.extend(rd.items())
        return deps

    def _emit_waits(self, eng, deps):
        need = {}
        for semkey, val in deps:
            if eng == "tensor" and semkey == ("e", "tensor"):
                continue
            if need.get(semkey, 0) < val:
                need[semkey] = val
        for semkey, val in need.items():
            if self.seen[eng].get(semkey, 0) >= val:
                continue
            self.seen[eng][semkey] = val
            s = self._semobj(semkey)
            self.streams[eng].append(lambda E, s=s, val=val: E.wait_ge(s, val))

    def _record(self, token, r, w):
        for k in w:
            self.lastw[k] = token
            self.readers[k] = {}
        for k in r:
            rd = self.readers.setdefault(k, {})
            if rd.get(token[0], 0) < token[1]:
                rd[token[0]] = token[1]

    def op(self, eng, fn, r=(), w=(), inc=True):
        self.nops += 1
        self._emit_waits(eng, self._deps(r, w))
        if inc:
            self.cnt[eng] += 1
            s = self.sem[eng]
            self.streams[eng].append(lambda E, fn=fn, s=s: fn(E).then_inc(s, 1))
            token = (("e", eng), self.cnt[eng])
        else:
            self.streams[eng].append(lambda E, fn=fn: fn(E))
            token = (("e", eng), self.cnt[eng] + 1)
        self._record(token, r, w)

    def dma(self, q, out, in_, r=(), w=()):
        self.nops += 1
        self._emit_waits(q, self._deps(r, w))
        i = self.dnext[q]
        self.dnext[q] = (i + 1) % self.NDMA
        if self.dcnt[q][i] > 0:
            self._emit_waits(q, [(("d", q, i), self.dcnt[q][i])])
        self.dcnt[q][i] += 16
        s = self.dsem[q][i]
        self.streams[q].append(lambda E, out=out, in_=in_, s=s: E.dma_start(out=out, in_=in_).then_inc(s, 16))
        token = (("d", q, i), self.dcnt[q][i])
        self._record(token, r, w)

    def barrier(self):
        deps = [(("e", e), self.cnt[e]) for e in self.ENGS if self.cnt[e] > 0]
        for q in self.QUEUES:
            for i in range(self.NDMA):
                if self.dcnt[q][i] > 0:
                    deps.append((("d", q, i), self.dcnt[q][i]))
        for e in self.ENGS:
            self._emit_waits(e, [d for d in deps if not (e == "tensor" and d[0] == ("e", "tensor"))])
        self.lastw = {}
        self.readers = {}

    def replay(self):
        nc = self.nc
        with nc.Block() as block:
            @block.tensor
            def _(E):
                for f in self.streams["tensor"]:
                    f(E)

            @block.vector
            def _(E):
                for f in self.streams["vector"]:
                    f(E)

            @block.scalar
            def _(E):
                for f in self.streams["scalar"]:
                    f(E)

            @block.gpsimd
            def _(E):
                for f in self.streams["gpsimd"]:
                    f(E)

            @block.sync
            def _(E):
                for f in self.streams["sync"]:
                    f(E)


class Arena:
    def __init__(self, nc, es, nbytes):
        self.words = nbytes // 4
        self.t = es.enter_context(nc.sbuf_tensor("arena", [128, self.words], F32))
        self.off = 0
        self.marks = []

    def alloc(self, shape, dtype):
        n = 1
        for s in shape:
            n *= s
        esz = 2 if dtype == BF16 else 4
        nwords = (n * esz + 3) // 4
        nwords = (nwords + 7) // 8 * 8
        assert self.off + nwords <= self.words, ("SBUF arena overflow", self.off, nwords, self.words)
        v = self.t[:, self.off:self.off + nwords]
        self.off += nwords
        if dtype != F32:
            v = v.bitcast(dtype)
        v = v[:, 0:n]
        if len(shape) == 2:
            v = v.rearrange("p (a b) -> p a b", b=shape[1])
        elif len(shape) == 3:
            v = v.rearrange("p (a b c) -> p a b c", b=shape[1], c=shape[2])
        return v

    def mark(self):
        self.marks.append(self.off)

    def release(self):
        self.off = self.marks.pop()


class Ctx:
    pass


def dram_bcast(handle, offset, n):
    return bass.AP(handle, offset, [[0, 128], [1, n]])


def setup_common(nc, es, P, A, io):
    C = Ctx()
    C.nc, C.P, C.A, C.io = nc, P, A, io
    C.x = A.alloc([NTILE, D], F32)
    C.ident = A.alloc([128], BF16)
    C.junk = A.alloc([D], BF16)
    C.cT = A.alloc([KC], BF16)
    C.cB = A.alloc([KC, 128], BF16)
    C.psb = [es.enter_context(nc.psum_tensor("psb%d" % i, [128, 1024], F32))[:, :] for i in range(3)]
    C.pst = [es.enter_context(nc.psum_tensor("pst%d" % i, [128, 1024], BF16))[:, :] for i in range(2)]
    C.psb_i = 0
    C.pst_i = 0
    C.epsc = A.alloc([1], F32)
    P.op("gpsimd", lambda E: E.memset(C.epsc, EPS), w=["epsc"])
    P.dma("gpsimd", C.ident, io["ident"].ap(), w=["ident"])
    ctmp = A.alloc([KC], F32)
    P.dma("sync", ctmp, io["c_col"].ap(), w=["ctmp"])
    csil = A.alloc([KC], F32)
    P.op("scalar", lambda E: E.activation(out=csil, in_=ctmp, func=AF.Silu), r=["ctmp"], w=["csil"])
    P.op("vector", lambda E: E.tensor_copy(out=C.cT, in_=csil), r=["csil"], w=["cT"])
    P.op("vector", lambda E: E.tensor_copy(out=C.cB, in_=csil.unsqueeze(2).to_broadcast([128, KC, 128])),
         r=["csil"], w=["cB"])
    return C


def next_psb(C):
    i = C.psb_i
    C.psb_i = (i + 1) % 3
    return i


def next_pst(C):
    i = C.pst_i
    C.pst_i = (i + 1) % 2
    return i


def load_x(C, name="x"):
    P = C.P
    xv = C.io[name].ap().rearrange("(n p) d -> p n d", p=128)
    for g in range(4):
        P.dma("sync", C.x[:, 4 * g:4 * g + 4, :], xv[:, 4 * g:4 * g + 4, :], w=[("x", i) for i in range(4 * g, 4 * g + 4)])


def store_x(C, name="y"):
    P = C.P
    yv = C.io[name].ap().rearrange("(n p) d -> p n d", p=128)
    for g in range(4):
        P.dma("sync", yv[:, 4 * g:4 * g + 4, :], C.x[:, 4 * g:4 * g + 4, :], r=[("x", i) for i in range(4 * g, 4 * g + 4)])


def modulation(C, ls):
    P, A, io = C.P, C.A, C.io
    M = Ctx()
    M.shiftT = A.alloc([KC], F32)
    M.AT = A.alloc([KC], F32)
    M.GG = A.alloc([D], F32)
    A.mark()
    wbuf = [A.alloc([KC, 512], BF16) for _ in range(6)]
    bcol = A.alloc([24], F32)
    gpre = A.alloc([KC], F32)
    bgate = A.alloc([D], F32)
    gpost = A.alloc([D], F32)
    modT = A.alloc([16], F32)
    key = ("mod", ls)
    li = C.sel["ls"].index(ls)
    wv = io["ada_w"].ap()[li].rearrange("(k p) n -> p k n", p=128)
    P.dma("sync", bcol, io["ada_b_col"].ap()[li], w=[key + ("bcol",)])
    P.dma("sync", gpre, io["norm_pre_col"].ap()[li], w=[key + ("gpre",)])
    P.dma("sync", bgate, dram_bcast(io["ada_b"], li * 3 * D + 2 * D, D), w=[key + ("bgate",)])
    P.dma("sync", gpost, dram_bcast(io["norm_post"], li * D, D), w=[key + ("gpost",)])
    pi = next_psb(C)
    ps = C.psb[pi]
    for piece in range(6):
        P.dma("gpsimd", wbuf[piece], wv[:, :, piece * 512:(piece + 1) * 512], w=[key + ("w", piece)])
    for piece in range(6):
        wb = wbuf[piece]
        wk = key + ("w", piece)
        if piece < 4:
            for jc in range(4):
                j = piece * 4 + jc
                for kc in range(KC):
                    P.op("tensor", lambda E, j=j, jc=jc, kc=kc, wb=wb: E.matmul(
                        ps[:, j:j + 1], lhsT=wb[:, kc, jc * 128:(jc + 1) * 128], rhs=C.cT[:, kc:kc + 1],
                        start=(kc == 0), stop=(kc == KC - 1)),
                        r=[wk, "cT"], w=psbk(pi), inc=(kc == KC - 1))
            if piece == 3:
                P.op("vector", lambda E: E.tensor_tensor(out=modT, in0=ps[:, 0:16], in1=bcol[:, 0:16], op=ALU.add),
                     r=psbk(pi) + [key + ("bcol",)], w=[key + ("modT",)])
                P.op("vector", lambda E: E.tensor_copy(out=M.shiftT, in_=modT[:, 0:8]),
                     r=[key + ("modT",)], w=[key + ("shiftT",)])
                P.op("vector", lambda E: E.scalar_tensor_tensor(out=M.AT, in0=modT[:, 8:16], scalar=1.0, in1=gpre,
                                                                  op0=ALU.add, op1=ALU.mult),
                     r=[key + ("modT",), key + ("gpre",)], w=[key + ("AT",)])
                pi2 = next_psb(C)
                ps2 = C.psb[pi2]
        else:
            ch = piece - 4
            for kc in range(KC):
                P.op("tensor", lambda E, ch=ch, kc=kc, wb=wb: E.matmul(
                    ps2[:, ch * 512:(ch + 1) * 512], lhsT=C.cB[:, kc, :], rhs=wb[:, kc, :],
                    start=(kc == 0), stop=(kc == KC - 1)),
                    r=[wk, "cB"], w=psbk(pi2), inc=(kc == KC - 1))
    P.op("vector", lambda E: E.tensor_tensor(out=M.GG, in0=ps2, in1=bgate, op=ALU.add),
         r=psbk(pi2) + [key + ("bgate",)], w=[key + ("GG",)])
    P.op("gpsimd", lambda E: E.tensor_tensor(out=M.GG, in0=M.GG, in1=gpost, op=ALU.mult),
         r=[key + ("GG",), key + ("gpost",)], w=[key + ("GG",)])
    M.key = key
    P.barrier()
    A.release()
    return M


def prenorm_p1(C, M, xa, xk, i, scratch):
    P = C.P
    ss, rstd, junk, xn = scratch
    P.op("scalar", lambda E: E.activation(out=junk[i % 2], in_=xa, func=AF.Square, accum_out=ss[:, i:i + 1]),
         r=[xk], w=[("pn_ss", i)])
    P.op("scalar", lambda E: E.activation(out=rstd[:, i:i + 1], in_=ss[:, i:i + 1], func=AF.Sqrt,
                                           bias=C.epsc, scale=1.0 / D),
         r=[("pn_ss", i), "epsc"], w=[("pn_rstd", i)])
    P.op("vector", lambda E: E.reciprocal(out=rstd[:, i:i + 1], in_=rstd[:, i:i + 1]),
         r=[("pn_rstd", i)], w=[("pn_rstd", i)])
    eng = "gpsimd"
    P.op(eng, lambda E: E.tensor_scalar(out=xn[i % 2], in0=xa, scalar1=rstd[:, i:i + 1],
                                         scalar2=1.0, op0=ALU.mult, op1=ALU.mult),
         r=[xk, ("pn_rstd", i)], w=[("pn_xn", i % 2)])


def prenorm_p2(C, M, hT, c0, i, scratch):
    P = C.P
    ss, rstd, junk, xn = scratch
    ti = next_pst(C)
    tp = C.pst[ti]
    for kc in range(KC):
        P.op("tensor", lambda E, kc=kc: E.transpose(tp[:, kc * 128:(kc + 1) * 128],
                                                     xn[i % 2][:, kc * 128:(kc + 1) * 128], C.ident),
             r=[("pn_xn", i % 2), "ident"], w=[("pst", ti)], inc=(kc == KC - 1))
    for kc in range(KC):
        if False:
            P.op("scalar", lambda E, kc=kc: E.activation(
                out=hT[:, kc, c0:c0 + 128], in_=tp[:, kc * 128:(kc + 1) * 128], func=AF.Identity,
                bias=M.shiftT[:, kc:kc + 1], scale=M.AT[:, kc:kc + 1]),
                r=[("pst", ti), M.key + ("shiftT",), M.key + ("AT",)], w=[("hT", c0 // 128, kc)])
        else:
            P.op("vector", lambda E, kc=kc: E.tensor_scalar(
                out=hT[:, kc, c0:c0 + 128], in0=tp[:, kc * 128:(kc + 1) * 128],
                scalar1=M.AT[:, kc:kc + 1], scalar2=M.shiftT[:, kc:kc + 1], op0=ALU.mult, op1=ALU.add),
                r=[("pst", ti), M.key + ("shiftT",), M.key + ("AT",)], w=[("hT", c0 // 128, kc)])


def prenorm_tiles(C, M, xsrc_tiles, hT, col0, scratch):
    for i, (xa, xk) in enumerate(xsrc_tiles):
        prenorm_p1(C, M, xa, xk, i, scratch)
        prenorm_p2(C, M, hT, col0 + 128 * i, i, scratch)


def alloc_pn_scratch(C):
    A = C.A
    ss = A.alloc([32], F32)
    rstd = A.alloc([32], F32)
    junk = [C.junk, C.junk]
    xn = [A.alloc([D], BF16) for _ in range(2)]
    return ss, rstd, junk, xn


def post_tile(C, M, pi, tile, scr):
    P = C.P
    ss2, rs2, junk2, tmp = scr
    ps = C.psb[pi]
    b = tile % 2
    P.op("scalar", lambda E: E.activation(out=junk2[b], in_=ps, func=AF.Square, accum_out=ss2[:, tile:tile + 1]),
         r=psbk(pi), w=[("po_ss", tile)])
    P.op("scalar", lambda E: E.activation(out=rs2[:, tile:tile + 1], in_=ss2[:, tile:tile + 1], func=AF.Sqrt,
                                           bias=C.epsc, scale=1.0 / D),
         r=[("po_ss", tile), "epsc"], w=[("po_rs", tile)])
    P.op("vector", lambda E: E.reciprocal(out=rs2[:, tile:tile + 1], in_=rs2[:, tile:tile + 1]),
         r=[("po_rs", tile)], w=[("po_rs", tile)])
    P.op("vector", lambda E: E.scalar_tensor_tensor(out=tmp[b], in0=ps, scalar=rs2[:, tile:tile + 1], in1=M.GG,
                                                     op0=ALU.mult, op1=ALU.mult),
         r=psbk(pi) + [("po_rs", tile), M.key + ("GG",)], w=[("po_tmp", 0)])
    P.op("vector", lambda E: E.tensor_tensor(out=C.x[:, tile, :], in0=C.x[:, tile, :], in1=tmp[b], op=ALU.add),
         r=[("po_tmp", 0), ("x", tile)], w=[("x", tile)])


def alloc_post_scratch(C):
    A = C.A
    ss2 = A.alloc([32], F32)
    rs2 = A.alloc([32], F32)
    junk2 = [C.junk, C.junk]
    t0 = A.alloc([D], F32)
    tmp = [t0, t0]
    return ss2, rs2, junk2, tmp


def ffn_sublayer(C, layer):
    P, A, io = C.P, C.A, C.io
    A.mark()
    M = modulation(C, 2 * layer + 1)
    HT = 1024
    hT = A.alloc([KC, HT], BF16)
    gT = A.alloc([FC, HT], BF16)
    wo = A.alloc([FC, D], BF16)
    wi = [A.alloc([KC, 512], BF16) for _ in range(2)]
    sa = [A.alloc([512], F32) for _ in range(2)]
    pn = alloc_pn_scratch(C)
    po = alloc_post_scratch(C)
    lf = C.sel["ffn"].index(layer)
    wov = io["ffn_w_out"].ap()[lf].rearrange("(j p) n -> p j n", p=128)

    def load_wo():
        for pc in range(2):
            P.dma("gpsimd", wo[:, 11 * pc:11 * pc + 11, :], wov[:, 11 * pc:11 * pc + 11, :], w=[("wo", pc)])
    wiv = io["ffn_w_in"].ap()[lf].rearrange("(k p) n -> p k n", p=128)

    def pre(half):
        tiles = [(C.x[:, half * 8 + i, :], ("x", half * 8 + i)) for i in range(8)]
        prenorm_tiles(C, M, tiles, hT, 0, pn)

    def mm1(half):
        for jj in range(FC // 2):
            wb = wi[jj % 2]
            wk = ("wi", jj % 2)
            P.dma("gpsimd", wb[:, :, 0:256], wiv[:, :, jj * 256:(jj + 1) * 256], w=[wk + ("a",)])
            P.dma("gpsimd", wb[:, :, 256:512], wiv[:, :, DFF + jj * 256:DFF + (jj + 1) * 256], w=[wk + ("b",)])
            if half == 0 and jj == 2:
                load_wo()
            for jl in range(2):
                j = 2 * jj + jl
                for tg in range(2):
                    if half == 0 and jj == 0 and jl == 0:
                        tl = [(C.x[:, tg * 4 + i, :], ("x", tg * 4 + i)) for i in range(4)]
                        prenorm_tiles(C, M, tl, hT, 512 * tg, pn)
                    pi = next_psb(C)
                    ps = C.psb[pi]
                    hk = [("hT", tg * 4 + q) for q in range(4)]
                    for part in range(2):
                        for kc in range(KC):
                            P.op("tensor", lambda E, part=part, kc=kc, wb=wb, jl=jl, tg=tg, ps=ps: E.matmul(
                                ps[:, part * 512:(part + 1) * 512],
                                lhsT=wb[:, kc, part * 256 + jl * 128:part * 256 + (jl + 1) * 128],
                                rhs=hT[:, kc, tg * 512:(tg + 1) * 512], start=(kc == 0), stop=(kc == KC - 1)),
                                r=[wk + ("a",), wk + ("b",)] + [(k_[0], k_[1], kc) for k_ in hk], w=psbk(pi),
                                inc=(part == 1 and kc == KC - 1))
                    sb = sa[(2 * j + tg) % 2]
                    sk = ("sa", (2 * j + tg) % 2)
                    P.op("scalar", lambda E, ps=ps, sb=sb: E.activation(out=sb, in_=ps[:, 0:512], func=AF.Silu),
                         r=psbk(pi), w=[sk])
                    P.op("vector", lambda E, ps=ps, sb=sb, j=j, tg=tg: E.tensor_tensor(
                        out=gT[:, j, tg * 512:(tg + 1) * 512], in0=sb, in1=ps[:, 512:1024], op=ALU.mult),
                        r=psbk(pi) + [sk], w=[("gT", j, tg)])

    def mm2(half):
        for ti in range(8):
            if half == 0:
                prenorm_p1(C, M, C.x[:, 8 + ti, :], ("x", 8 + ti), ti, pn)
            pi = next_psb(C)
            ps = C.psb[pi]
            for ch in range(2):
                for j in range(FC):
                    P.op("tensor", lambda E, ch=ch, j=j, ti=ti, ps=ps: E.matmul(
                        ps[:, ch * 512:(ch + 1) * 512], lhsT=gT[:, j, ti * 128:(ti + 1) * 128],
                        rhs=wo[:, j, ch * 512:(ch + 1) * 512], start=(j == 0), stop=(j == FC - 1)),
                        r=[("gT", j, ti // 4), ("wo", j // 11)], w=psbk(pi),
                        inc=(ch == 1 and j == FC - 1))
            if half == 0:
                prenorm_p2(C, M, hT, 128 * ti, ti, pn)
            post_tile(C, M, pi, half * 8 + ti, po)

    mm1(0)
    mm2(0)
    mm1(1)
    mm2(1)
    P.barrier()
    A.release()


def psbk(pi):
    return [("psq", 2 * pi), ("psq", 2 * pi + 1)]


class Halves:
    def __init__(self, C):
        self.C = C
        self.i = 0

    def next(self):
        i = self.i
        self.i = (i + 1) % 6
        ap = self.C.psb[i // 2][:, (i % 2) * 512:(i % 2) * 512 + 512]
        keys = [("psq", i)]
        return ap, keys

    def bank(self, i):
        return self.C.psb[i // 2][:, (i % 2) * 512:(i % 2) * 512 + 512], [("psq", i)]


def hybrid_consts(C, j):
    P, A, io = C.P, C.A, C.io
    H = Ctx()
    H.maskBD = A.alloc([128], F32)
    H.cmask = A.alloc([NCH, 128], BF16)
    H.rowmask = A.alloc([NCH], F32)
    H.smask = A.alloc([512], F32)
    H.smask128 = A.alloc([512], F32)
    H.lb = A.alloc([4], F32)
    H.oml = A.alloc([4], F32)
    H.poolw = A.alloc([4, 128], BF16)
    H.pscale = A.alloc([4], F32)
    H.gain = A.alloc([512], F32)
    H.invcnt = A.alloc([4, 16], F32)
    H.hmask = A.alloc([1], F32)
    H.epsh = A.alloc([1], F32)
    l0 = A.alloc([4], F32)
    l1 = A.alloc([4], F32)
    P.dma("sync", H.maskBD, io["maskBD"].ap(), w=["maskBD"])
    P.dma("gpsimd", H.cmask, io["cmask"].ap(), w=["cmask"])
    P.dma("sync", H.rowmask, io["rowmask"].ap(), w=["rowmask"])
    P.dma("sync", H.smask, io["smask"].ap(), w=["smask"])
    P.dma("sync", H.smask128, io["smask128"].ap(), w=["smask"])
    jl = C.sel["hyb"].index(j)
    P.dma("gpsimd", H.poolw, io["pool_w"].ap()[jl].rearrange("g c d -> c g d"), w=["poolw"])
    P.dma("sync", H.pscale, io["pool_scale_col"].ap()[jl], w=["pscale"])
    P.dma("sync", H.gain, dram_bcast(io["hgrn_gain4"], jl * 512, 512), w=["gain"])
    P.dma("sync", H.invcnt, io["invcnt"].ap(), w=["invcnt"])
    P.dma("sync", H.hmask, io["hmask"].ap(), w=["hmask"])
    P.op("gpsimd", lambda E: E.memset(H.epsh, EPS), w=["epsh"])
    if j == 0:
        P.op("gpsimd", lambda E: E.memset(H.lb, 0.0), w=["lb"])
        P.op("gpsimd", lambda E: E.memset(H.oml, 1.0), w=["oml"])
    else:
        P.dma("sync", l0, io["lb_logits_col"].ap()[0], w=["l0"])
        P.dma("sync", l1, io["lb_logits_col"].ap()[1], w=["l1"])
        P.op("vector", lambda E: E.tensor_tensor(out=l1, in0=l1, in1=l0, op=ALU.subtract), r=["l0", "l1"], w=["l1"])
        P.op("scalar", lambda E: E.activation(out=H.lb, in_=l1, func=AF.Sigmoid), r=["l1"], w=["lb"])
        P.op("vector", lambda E: E.tensor_scalar(out=H.oml, in0=H.lb, scalar1=-1.0, scalar2=1.0,
                                                  op0=ALU.mult, op1=ALU.add), r=["lb"], w=["oml"])
    return H


def hybrid_mixer(C, layer, full):
    P, A, io = C.P, C.A, C.io
    j = layer // 2
    A.mark()
    M = modulation(C, 2 * layer)
    H = hybrid_consts(C, j)
    hv = Halves(C)
    NB = 512
    chl = CH if full else 128
    nch = 128 // chl
    smask = H.smask if full else H.smask128
    hT = A.alloc([KC, 128 + NB], BF16)
    wbuf = [A.alloc([KC, 512], BF16) for _ in range(2)]
    wcnt = [0]
    S = A.alloc([4, 128], F32)
    btot = A.alloc([4], F32)
    bsum = A.alloc([4], F32)
    Sb = [A.alloc([4, 128], BF16) for _ in range(2)]
    tf = [[A.alloc([NB], F32) for _ in range(4)] for _ in range(3)]
    KH = A.alloc([4, NB], BF16)
    KD = A.alloc([4, NB], BF16)
    Dn = A.alloc([4, 4 * NCH], F32)
    V = A.alloc([4, 512], BF16)
    Vm = A.alloc([NCH, 512], BF16)
    KDt = [A.alloc([4, 128], BF16) for _ in range(2)]
    pn = alloc_pn_scratch(C)
    Skeys = [("S", hd) for hd in range(4)]
    if full:
        QH = A.alloc([4, NB], BF16)
        QHm = A.alloc([4, NCH, 128], BF16)
        G = A.alloc([4, 512], BF16)
        ATs = [A.alloc([4, 128], BF16) for _ in range(2)]
        ub = [A.alloc([16 + NB], F32) for _ in range(2)]
        tails = A.alloc([4, 16], F32)
        pt = [A.alloc([16 + NB], F32) for _ in range(2)]
        db = [A.alloc([NB], BF16) for _ in range(2)]
        yT = A.alloc([KC, NB], BF16)
        ysb = [A.alloc([512], BF16) for _ in range(2)]
        ssh = A.alloc([16, 4], F32)
        rsh = A.alloc([16, 4], F32)
        po = alloc_post_scratch(C)
        xh = po[3][0]
        P.dma("sync", xh, io["x_halo"].ap(), w=[("po_tmp", 0)])
        pSl = [tf[i // 4][i % 4].rearrange("p (h v) -> p h v", v=128) for i in range(7)]
        pD = A.alloc([7, 4], F32)
        P.dma("sync", pD, io["pred_D"].ap().rearrange("i p h -> p i h"), w=["pD"])
        P.op("gpsimd", lambda E: E.memset(S, 0.0), w=Skeys)
        for i in range(7):
            P.dma("sync", pSl[i], io["pred_S"].ap()[i], w=[("pS", i)])
        for i in range(7):
            for hd in range(4):
                P.op("vector", lambda E, i=i, hd=hd: E.scalar_tensor_tensor(
                    out=S[:, hd, :], in0=S[:, hd, :], scalar=pD[:, i, hd:hd + 1], in1=pSl[i][:, hd, :],
                    op0=ALU.mult, op1=ALU.add), r=[("S", hd), ("pS", i), "pD"], w=[("S", hd)])
        P.op("scalar", lambda E: E.activation(out=Sb[0], in_=S, func=AF.Copy), r=Skeys,
             w=[("Sb", 0, hd) for hd in range(4)])
        P.barrier()
        wov = io["hyb_w_out"].ap()[C.sel["hyb"].index(j)].rearrange("(k p) n -> p k n", p=128)
    else:
        P.op("gpsimd", lambda E: E.memset(S, 0.0), w=Skeys)
        P.op("gpsimd", lambda E: E.memset(btot, 0.0), w=["btot"])
    sbi = [0, 0, 0, 0]

    wv = io["hyb_w_in"].ap()[C.sel["hyb"].index(j)].rearrange("(k p) n -> p k n", p=128)

    if full:
        block_seq = [("in", 0), ("in", 3), ("in", 4), ("in", 2), ("in", 1), ("out", 0), ("out", 1)]
    else:
        block_seq = [("in", 3), ("in", 2)]
    wseq = block_seq * 4
    issued = [0]
    used = [0]

    def issue_next():
        k = issued[0]
        if k >= len(wseq):
            return
        kind, piece = wseq[k]
        src = wv if kind == "in" else wov
        bb = k % 2
        P.dma("gpsimd", wbuf[bb], src[:, :, piece * 512:(piece + 1) * 512], w=[("hw", bb)])
        issued[0] += 1

    def load_w(src, piece):
        k = used[0]
        used[0] += 1
        assert wseq[k][1] == piece, (wseq[k], piece)
        while issued[0] <= k:
            issue_next()
        nxt = k + 1
        if nxt < len(wseq) and issued[0] == nxt and not (wseq[k][0] == "out" and wseq[nxt][0] == "in"):
            issue_next()
        bb = k % 2
        return wbuf[bb], ("hw", bb)

    def proj_fm(wb, wk, cc, c0, n, hkeys):
        ap, keys = hv.next()
        for kc in range(KC):
            P.op("tensor", lambda E, kc=kc, ap=ap: E.matmul(ap[:, 0:n], lhsT=wb[:, kc, cc * 128:(cc + 1) * 128],
                                                             rhs=hT[:, kc, c0:c0 + n], start=(kc == 0), stop=(kc == KC - 1)),
                 r=[wk] + [(k_[0], k_[1], kc) for k_ in hkeys], w=keys, inc=(kc == KC - 1))
        return ap, keys

    def proj_tm(wb, wk, t):
        ap, keys = hv.next()
        for kc in range(KC):
            P.op("tensor", lambda E, kc=kc, ap=ap: E.matmul(
                ap, lhsT=hT[:, kc, 128 + t * 128:128 + (t + 1) * 128], rhs=wb[:, kc, :],
                start=(kc == 0), stop=(kc == KC - 1)), r=[wk, ("hT", 1 + t, kc)], w=keys, inc=(kc == KC - 1))
        return ap, keys

    for b in range(4):
        tiles = [(C.x[:, 4 * b + i, :], ("x", 4 * b + i)) for i in range(4)]
        hk = [("hT", 1 + i) for i in range(4)]
        if full and b == 0:
            prenorm_tiles(C, M, [(xh, ("po_tmp", 0))], hT, 0, pn)
        if b == 0:
            prenorm_tiles(C, M, tiles, hT, 128, pn)

        if full:
            wb, wk = load_w(wv, 0)
            if b == 0:
                for g in range(4):
                    ap, keys = proj_fm(wb, wk, g, 112, 16, [("hT", 0)])
                    P.op("vector", lambda E, g=g, ap=ap: E.tensor_scalar(out=tails[:, g, :], in0=ap[:, 0:16],
                                                                          scalar1=H.hmask[:, 0:1], scalar2=1.0,
                                                                          op0=ALU.mult, op1=ALU.mult),
                         r=keys + ["hmask"], w=[("tails", g)])
            def pool_head(g):
                ug = ub[g % 2]
                ugk = ("ug", g % 2)
                P.op("gpsimd", lambda E, g=g, ug=ug: E.tensor_copy(out=ug[:, 0:16], in_=tails[:, g, :]),
                     r=[("tails", g)], w=[ugk])
                ap, keys = proj_fm(wb, wk, g, 128, NB, hk)
                P.op("scalar", lambda E, ug=ug, ap=ap: E.activation(out=ug[:, 16:16 + NB], in_=ap, func=AF.Copy),
                     r=keys, w=[ugk])
                P.op("gpsimd", lambda E, g=g, ug=ug: E.tensor_copy(out=tails[:, g, :], in_=ug[:, NB:NB + 16]),
                     r=[ugk], w=[("tails", g)])

            def pool_tail(g):
                win = 2 << g
                ug = ub[g % 2]
                ugk = ("ug", g % 2)
                dg = db[g % 2]
                dgk = ("dg", g % 2)
                cur = ug
                curk = ugk
                sh = 1
                for lev in range(g + 1):
                    o_ = pt[lev % 2]
                    ok = ("pt", lev % 2)
                    eng = "gpsimd" if (lev + g) % 2 == 0 else "vector"
                    P.op(eng, lambda E, o_=o_, cur=cur, sh=sh: E.tensor_tensor(
                        out=o_[:, sh:16 + NB], in0=cur[:, sh:16 + NB], in1=cur[:, 0:16 + NB - sh], op=ALU.add),
                        r=[curk], w=[ok])
                    cur, curk = o_, ok
                    sh *= 2
                P.op("vector", lambda E, dg=dg, ug=ug, cur=cur, win=win: E.scalar_tensor_tensor(
                    out=dg, in0=cur[:, 16:16 + NB], scalar=1.0 / win, in1=ug[:, 16:16 + NB],
                    op0=ALU.mult, op1=ALU.subtract), r=[curk, ugk], w=[dgk])
                if b == 0:
                    P.op("vector", lambda E, g=g, cur=cur: E.tensor_tensor(
                        out=cur[:, 0:16], in0=cur[:, 16:32], in1=H.invcnt[:, g, :], op=ALU.mult),
                        r=[curk, "invcnt", dgk], w=[curk])
                    P.op("vector", lambda E, dg=dg, ug=ug, cur=cur: E.tensor_tensor(
                        out=dg[:, 0:16], in0=cur[:, 0:16], in1=ug[:, 16:32], op=ALU.subtract),
                        r=[curk, ugk], w=[dgk])
                ap, keys = hv.next()
                P.op("tensor", lambda E, g=g, ap=ap, dg=dg: E.matmul(ap, lhsT=H.poolw[:, g, :], rhs=dg,
                                                                     start=True, stop=True),
                     r=[dgk, "poolw"], w=keys)
                P.op("scalar", lambda E, g=g, ap=ap: E.activation(
                    out=yT[:, g, :], in_=ap, func=AF.Copy, scale=H.pscale[:, g:g + 1]),
                    r=keys + ["pscale"], w=[("yT", g)])

            pool_head(0)
            for g in range(4):
                if g + 1 < 4:
                    pool_head(g + 1)
                pool_tail(g)

        wb, wk = load_w(wv, 3)
        for t in range(4):
            ap, keys = proj_tm(wb, wk, t)
            P.op("scalar", lambda E, t=t, ap=ap: E.activation(out=V[:, t, :], in_=ap, func=AF.Copy),
                 r=keys, w=[("V", t)])
        if full:
            wb, wk = load_w(wv, 4)
            for t in range(4):
                ap, keys = proj_tm(wb, wk, t)
                P.op("scalar", lambda E, t=t, ap=ap: E.activation(out=tf[0][t], in_=ap, func=AF.Silu),
                     r=keys, w=[("tf", 0, t)])
                P.op("vector", lambda E, t=t: E.tensor_tensor(out=G[:, t, :], in0=tf[0][t], in1=H.gain, op=ALU.mult),
                     r=[("tf", 0, t), "gain"], w=[("G", t)])

        wb, wk = load_w(wv, 2)
        for hd in range(4):
            ap, keys = proj_fm(wb, wk, hd, 128, NB, hk)
            P.op("scalar", lambda E, hd=hd, ap=ap: E.activation(out=tf[0][hd], in_=ap, func=AF.Sigmoid),
                 r=keys, w=[("tf", 0, hd)])
        for hd in range(4):
            P.op("vector", lambda E, hd=hd: E.tensor_scalar(out=tf[0][hd], in0=tf[0][hd], scalar1=H.oml[:, hd:hd + 1],
                                                             scalar2=H.lb[:, hd:hd + 1], op0=ALU.mult, op1=ALU.add),
                 r=[("tf", 0, hd), "lb", "oml"], w=[("tf", 0, hd)])
        for hd in range(4):
            P.op("scalar", lambda E, hd=hd: E.activation(out=tf[1][hd], in_=tf[0][hd], func=AF.Ln),
                 r=[("tf", 0, hd)], w=[("tf", 1, hd)])
        for hd in range(4):
            P.op("gpsimd", lambda E, hd=hd: E.tensor_scalar(out=tf[0][hd], in0=tf[0][hd], scalar1=-1.0, scalar2=1.0,
                                                             op0=ALU.mult, op1=ALU.add),
                 r=[("tf", 0, hd), ("tf", 1, hd)], w=[("tf", 0, hd)])
            P.op("vector", lambda E, hd=hd: E.tensor_tensor_scan(out=tf[2][hd], data0=smask, data1=tf[1][hd],
                                                                  initial=0.0, op0=ALU.mult, op1=ALU.add),
                 r=[("tf", 1, hd), "smask"], w=[("tf", 2, hd)])
        for hd in range(4):
            bend = tf[2][hd].rearrange("p (n c) -> p n c", c=chl)[:, :, chl - 1]
            P.op("scalar", lambda E, hd=hd, bend=bend: E.activation(out=Dn[:, hd, 0:4 * nch], in_=bend, func=AF.Exp),
                 r=[("tf", 2, hd)], w=[("Dn", hd)])
            if not full:
                P.op("vector", lambda E, hd=hd, bend=bend: E.tensor_reduce(out=bsum[:, hd:hd + 1], in_=bend,
                                                                             axis=AX.X, op=ALU.add),
                     r=[("tf", 2, hd)], w=[("bsum", hd)])
                P.op("vector", lambda E, hd=hd: E.tensor_tensor(out=btot[:, hd:hd + 1], in0=btot[:, hd:hd + 1],
                                                                 in1=bsum[:, hd:hd + 1], op=ALU.add),
                     r=[("bsum", hd), "btot"], w=["btot"])
            if full:
                P.op("scalar", lambda E, hd=hd: E.activation(out=tf[1][hd], in_=tf[2][hd], func=AF.Exp, scale=-1.0),
                     r=[("tf", 2, hd)], w=[("tf", 1, hd)])
            else:
                P.op("vector", lambda E, hd=hd, bend=bend: E.tensor_tensor(
                    out=tf[1][hd].rearrange("p (n c) -> p n c", c=chl),
                    in0=bend.unsqueeze(2).to_broadcast([128, 4 * nch, chl]),
                    in1=tf[2][hd].rearrange("p (n c) -> p n c", c=chl), op=ALU.subtract),
                    r=[("tf", 2, hd)], w=[("tf", 1, hd)])
                P.op("scalar", lambda E, hd=hd: E.activation(out=tf[1][hd], in_=tf[1][hd], func=AF.Exp),
                     r=[("tf", 1, hd)], w=[("tf", 1, hd)])
            if full:
                P.op("scalar", lambda E, hd=hd: E.activation(out=tf[2][hd], in_=tf[2][hd], func=AF.Exp),
                     r=[("tf", 2, hd), ("Dn", hd), ("tf", 1, hd)], w=[("tf", 2, hd)])
        for hd in range(4):
            P.op("gpsimd", lambda E, hd=hd: E.tensor_tensor(out=tf[1][hd], in0=tf[1][hd], in1=tf[0][hd], op=ALU.mult),
                 r=[("tf", 0, hd), ("tf", 1, hd)], w=[("tf", 1, hd)])
            if full:
                P.op("gpsimd", lambda E, hd=hd: E.tensor_copy(out=KH[:, hd, :], in_=tf[1][hd]),
                     r=[("tf", 1, hd)], w=[("KH", hd)])
            if full:
                P.op("vector", lambda E, hd=hd: E.tensor_tensor(
                    out=KD[:, hd, :].rearrange("p (n c) -> p n c", c=CH),
                    in0=tf[1][hd].rearrange("p (n c) -> p n c", c=CH),
                    in1=Dn[:, hd, :].unsqueeze(2).to_broadcast([128, 4 * NCH, CH]), op=ALU.mult),
                    r=[("tf", 1, hd), ("Dn", hd)], w=[("KD", hd)])
            else:
                P.op("vector", lambda E, hd=hd: E.tensor_copy(out=KD[:, hd, :], in_=tf[1][hd]),
                     r=[("tf", 1, hd)], w=[("KD", hd)])
        if full:
            wb, wk = load_w(wv, 1)
            for hd in range(4):
                ap, keys = proj_fm(wb, wk, hd, 128, NB, hk)
                P.op("scalar", lambda E, hd=hd, ap=ap: E.activation(out=tf[0][hd], in_=ap, func=AF.Silu),
                     r=keys + [("tf", 1, hd)], w=[("tf", 0, hd)])
                P.op("vector", lambda E, hd=hd: E.tensor_tensor(out=QH[:, hd, :], in0=tf[0][hd], in1=tf[2][hd], op=ALU.mult),
                     r=[("tf", 0, hd), ("tf", 2, hd)], w=[("QH", hd)])
            wo0, wok0 = load_w(wov, 0)
            wo1, wok1 = load_w(wov, 1)

        for t in range(4):
            tile = 4 * b + t
            kb = t % 2
            if b < 3:
                prenorm_p1(C, M, C.x[:, 4 * (b + 1) + t, :], ("x", 4 * (b + 1) + t), t, pn)
            ti = next_pst(C)
            tp = C.pst[ti]
            for hd in range(4):
                P.op("tensor", lambda E, hd=hd, t=t, tp=tp: E.transpose(tp[:, hd * 128:(hd + 1) * 128],
                                                                          KD[:, hd, t * 128:(t + 1) * 128], C.ident),
                     r=[("KD", hd), "ident"], w=[("pst", ti)], inc=(hd == 3))
            P.op("scalar", lambda E, tp=tp, kb=kb: E.activation(out=KDt[kb], in_=tp[:, 0:512].rearrange("p (h k) -> p h k", k=128),
                                                                func=AF.Copy),
                 r=[("pst", ti)], w=[("KDt", kb)])
            for n in range(NCH if full else 0):
                if n % 2 == 0:
                    P.op("gpsimd", lambda E, t=t, n=n: E.tensor_scalar(out=Vm[:, n, :], in0=V[:, t, :],
                                                                       scalar1=H.rowmask[:, n:n + 1], scalar2=1.0,
                                                                       op0=ALU.mult, op1=ALU.mult),
                         r=[("V", t), "rowmask"], w=[("Vm", n)])
                else:
                    P.op("scalar", lambda E, t=t, n=n: E.activation(out=Vm[:, n, :], in_=V[:, t, :], func=AF.Copy,
                                                                    scale=H.rowmask[:, n:n + 1]),
                         r=[("V", t), "rowmask"], w=[("Vm", n)])
            if full:
                for hd in range(4):
                    P.op("vector", lambda E, hd=hd, t=t: E.tensor_tensor(
                        out=QHm[:, hd, :, :], in0=QH[:, hd, t * 128:(t + 1) * 128].unsqueeze(1).to_broadcast([128, NCH, 128]),
                        in1=H.cmask, op=ALU.mult), r=[("QH", hd), "cmask"], w=[("QHm", hd)])
                a_ap, a_keys = hv.bank(5 - (t % 2))
                for hd in range(4):
                    P.op("tensor", lambda E, hd=hd, t=t, a_ap=a_ap: E.matmul(
                        a_ap[:, hd * 128:(hd + 1) * 128], lhsT=KH[:, hd, t * 128:(t + 1) * 128],
                        rhs=QH[:, hd, t * 128:(t + 1) * 128], start=True, stop=True),
                        r=[("KH", hd), ("QH", hd)], w=a_keys, inc=(hd == 3))
                P.op("vector", lambda E, a_ap=a_ap, kb=kb: E.tensor_tensor(
                    out=ATs[kb], in0=a_ap.rearrange("p (h t) -> p h t", t=128),
                    in1=H.maskBD.unsqueeze(1).to_broadcast([128, 4, 128]), op=ALU.mult),
                    r=a_keys + ["maskBD"], w=[("ATs", kb)])
                o_ap, o_keys = hv.bank(4 + (t % 2))
                for hd in range(4):
                    P.op("tensor", lambda E, hd=hd, t=t, o_ap=o_ap, kb=kb: E.matmul(
                        o_ap[:, hd * 128:(hd + 1) * 128], lhsT=ATs[kb][:, hd, :], rhs=V[:, t, hd * 128:(hd + 1) * 128],
                        start=(hd == 0), stop=False), r=[("ATs", kb), ("V", t)], w=o_keys, inc=False)
            for n in range(nch):
                for hd in range(4):
                    u_ap, u_keys = hv.bank(hd)
                    vrhs = Vm[:, n, hd * 128:(hd + 1) * 128] if full else V[:, t, hd * 128:(hd + 1) * 128]
                    vkey = ("Vm", n) if full else ("V", t)
                    if full:
                        cb = sbi[hd]
                        P.op("tensor", lambda E, hd=hd, n=n, cb=cb, o_ap=o_ap: E.matmul(
                            o_ap[:, hd * 128:(hd + 1) * 128], lhsT=QHm[:, hd, n, :], rhs=Sb[cb][:, hd, :],
                            start=False, stop=(n == NCH - 1 and hd == 3)), r=[("QHm", hd), ("Sb", cb, hd)], w=o_keys,
                            inc=False)
                    P.op("tensor", lambda E, hd=hd, kb=kb, u_ap=u_ap, vrhs=vrhs: E.matmul(
                        u_ap[:, 0:128], lhsT=KDt[kb][:, hd, :], rhs=vrhs,
                        start=True, stop=True), r=[("KDt", kb), vkey], w=u_keys, inc=True)
                    P.op("vector", lambda E, hd=hd, t=t, n=n, u_ap=u_ap: E.scalar_tensor_tensor(
                        out=S[:, hd, :], in0=S[:, hd, :], scalar=Dn[:, hd, nch * t + n:nch * t + n + 1],
                        in1=u_ap[:, 0:128], op0=ALU.mult, op1=ALU.add),
                        r=u_keys + [("Dn", hd), ("S", hd)], w=[("S", hd)])
                    if full:
                        nb_ = 1 - sbi[hd]
                        P.op("scalar", lambda E, hd=hd, nb_=nb_: E.activation(out=Sb[nb_][:, hd, :], in_=S[:, hd, :], func=AF.Copy),
                             r=[("S", hd)], w=[("Sb", nb_, hd)])
                        sbi[hd] = nb_
            if full:
                yb = ysb[t % 2]
                for hd in range(4):
                    P.op("scalar", lambda E, hd=hd, o_ap=o_ap, tile=tile: E.activation(
                        out=C.junk[:, hd * 128:(hd + 1) * 128], in_=o_ap[:, hd * 128:(hd + 1) * 128], func=AF.Square,
                        accum_out=ssh[:, tile, hd:hd + 1]), r=o_keys, w=[("ssh", tile, hd)])
                P.op("scalar", lambda E, tile=tile: E.activation(out=rsh[:, tile, :], in_=ssh[:, tile, :], func=AF.Sqrt,
                                                                  bias=H.epsh, scale=1.0 / 128),
                     r=[("ssh", tile, hd) for hd in range(4)] + ["epsh"], w=[("rsh", tile)])
                P.op("vector", lambda E, tile=tile: E.reciprocal(out=rsh[:, tile, :], in_=rsh[:, tile, :]),
                     r=[("rsh", tile)], w=[("rsh", tile)])
                for hd in range(4):
                    P.op("vector", lambda E, hd=hd, o_ap=o_ap, tile=tile, t=t, yb=yb: E.scalar_tensor_tensor(
                        out=yb[:, hd * 128:(hd + 1) * 128], in0=o_ap[:, hd * 128:(hd + 1) * 128],
                        scalar=rsh[:, tile, hd:hd + 1], in1=G[:, t, hd * 128:(hd + 1) * 128], op0=ALU.mult, op1=ALU.mult),
                        r=o_keys + [("rsh", tile), ("G", t)], w=[("ysb", t % 2)])
                ti2 = next_pst(C)
                tp2 = C.pst[ti2]
                for hd in range(4):
                    P.op("tensor", lambda E, hd=hd, tp2=tp2, yb=yb: E.transpose(tp2[:, hd * 128:(hd + 1) * 128],
                                                                                  yb[:, hd * 128:(hd + 1) * 128], C.ident),
                         r=[("ysb", t % 2), "ident"], w=[("pst", ti2)], inc=(hd == 3))
                P.op("vector", lambda E, tp2=tp2, t=t: E.tensor_copy(
                    out=yT[:, 4:8, t * 128:(t + 1) * 128], in_=tp2[:, 0:512].rearrange("p (h t) -> p h t", t=128)),
                    r=[("pst", ti2)], w=[("yT", 4, t)])
            if b < 3:
                prenorm_p2(C, M, hT, 128 + 128 * t, t, pn)
        if full:
            for t in range(4):
                pi = next_psb(C)
                ps = C.psb[pi]
                for ch, (wo_, wok_) in enumerate(((wo0, wok0), (wo1, wok1))):
                    for c in range(KC):
                        P.op("tensor", lambda E, ch=ch, c=c, t=t, ps=ps, wo_=wo_: E.matmul(
                            ps[:, ch * 512:(ch + 1) * 512], lhsT=yT[:, c, t * 128:(t + 1) * 128], rhs=wo_[:, c, :],
                            start=(c == 0), stop=(c == KC - 1)),
                            r=[wok_, ("yT", c) if c < 4 else ("yT", 4, t)], w=psbk(pi), inc=(ch == 1 and c == KC - 1))
                post_tile(C, M, pi, 4 * b + t, po)
            if issued[0] == used[0]:
                issue_next()

    if not full:
        P.op("scalar", lambda E: E.activation(out=btot, in_=btot, func=AF.Exp), r=["btot"], w=["btot"])
        P.dma("sync", io["S_out"].ap(), S, r=Skeys)
        P.dma("sync", io["D_out"].ap(), btot, r=["btot"])
    P.barrier()
    A.release()


DIL = (1, 4, 16)
NEG = -30000.0
DBG = {}


def attn_qkv(C, layer):
    P, A, io = C.P, C.A, C.io
    ja = layer // 2
    A.mark()
    M = modulation(C, 2 * layer)
    hv = Halves(C)
    hT = A.alloc([KC, NT], BF16)
    pn = alloc_pn_scratch(C)
    wbuf = [A.alloc([KC, 512], BF16) for _ in range(2)]
    stg = [A.alloc([4, NT], BF16) for _ in range(2)]
    vst = [A.alloc([NTILE, 512], BF16) for _ in range(2)]
    wv = io["att_w_qkv"].ap()[C.sel["att"].index(ja)].rearrange("(k p) n -> p k n", p=128)
    ev = 0
    for piece in range(9):
        sidx, g = piece // 3, piece % 3
        wb = wbuf[piece % 2]
        wk = ("aw", piece % 2)
        P.dma("gpsimd", wb, wv[:, :, piece * 512:(piece + 1) * 512], w=[wk])
        if sidx < 2:
            st = stg[piece % 2]
            sk = ("stg", piece % 2)
            for tg in range(4):
                if piece == 0:
                    tl = [(C.x[:, tg * 4 + i, :], ("x", tg * 4 + i)) for i in range(4)]
                    prenorm_tiles(C, M, tl, hT, 512 * tg, pn)
                for cc in range(4):
                    ap, keys = hv.next()
                    hk = [("hT", tg * 4 + q) for q in range(4)]
                    for kc in range(KC):
                        P.op("tensor", lambda E, kc=kc, ap=ap, wb=wb, cc=cc, tg=tg: E.matmul(
                            ap, lhsT=wb[:, kc, cc * 128:(cc + 1) * 128], rhs=hT[:, kc, tg * 512:(tg + 1) * 512],
                            start=(kc == 0), stop=(kc == KC - 1)), r=[wk] + [(k_[0], k_[1], kc) for k_ in hk], w=keys, inc=(kc == KC - 1))
                    sc = 0.125 if sidx == 0 else 1.0
                    if ev % 2 == 0:
                        P.op("scalar", lambda E, ap=ap, st=st, cc=cc, tg=tg, sc=sc: E.activation(
                            out=st[:, cc, tg * 512:(tg + 1) * 512], in_=ap, func=AF.Copy, scale=sc), r=keys, w=[sk + (cc, tg)])
                    else:
                        P.op("vector", lambda E, ap=ap, st=st, cc=cc, tg=tg, sc=sc: E.tensor_scalar(
                            out=st[:, cc, tg * 512:(tg + 1) * 512], in0=ap, scalar1=sc, scalar2=1.0,
                            op0=ALU.mult, op1=ALU.mult), r=keys, w=[sk + (cc, tg)])
                    ev += 1
            dst = io["qT_out" if sidx == 0 else "kT_out"].ap()[g].rearrange("c p t -> p c t")
            P.dma("sync", dst, st, r=[sk + (cc_, tg_) for cc_ in range(4) for tg_ in range(4)])
        else:
            st = vst[piece % 2]
            sk = ("vst", piece % 2)
            for t in range(NTILE):
                ap, keys = hv.next()
                for kc in range(KC):
                    P.op("tensor", lambda E, kc=kc, ap=ap, wb=wb, t=t: E.matmul(
                        ap, lhsT=hT[:, kc, t * 128:(t + 1) * 128], rhs=wb[:, kc, :],
                        start=(kc == 0), stop=(kc == KC - 1)), r=[wk, ("hT", t, kc)], w=keys, inc=(kc == KC - 1))
                if ev % 2 == 0:
                    P.op("scalar", lambda E, ap=ap, st=st, t=t: E.activation(out=st[:, t, :], in_=ap, func=AF.Copy),
                         r=keys, w=[sk + (t,)])
                else:
                    P.op("vector", lambda E, ap=ap, st=st, t=t: E.tensor_copy(out=st[:, t, :], in_=ap), r=keys, w=[sk + (t,)])
                ev += 1
            dst = io["v_out"].ap()[g].rearrange("(n p) f -> p n f", p=128)
            P.dma("sync", dst, st, r=[sk + (t_,) for t_ in range(NTILE)])
    P.barrier()
    A.release()


def attn_bias_tables(C):
    P, A, io = C.P, C.A, C.io
    A.mark()
    hv = Halves(C)
    tab = A.alloc([24], F32)
    oh = A.alloc([3 * 510], F32)
    ngm = A.alloc([3 * 510], F32)
    wsb = A.alloc([3 * 510], F32)
    P.dma("sync", tab[0:32, :], io["rel_bias"].ap(), w=["tab"])
    P.dma("sync", oh[0:32, :], io["bias_onehot"].ap(), w=["oh"])
    P.dma("sync", ngm[0:24, :], io["bias_neg"].ap(), w=["ngm"])
    for g in range(3):
        ap, keys = hv.next()
        P.op("tensor", lambda E, ap=ap, g=g: E.matmul(ap[0:24, 0:510], lhsT=tab[0:32, :], rhs=oh[0:32, g * 510:(g + 1) * 510],
                                                       start=True, stop=True), r=["tab", "oh"], w=keys)
        P.op("vector", lambda E, ap=ap, g=g: E.tensor_tensor(out=wsb[0:24, g * 510:(g + 1) * 510], in0=ap[0:24, 0:510],
                                                              in1=ngm[0:24, g * 510:(g + 1) * 510], op=ALU.add),
             r=keys + ["ngm"], w=["wsb"])
    P.dma("sync", C.wd.ap(), wsb[0:24, :], r=["wsb"], w=["wd"])
    P.barrier()
    A.release()


def attn_core(C, layer):
    P, A, io = C.P, C.A, C.io
    ja = layer // 2
    A.mark()
    M = modulation(C, 2 * layer)
    hv = Halves(C)
    acc = A.alloc([2, NT], F32)
    yT = A.alloc([8, NT], BF16)
    sel = A.alloc([64], F32)
    hb = A.alloc([1], F32)
    zb = A.alloc([1], F32)
    hm01 = A.alloc([1], F32)
    rec = [A.alloc([512], F32) for _ in range(2)]
    A.mark()
    qb = [A.alloc([2, 16, 128], BF16) for _ in range(2)]
    kb_ = [A.alloc([2, 16, 128], BF16) for _ in range(2)]
    khb = [A.alloc([2, 16, 128], BF16) for _ in range(2)]
    vb = [A.alloc([16, 2, 65], BF16) for _ in range(2)]
    vhb = [A.alloc([16, 2, 65], BF16) for _ in range(2)]
    btb = [A.alloc([2, 256], F32) for _ in range(2)]
    Tb = [A.alloc([512], F32) for _ in range(2)]
    Pb = [A.alloc([512], BF16) for _ in range(4)]
    ebh = [A.alloc([2, 256], F32) for _ in range(2)]
    P.dma("sync", sel[0:65, :], io["sel65"].ap(), w=["sel"])
    P.dma("sync", hb, io["halo_bias"].ap(), w=["hb"])
    P.op("gpsimd", lambda E: E.memset(zb, 0.0), w=["zb"])
    P.op("vector", lambda E: E.tensor_scalar(out=hm01, in0=hb, scalar1=-1.0 / NEG, scalar2=1.0, op0=ALU.mult, op1=ALU.add),
         r=["hb"], w=["hm01"])
    wov = io["att_w_out"].ap()[C.sel["att"].index(ja)].rearrange("(h e) n -> e h n", e=64)
    NH = (1, 4, 16)
    it = 0
    for c in range(4):
        for g in range(3):
            b = it % 2
            it += 1
            dil = DIL[g]
            nh = NH[g]
            P.dma("sync", qb[b][0:64], io["qn"].ap()[g, 2 * c:2 * c + 2].rearrange("h p b t -> p h b t"), w=[("qb", b)])
            P.dma("sync", kb_[b][0:64], io["kn"].ap()[g, 2 * c:2 * c + 2].rearrange("h p b t -> p h b t"), w=[("kb", b)])
            P.dma("sync", khb[b][0:64, :, 0:nh, :], io["kh%d" % g].ap()[2 * c:2 * c + 2].rearrange("h p b t -> p h b t"),
                  w=[("khb", b)])
            if DBG.get("nov"):
                P.op("gpsimd", lambda E, b=b: E.memset(vb[b], 1.0), w=[("vb", b)])
                P.op("gpsimd", lambda E, b=b: E.memset(vhb[b], 1.0), w=[("vhb", b)])
            else:
                P.dma("sync", vb[b], io["vn"].ap()[g, :, :, 2 * c:2 * c + 2, :], w=[("vb", b)])
                P.dma("sync", vhb[b][:, 0:nh, :, :], io["vh%d" % g].ap()[:, :, 2 * c:2 * c + 2, :], w=[("vhb", b)])
            for hh in range(2):
                for part in range(2):
                    if DBG.get("nobias"):
                        P.op("gpsimd", lambda E, b=b, hh=hh, part=part: E.memset(btb[b][:, hh, part * 128:(part + 1) * 128], 0.0),
                             w=[("btb", b)])
                        continue
                    src = bass.AP(C.wd, (2 * c + hh + 8 * g) * 1530 + g * 510 + part * 255, [[1, 128], [1, 128]])
                    P.dma("sync", btb[b][:, hh, part * 128:(part + 1) * 128], src, r=["wd"], w=[("btb", b)])
            P.op("scalar", lambda E, b=b: E.activation(out=btb[b], in_=btb[b], func=AF.Exp, bias=zb), r=[("btb", b), "zb"], w=[("btb", b)])
            P.op("vector", lambda E, b=b: E.tensor_scalar(out=ebh[b][:, :, 0:128], in0=btb[b][:, :, 0:128], scalar1=hm01[:, 0:1],
                                                      scalar2=1.0, op0=ALU.mult, op1=ALU.mult), r=[("btb", b), "hm01"], w=[("ebh", b)])
            P.op("gpsimd", lambda E, b=b: E.tensor_copy(out=ebh[b][:, :, 128:256], in_=btb[b][:, :, 128:256]), r=[("btb", b)], w=[("ebh", b)])
            nres = dil
            nblk = 16 // dil
            units = [(r, n) for r in range(nres) for n in range(nblk)]
            if DBG.get("g0only") and g > 0:
                units = []
            if DBG.get("noattn"):
                if g == 0:
                    P.op("gpsimd", lambda E: E.memset(acc[0:65, :, :], 1.0), w=[("acc", q) for q in range(4)])
                units = []
            st = {}

            def stage_a(i):
                r, n = units[i]
                bid = r * nblk + n
                halo = (n == 0)
                s_ap, s_keys = hv.next()
                st[i] = (s_ap, s_keys)
                for hh in range(2):
                    kprev = khb[b][0:64, hh, r, :] if halo else kb_[b][0:64, hh, bid - 1, :]
                    P.op("tensor", lambda E, b=b, hh=hh, kprev=kprev, bid=bid, s_ap=s_ap: E.matmul(
                        s_ap[:, hh * 256:hh * 256 + 128], lhsT=kprev, rhs=qb[b][0:64, hh, bid, :], start=True, stop=True),
                        r=[("khb", b), ("kb", b), ("qb", b)], w=s_keys, inc=False)
                    P.op("tensor", lambda E, b=b, hh=hh, bid=bid, s_ap=s_ap: E.matmul(
                        s_ap[:, hh * 256 + 128:hh * 256 + 256], lhsT=kb_[b][0:64, hh, bid, :], rhs=qb[b][0:64, hh, bid, :],
                        start=True, stop=True), r=[("kb", b), ("qb", b)], w=s_keys, inc=(hh == 1))

            def stage_b(i):
                r, n = units[i]
                halo = (n == 0)
                s_ap, s_keys = st[i]
                tb = i % 2
                pb = i % 4
                P.op("scalar", lambda E, b=b, s_ap=s_ap, tb=tb: E.activation(out=Tb[tb], in_=s_ap, func=AF.Exp, bias=zb),
                     r=s_keys + ["zb"], w=[("Tb", tb)])
                ebt = ebh[b] if halo else btb[b]
                ebk = ("ebh", b) if halo else ("btb", b)
                eng = "vector" if i % 2 == 0 else "gpsimd"
                P.op(eng, lambda E, b=b, tb=tb, pb=pb, ebt=ebt: E.tensor_tensor(
                    out=Pb[pb], in0=Tb[tb], in1=ebt.rearrange("p h c -> p (h c)"), op=ALU.mult),
                    r=[("Tb", tb), ebk], w=[("Pb", pb)])

            def stage_c(i):
                r, n = units[i]
                bid = r * nblk + n
                halo = (n == 0)
                tb = i % 4
                o_ap, o_keys = hv.next()
                for hh in range(2):
                    vprev = vhb[b][:, r, hh, :] if halo else vb[b][:, bid - 1, hh, :]
                    P.op("tensor", lambda E, b=b, hh=hh, vprev=vprev, o_ap=o_ap, tb=tb: E.matmul(
                        o_ap[0:65, hh * 128:(hh + 1) * 128], lhsT=vprev, rhs=Pb[tb][:, hh * 256:hh * 256 + 128],
                        start=True, stop=False), r=[("vhb", b), ("vb", b), ("Pb", tb)], w=o_keys, inc=False)
                    P.op("tensor", lambda E, b=b, hh=hh, bid=bid, o_ap=o_ap, tb=tb: E.matmul(
                        o_ap[0:65, hh * 128:(hh + 1) * 128], lhsT=vb[b][:, bid, hh, :], rhs=Pb[tb][:, hh * 256 + 128:hh * 256 + 256],
                        start=False, stop=True), r=[("vb", b), ("Pb", tb)], w=o_keys, inc=(hh == 1))
                t0 = dil * 128 * n + r
                dst = acc[0:65, :, t0:t0 + dil * 127 + 1:dil]
                srcp = o_ap[0:65, 0:256].rearrange("p (h t) -> p h t", t=128)
                akeys = [("acc", q) for q in range(4)] if dil == 16 else [("acc", (dil * 128 * n) // 512)]
                if g == 0:
                    P.op("scalar", lambda E, b=b, dst=dst, srcp=srcp: E.activation(out=dst, in_=srcp, func=AF.Copy),
                         r=o_keys, w=akeys)
                else:
                    P.op("vector", lambda E, b=b, dst=dst, srcp=srcp: E.tensor_tensor(out=dst, in0=dst, in1=srcp, op=ALU.add),
                         r=o_keys + akeys, w=akeys)

            nu = len(units)
            for i in range(nu + 4):
                if i < nu:
                    stage_a(i)
                if 0 <= i - 2 < nu:
                    stage_b(i - 2)
                if 0 <= i - 4 < nu:
                    stage_c(i - 4)
        for hh in range(2):
            for tq in range(4):
                if DBG.get("nonorm"):
                    P.op("gpsimd", lambda E, hh=hh, tq=tq, c=c: E.tensor_copy(
                        out=yT[0:64, 2 * c + hh, tq * 512:(tq + 1) * 512], in_=acc[0:64, hh, tq * 512:(tq + 1) * 512]),
                        r=[("acc", tq)], w=[("yT", 2 * c + hh, tq)])
                    continue
                l_ap, l_keys = hv.next()
                P.op("tensor", lambda E, l_ap=l_ap, hh=hh, tq=tq: E.matmul(
                    l_ap[0:64, :], lhsT=sel[64:65, :], rhs=acc[64:65, hh, tq * 512:(tq + 1) * 512], start=True, stop=True),
                    r=["sel", ("acc", tq)], w=l_keys)
                rb = (hh * 4 + tq) % 2
                P.op("vector", lambda E, l_ap=l_ap, rb=rb: E.reciprocal(out=rec[rb][0:64, :], in_=l_ap[0:64, :]),
                     r=l_keys, w=[("rec", rb)])
                P.op("vector", lambda E, rb=rb, hh=hh, tq=tq, c=c: E.tensor_tensor(
                    out=yT[0:64, 2 * c + hh, tq * 512:(tq + 1) * 512], in0=acc[0:64, hh, tq * 512:(tq + 1) * 512],
                    in1=rec[rb][0:64, :], op=ALU.mult), r=[("rec", rb), ("acc", tq)], w=[("yT", 2 * c + hh, tq)])
    P.barrier()
    A.release()
    wo = A.alloc([8, D], BF16)
    po = alloc_post_scratch(C)
    P.dma("gpsimd", wo[0:64, :, :], wov, w=["wo"])
    for t in range(NTILE):
        pi = next_psb(C)
        ps = C.psb[pi]
        for ch in range(2):
            for h in range(8):
                P.op("tensor", lambda E, ch=ch, h=h, t=t, ps=ps: E.matmul(
                    ps[:, ch * 512:(ch + 1) * 512], lhsT=yT[0:64, h, t * 128:(t + 1) * 128],
                    rhs=wo[0:64, h, ch * 512:(ch + 1) * 512], start=(h == 0), stop=(h == 7)),
                    r=["wo", ("yT", h, t // 4)], w=psbk(pi), inc=(ch == 1 and h == 7))
        post_tile(C, M, pi, t, po)
    P.barrier()
    A.release()


def declare_io(nc, names_shapes_in, names_shapes_out):
    io = {}
    for name, shape, dt in names_shapes_in:
        io[name] = nc.dram_tensor(name, list(shape), dt, kind="ExternalInput")
    for name, shape, dt in names_shapes_out:
        io[name] = nc.dram_tensor(name, list(shape), dt, kind="ExternalOutput")
    return io


HYB_IN = [
    ("hyb_w_in", (1, D, 2560), F32),
    ("hyb_w_out", (1, D, D), F32),
    ("pool_w", (1, 4, 128, 128), F32),
    ("pool_scale_col", (1, 128, 4), F32),
    ("hgrn_gain4", (512,), F32),
    ("lb_logits_col", (2, 128, 4), F32),
    ("maskBD", (128, 128), F32),
    ("cmask", (128, NCH, 128), F32),
    ("rowmask", (128, NCH), F32),
    ("smask", (128, 512), F32),
    ("smask128", (128, 512), F32),
    ("invcnt", (128, 4, 16), F32),
    ("hmask", (128, 1), F32),
]


def hybrid_inputs(inp, core):
    p = np.arange(128)
    maskBD = ((p[:, None] // CH == p[None, :] // CH) & (p[:, None] <= p[None, :])).astype(np.float32)
    cmask = np.zeros((128, NCH, 128), np.float32)
    for n in range(NCH):
        cmask[:, n, CH * n:CH * n + CH] = 1.0
    rowmask = (p[:, None] // CH == np.arange(NCH)[None, :]).astype(np.float32)
    smask = np.ones((128, 512), np.float32)
    smask[:, ::CH] = 0.0
    smask128 = np.ones((128, 512), np.float32)
    smask128[:, ::128] = 0.0
    invcnt = np.zeros((128, 4, 16), np.float32)
    for g in range(4):
        win = 2 << g
        if core == 0:
            invcnt[:, g, :] = 1.0 / np.minimum(np.arange(16) + 1, win)
        else:
            invcnt[:, g, :] = 1.0 / win
    return {
        "maskBD": maskBD, "cmask": cmask, "rowmask": rowmask, "smask": smask, "smask128": smask128, "invcnt": invcnt,
        "hmask": np.full((128, 1), 0.0 if core == 0 else 1.0, np.float32),
    }


def x_halo_tile(xfull, core):
    t = np.zeros((128, D), np.float32)
    if core > 0:
        t[112:128] = xfull[core * NT - 16:core * NT]
    return t


def pred_states(S_list, D_list, core):
    pS = np.zeros((7, 128, 4, 128), np.float32)
    pD = np.ones((7, 128, 4), np.float32)
    for i in range(core):
        pS[i] = np.asarray(S_list[i]).reshape(128, 4, 128)
        pD[i] = np.asarray(D_list[i]).reshape(128, 4)
    return pS, pD


ATT_QKV_IN = [("att_w_qkv", (1, D, 4608), F32)]
ATT_QKV_OUT = [("qT_out", (3, 4, 128, NT), BF16), ("kT_out", (3, 4, 128, NT), BF16), ("v_out", (3, NT, 512), BF16)]
ATT_CORE_IN = [
    ("att_w_out", (1, 512, D), F32),
    ("rel_bias", (32, 24), F32),
    ("bias_onehot", (32, 1530), F32),
    ("bias_neg", (24, 1530), F32),
    ("sel65", (65, 64), F32),
    ("halo_bias", (128, 1), F32),
    ("qn", (3, 8, 64, 16, 128), BF16),
    ("kn", (3, 8, 64, 16, 128), BF16),
    ("vn", (3, 128, 16, 8, 65), BF16),
    ("kh0", (8, 64, 1, 128), BF16), ("kh1", (8, 64, 4, 128), BF16), ("kh2", (8, 64, 16, 128), BF16),
    ("vh0", (128, 1, 8, 65), BF16), ("vh1", (128, 4, 8, 65), BF16), ("vh2", (128, 16, 8, 65), BF16),
]


def t5_bucket_np(dist):
    dist = np.asarray(dist, np.int64)
    df = np.maximum(dist, 1).astype(np.float32)
    large = 16 + (np.log(df / np.float32(16)) / np.float32(np.log(2048 / 16)) * np.float32(16)).astype(np.int32)
    large = np.minimum(large, 31)
    return np.where(dist < 16, dist, large)


def attn_consts(inp, core):
    oh = np.zeros((32, 1530), np.float32)
    ng = np.zeros((24, 1530), np.float32)
    m = np.arange(255)
    for g in range(3):
        dil = DIL[g]
        dp = 1 + m
        bp = t5_bucket_np(dp * dil)
        do = m - 127
        bo = t5_bucket_np(np.maximum(do, 0) * dil)
        for mm in range(255):
            if mm <= 127:
                oh[bp[mm], g * 510 + mm] = 1.0
            else:
                ng[:, g * 510 + mm] = NEG
            if mm >= 127:
                oh[bo[mm], g * 510 + 255 + mm] = 1.0
            else:
                ng[:, g * 510 + 255 + mm] = NEG
    sel = np.zeros((65, 64), np.float32)
    sel[64, :] = 1.0
    return {
        "att_w_out": np.ascontiguousarray(inp["att_w_out"], np.float32),
        "rel_bias": np.ascontiguousarray(inp["rel_bias"], np.float32),
        "bias_onehot": oh, "bias_neg": ng, "sel65": sel,
        "halo_bias": np.full((128, 1), NEG if core == 0 else 0.0, np.float32),
    }


def block_tokens(g):
    dil = DIL[g]
    nblk = 16 // dil
    idx = np.zeros((16, 128), np.int64)
    for r in range(dil):
        for n in range(nblk):
            idx[r * nblk + n] = dil * (128 * n + np.arange(128)) + r
    return idx


def attn_layout(qT, kT, v, kT_prev, v_prev):
    out = {}
    bf = qT.dtype
    qn = np.zeros((3, 8, 64, 16, 128), bf)
    kn = np.zeros((3, 8, 64, 16, 128), bf)
    qT = qT.reshape(3, 8, 64, NT)
    kT = kT.reshape(3, 8, 64, NT)
    if kT_prev is not None:
        kT_prev = kT_prev.reshape(3, 8, 64, NT)
    vn = np.zeros((3, 128, 16, 8, 65), bf)
    NHs = (1, 4, 16)
    for g in range(3):
        idx = block_tokens(g)
        ridx = idx[:, ::-1]
        qn[g] = qT[g][:, :, idx]
        kn[g] = kT[g][:, :, ridx]
        vg = v[g][ridx]
        vn[g, :, :, :, 0:64] = vg.reshape(16, 128, 8, 64).transpose(1, 0, 2, 3)
        vn[g, :, :, :, 64] = 1.0
        nh = NHs[g]
        nblk = 16 // DIL[g]
        kh = np.zeros((8, 64, nh, 128), bf)
        vh = np.zeros((128, nh, 8, 65), bf)
        if kT_prev is not None:
            last = np.array([r * nblk + (nblk - 1) for r in range(DIL[g])])
            hidx = ridx[last]
            kh[:] = kT_prev[g][:, :, hidx]
            vhh = v_prev[g][hidx]
            vh[:, :, :, 0:64] = vhh.reshape(nh, 128, 8, 64).transpose(1, 0, 2, 3)
            vh[:, :, :, 64] = 1.0
        out["kh%d" % g] = kh
        out["vh%d" % g] = vh
    out["qn"], out["kn"], out["vn"] = qn, kn, vn
    return out


def common_spec(nls):
    return [
        ("ident", (128, 128), F32),
        ("c_col", (128, KC), F32),
        ("ada_w", (nls, D, 3 * D), F32),
        ("ada_b", (nls * 3 * D,), F32),
        ("ada_b_col", (nls, 128, 24), F32),
        ("norm_pre_col", (nls, 128, KC), F32),
        ("norm_post", (nls * D,), F32),
    ]


def common_inputs(inp, lss):
    c = np.asarray(inp["c"], np.float32).reshape(D)
    aw = np.asarray(inp["ada_w"], np.float32).reshape(8, D, 3 * D)
    ab = np.asarray(inp["ada_b"], np.float32).reshape(8, 3 * D)
    npre = np.asarray(inp["norm_pre"], np.float32).reshape(8, D)
    npost = np.asarray(inp["norm_post"], np.float32).reshape(8, D)
    return {
        "ident": np.eye(128, dtype=np.float32),
        "c_col": np.ascontiguousarray(c.reshape(KC, 128).T),
        "ada_w": np.ascontiguousarray(aw[lss]),
        "ada_b": np.ascontiguousarray(ab[lss].reshape(-1)),
        "ada_b_col": np.ascontiguousarray(ab[lss].reshape(len(lss), 24, 128).transpose(0, 2, 1)),
        "norm_pre_col": np.ascontiguousarray(npre[lss].reshape(len(lss), KC, 128).transpose(0, 2, 1)),
        "norm_post": np.ascontiguousarray(npost[lss].reshape(-1)),
    }


def hyb_weights(inp, j):
    return {
        "hyb_w_in": np.ascontiguousarray(np.asarray(inp["hyb_w_in"], np.float32)[j:j + 1]),
        "hyb_w_out": np.ascontiguousarray(np.asarray(inp["hyb_w_out"], np.float32)[j:j + 1]),
        "pool_w": np.ascontiguousarray(np.asarray(inp["pool_w"], np.float32)[j:j + 1]),
        "pool_scale_col": np.ascontiguousarray(np.asarray(inp["pool_scale"], np.float32)[j].reshape(1, 4, 128).transpose(0, 2, 1)),
        "hgrn_gain4": np.ascontiguousarray(np.tile(np.asarray(inp["hgrn_out_norm"], np.float32)[j], 4)),
        "lb_logits_col": np.ascontiguousarray(np.asarray(inp["hgrn_lb_logits"], np.float32).reshape(2, 4, 128).transpose(0, 2, 1)),
    }


def build_launch(stages):
    lss, ffn, hyb, att = [], [], [], []
    ins, outs = [("x", (NT, D), F32)], []
    kinds = [k for k, _ in stages]
    for kind, layer in stages:
        if kind in ("hyb_state", "hyb_full"):
            if 2 * layer not in lss:
                lss.append(2 * layer)
            if layer // 2 not in hyb:
                hyb.append(layer // 2)
        elif kind == "ffn":
            lss.append(2 * layer + 1)
            ffn.append(layer)
        elif kind in ("qkv", "att"):
            if 2 * layer not in lss:
                lss.append(2 * layer)
            att.append(layer // 2)
    ins += common_spec(len(lss))
    if hyb:
        ins += HYB_IN
    if "hyb_full" in kinds:
        ins += [("x_halo", (128, D), F32), ("pred_S", (7, 128, 4, 128), F32), ("pred_D", (7, 128, 4), F32)]
    if "hyb_state" in kinds:
        outs += [("S_out", (128, 4, 128), F32), ("D_out", (128, 4), F32)]
    if ffn:
        ins += [("ffn_w_in", (1, D, 2 * DFF), F32), ("ffn_w_out", (1, DFF, D), F32)]
    if "qkv" in kinds:
        ins += ATT_QKV_IN
        outs += ATT_QKV_OUT
    if "att" in kinds:
        ins += ATT_CORE_IN
    if kinds != ["hyb_state"]:
        outs += [("y", (NT, D), F32)]
    nc = bass.Bass("TRN2", target_bir_lowering=False)
    io = declare_io(nc, ins, outs)
    with ExitStack() as es:
        P = Prog(nc, es)
        A = Arena(nc, es, 212000)
        C = setup_common(nc, es, P, A, io)
        C.sel = {"ls": lss, "ffn": ffn, "hyb": hyb, "att": att}
        if "att" in kinds:
            C.wd = nc.dram_tensor("wd", [24, 1530], F32)
        load_x(C)
        stored = False
        for kind, layer in stages:
            if kind == "hyb_state":
                if len(stages) > 1 and not stored:
                    store_x(C)
                    stored = True
                hybrid_mixer(C, layer, False)
            elif kind == "hyb_full":
                hybrid_mixer(C, layer, True)
            elif kind == "ffn":
                ffn_sublayer(C, layer)
            elif kind == "qkv":
                store_x(C)
                stored = True
                attn_qkv(C, layer)
            elif kind == "att":
                if not DBG.get("nobias"):
                    attn_bias_tables(C)
                attn_core(C, layer)
        if not stored and kinds != ["hyb_state"]:
            store_x(C)
        P.barrier()
        P.replay()
    return nc


_PROGS = {}


def get_prog(stages):
    key = tuple(stages)
    if key not in _PROGS:
        _PROGS[key] = build_launch(list(stages))
    return _PROGS[key]


def run_launch(stages, in_maps):
    nc = get_prog(stages)
    res = run_bass_kernel_spmd(nc, in_maps, core_ids=list(range(NCORES)))
    return res.results


def kernel(**inp):
    inp = {k: np.asarray(v) for k, v in inp.items()}
    x = np.ascontiguousarray(inp["x"].reshape(SEQ, D).astype(np.float32))

    def xs(xa, c):
        return np.ascontiguousarray(xa[c * NT:(c + 1) * NT])

    def ffn_w(layer):
        return {"ffn_w_in": np.ascontiguousarray(np.asarray(inp["ffn_w_in"], np.float32)[layer:layer + 1]),
                "ffn_w_out": np.ascontiguousarray(np.asarray(inp["ffn_w_out"], np.float32)[layer:layer + 1])}

    def att_w(ja):
        return {"att_w_qkv": np.ascontiguousarray(np.asarray(inp["att_w_qkv"], np.float32)[ja:ja + 1])}

    hconst = [hybrid_inputs(inp, c) for c in range(NCORES)]
    for hc in hconst:
        for k in ("hyb_w_in", "hyb_w_out", "pool_w", "pool_scale_col", "hgrn_gain4", "lb_logits_col"):
            hc.pop(k, None)
    aconst = [attn_consts(inp, c) for c in range(NCORES)]

    stages = (("hyb_state", 0),)
    com = common_inputs(inp, [0])
    hw = hyb_weights(inp, 0)
    maps = [dict(com, **hw, **hconst[c], x=xs(x, c)) for c in range(NCORES)]
    res = run_launch(stages, maps)
    S_list = [r["S_out"] for r in res]
    D_list = [r["D_out"] for r in res]
    xcur = x
    for lh in (0, 2):
        la = lh + 1
        stages = (("hyb_full", lh), ("ffn", lh), ("qkv", la))
        com = common_inputs(inp, [2 * lh, 2 * lh + 1, 2 * la])
        hw = hyb_weights(inp, lh // 2)
        fw = ffn_w(lh)
        aw = att_w(la // 2)
        maps = []
        for c in range(NCORES):
            pS, pD = pred_states(S_list, D_list, c)
            maps.append(dict(com, **hw, **fw, **aw, **hconst[c], x=xs(xcur, c), x_halo=x_halo_tile(xcur, c),
                             pred_S=pS, pred_D=pD))
        res = run_launch(stages, maps)
        xcur = np.concatenate([r["y"].reshape(NT, D) for r in res], 0)
        proj = [(r["qT_out"].reshape(3, 4, 128, NT), r["kT_out"].reshape(3, 4, 128, NT), r["v_out"].reshape(3, NT, 512))
                for r in res]
        last = (la == 3)
        stages = (("att", la), ("ffn", la)) if last else (("att", la), ("ffn", la), ("hyb_state", la + 1))
        lss = [2 * la, 2 * la + 1] + ([] if last else [2 * (la + 1)])
        com = common_inputs(inp, lss)
        fw = ffn_w(la)
        awo = {"att_w_out": np.ascontiguousarray(np.asarray(inp["att_w_out"], np.float32)[la // 2:la // 2 + 1])}
        hw = {} if last else hyb_weights(inp, (la + 1) // 2)
        maps = []
        for c in range(NCORES):
            q, k, v = proj[c]
            lay = attn_layout(q, k, v, proj[c - 1][1] if c > 0 else None, proj[c - 1][2] if c > 0 else None)
            m = dict(com, **fw, **aconst[c], **lay, x=xs(xcur, c))
            m.update(awo)
            if not last:
                m.update(hw)
                m.update(hconst[c])
            maps.append(m)
        res = run_launch(stages, maps)
        xcur = np.concatenate([r["y"].reshape(NT, D) for r in res], 0)
        if not last:
            S_list = [r["S_out"] for r in res]
            D_list = [r["D_out"] for r in res]
    return xcur.reshape(1, SEQ, D).astype(np.float32)
```
